# Optimizing a Trainium2 kernel written in Bass

```python
import jax, jax.numpy as jnp
from jax import lax
import numpy as np

D_MODEL = 1024
BATCH = 8
SEQ = 4096
DEPTH = 2

CHUNK = 64
Q_BLOCK = 128
P_DIM = 256
MLA_HEADS = 8
MLA_NOPE = 64
MLA_ROPE = 32
MLA_V = 64
Q_LORA = 256
KV_LORA = 256
ROPE_BASE = 10000.0
SB_HEADS = 8
SB_DIM = 64
FOX_HEADS = 16
FOX_DIM = 64
D_FF = ((-(-8 * D_MODEL // 3)) + 255) // 256 * 256
DEEPNORM_ALPHA = (2.0 * DEPTH) ** 0.25
DEEPNORM_BETA = (8.0 * DEPTH) ** -0.25
N_EVEN = (DEPTH + 1) // 2
N_ODD = DEPTH // 2
IN_A = Q_LORA + KV_LORA + MLA_ROPE
IN_B = 3 * SB_HEADS * SB_DIM
IN_C = 3 * FOX_HEADS * FOX_DIM + FOX_HEADS
MIX_A = MLA_HEADS * MLA_V + SB_HEADS * SB_DIM
MIX_C = FOX_HEADS * FOX_DIM

kernel_name = "hybrid_mla_stickbreak_fox_deepnorm"


def layer_norm(x, g, b, eps=1e-5):
    xf = x.astype(jnp.float32)
    mu = jnp.mean(xf, -1, keepdims=True)
    var = jnp.mean(jnp.square(xf - mu), -1, keepdims=True)
    return ((xf - mu) * lax.rsqrt(var + eps) * g + b).astype(x.dtype)


def rms_norm(x, g, eps=1e-6):
    xf = x.astype(jnp.float32)
    return (xf * lax.rsqrt(jnp.mean(jnp.square(xf), -1, keepdims=True) + eps) * g).astype(x.dtype)


def rope_tables(seq, dim):
    inv = 1.0 / (ROPE_BASE ** (jnp.arange(0, dim, 2, dtype=jnp.float32) / dim))
    ang = jnp.arange(seq, dtype=jnp.float32)[:, None] * inv[None, :]
    return jnp.cos(ang), jnp.sin(ang)


def apply_rope(x, cos, sin):
    x1, x2 = jnp.split(x, 2, axis=-1)
    return jnp.concatenate([x1 * cos - x2 * sin, x1 * sin + x2 * cos], -1).astype(x.dtype)


def sweep_query_blocks(block_fn, q):
    b, h, s, d = q.shape
    nb = s // Q_BLOCK
    q_blocks = q.reshape(b, h, nb, Q_BLOCK, d).transpose(2, 0, 1, 3, 4)
    out = lax.map(lambda a: block_fn(a[0], a[1]), (q_blocks, jnp.arange(nb)))
    dv = out.shape[-1]
    return out.transpose(1, 0, 3, 2, 4).reshape(b, s, h * dv)


def mla_block(q_blk, blk, k, v):
    t = blk * Q_BLOCK + jnp.arange(Q_BLOCK)
    s = jnp.arange(k.shape[2])
    allowed = (s[None, :] // CHUNK) <= (t[:, None] // CHUNK)
    logits = jnp.einsum('bhqd,bhkd->bhqk', q_blk, k).astype(jnp.float32) * (MLA_NOPE + MLA_ROPE) ** -0.5
    w = jax.nn.softmax(jnp.where(allowed, logits, -jnp.inf), axis=-1)
    return jnp.einsum('bhqk,bhkd->bhqd', w.astype(v.dtype), v)


def stick_breaking_block(q_blk, blk, k, v):
    t = blk * Q_BLOCK + jnp.arange(Q_BLOCK)
    s = jnp.arange(k.shape[2])
    past = s[None, :] < t[:, None]
    z = jnp.einsum('bhqd,bhkd->bhqk', q_blk, k).astype(jnp.float32) * SB_DIM ** -0.5
    log_beta = jax.nn.log_sigmoid(z)
    log_rem = jnp.where(past, jax.nn.log_sigmoid(-z), 0.0)
    between = lax.cumsum(log_rem, axis=3, reverse=True) - log_rem
    att = jnp.where(past, jnp.exp(log_beta + between), 0.0)
    return jnp.einsum('bhqk,bhkd->bhqd', att.astype(v.dtype), v)


def forgetting_block(q_blk, blk, k, v, dcum):
    t = blk * Q_BLOCK + jnp.arange(Q_BLOCK)
    s = jnp.arange(k.shape[2])
    causal = s[None, :] <= t[:, None]
    d_t = lax.dynamic_slice_in_dim(dcum, blk * Q_BLOCK, Q_BLOCK, axis=2)
    logits = (jnp.einsum('bhqd,bhkd->bhqk', q_blk, k).astype(jnp.float32) * FOX_DIM ** -0.5
              + d_t[..., :, None] - dcum[..., None, :])
    w = jax.nn.softmax(jnp.where(causal, logits, -jnp.inf), axis=-1)
    return jnp.einsum('bhqk,bhkd->bhqd', w.astype(v.dtype), v)


def mixer_mla_sb(x, w_in, q_norm_g, w_uq, kv_norm_g, w_ukv, w_out, cos, sin):
    b, s, _ = x.shape
    h = x @ w_in
    c_q, c_kv, k_rope, sb_qkv = jnp.split(h, [Q_LORA, Q_LORA + KV_LORA, IN_A], axis=-1)
    q = (rms_norm(c_q, q_norm_g) @ w_uq).reshape(b, s, MLA_HEADS, MLA_NOPE + MLA_ROPE).transpose(0, 2, 1, 3)
    q_nope, q_rope = jnp.split(q, [MLA_NOPE], axis=-1)
    q = jnp.concatenate([q_nope, apply_rope(q_rope, cos, sin)], -1)
    kv = (rms_norm(c_kv, kv_norm_g) @ w_ukv).reshape(b, s, MLA_HEADS, MLA_NOPE + MLA_V).transpose(0, 2, 1, 3)
    k_nope, v_a = jnp.split(kv, [MLA_NOPE], axis=-1)
    k_rope = apply_rope(k_rope[:, None], cos, sin)
    k_a = jnp.concatenate([k_nope, jnp.broadcast_to(k_rope, (b, MLA_HEADS, s, MLA_ROPE))], -1)
    o_a = sweep_query_blocks(lambda qb, i: mla_block(qb, i, k_a, v_a), q)
    q_b, k_b, v_b = sb_qkv.reshape(b, s, 3, SB_HEADS, SB_DIM).transpose(2, 0, 3, 1, 4)
    o_b = sweep_query_blocks(lambda qb, i: stick_breaking_block(qb, i, k_b, v_b), q_b)
    return jnp.concatenate([o_a, o_b], -1) @ w_out


def mixer_fox(x, w_in, b_f, w_out):
    b, s, _ = x.shape
    h = x @ w_in
    qkv, f_logit = jnp.split(h, [3 * FOX_HEADS * FOX_DIM], axis=-1)
    q, k, v = qkv.reshape(b, s, 3, FOX_HEADS, FOX_DIM).transpose(2, 0, 3, 1, 4)
    log_f = jax.nn.log_sigmoid((f_logit + b_f).astype(jnp.float32))
    dcum = lax.cumsum(log_f, axis=1).transpose(0, 2, 1)
    o = sweep_query_blocks(lambda qb, i: forgetting_block(qb, i, k, v, dcum), q)
    return o @ w_out


def swiglu(x, w1, w3, w2):
    return (jax.nn.silu(x @ w1) * (x @ w3)) @ w2


def setup_inputs(seed: int = 0) -> dict:
    key = jax.random.key(seed)
    ks = iter(jax.random.split(key, 32))

    def dense(shape, fan_in, scale=1.0):
        return jax.random.normal(next(ks), shape, jnp.float32) * (fan_in ** -0.5) * scale

    def gain(shape):
        return 1.0 + 0.02 * jax.random.normal(next(ks), shape, jnp.float32)

    def bias(shape):
        return 0.02 * jax.random.normal(next(ks), shape, jnp.float32)

    return {
        "x": jax.random.normal(next(ks), (BATCH, SEQ, D_MODEL), jnp.float32),
        "p": jax.random.normal(next(ks), (DEPTH, BATCH, SEQ, P_DIM), jnp.float32),
        "a_w_in": dense((N_EVEN, D_MODEL, IN_A + IN_B), D_MODEL),
        "a_q_norm": gain((N_EVEN, Q_LORA)),
        "a_w_uq": dense((N_EVEN, Q_LORA, MLA_HEADS * (MLA_NOPE + MLA_ROPE)), Q_LORA),
        "a_kv_norm": gain((N_EVEN, KV_LORA)),
        "a_w_ukv": dense((N_EVEN, KV_LORA, MLA_HEADS * (MLA_NOPE + MLA_V)), KV_LORA),
        "a_w_out": dense((N_EVEN, MIX_A, D_MODEL), MIX_A, DEEPNORM_BETA),
        "c_w_in": dense((N_ODD, D_MODEL, IN_C), D_MODEL),
        "c_b_f": jax.random.uniform(next(ks), (N_ODD, FOX_HEADS), jnp.float32, 1.0, 4.0),
        "c_w_out": dense((N_ODD, MIX_C, D_MODEL), MIX_C, DEEPNORM_BETA),
        "ffn_w1": dense((DEPTH, D_MODEL, D_FF), D_MODEL),
        "ffn_w3": dense((DEPTH, D_MODEL, D_FF), D_MODEL),
        "ffn_w2": dense((DEPTH, D_FF, D_MODEL), D_FF, DEEPNORM_BETA),
        "ln1_g": gain((DEPTH, D_MODEL)),
        "ln1_b": bias((DEPTH, D_MODEL)),
        "ln2_g": gain((DEPTH, D_MODEL)),
        "ln2_b": bias((DEPTH, D_MODEL)),
        "ple_w_proj": dense((DEPTH, P_DIM, D_MODEL), P_DIM),
        "ple_w_gate": dense((DEPTH, D_MODEL, D_MODEL), D_MODEL),
        "ple_b_gate": bias((DEPTH, D_MODEL)),
    }


def reference(x, p, a_w_in, a_q_norm, a_w_uq, a_kv_norm, a_w_ukv, a_w_out,
              c_w_in, c_b_f, c_w_out, ffn_w1, ffn_w3, ffn_w2,
              ln1_g, ln1_b, ln2_g, ln2_b, ple_w_proj, ple_w_gate, ple_b_gate):
    cos, sin = rope_tables(x.shape[1], MLA_ROPE)
    for i in range(DEPTH):
        j = i // 2
        if i % 2 == 0:
            mix = mixer_mla_sb(x, a_w_in[j], a_q_norm[j], a_w_uq[j], a_kv_norm[j],
                               a_w_ukv[j], a_w_out[j], cos, sin)
        else:
            mix = mixer_fox(x, c_w_in[j], c_b_f[j], c_w_out[j])
        x = layer_norm(DEEPNORM_ALPHA * x + mix, ln1_g[i], ln1_b[i])
        x = layer_norm(DEEPNORM_ALPHA * x + swiglu(x, ffn_w1[i], ffn_w3[i], ffn_w2[i]), ln2_g[i], ln2_b[i])
        x = x + jax.nn.sigmoid(x @ ple_w_gate[i] + ple_b_gate[i]) * (p[i] @ ple_w_proj[i])
    return x
```

```python
import contextlib
import math
import numpy as np
import concourse.bass as bass
import concourse.mybir as mybir
from concourse.bass_utils import run_bass_kernel_spmd

F32 = mybir.dt.float32
BF16 = mybir.dt.bfloat16
I32 = mybir.dt.int32
AF = mybir.ActivationFunctionType
ALU = mybir.AluOpType

D = 1024
DFF = 2816
NFC = DFF // 128
ALPHA = (2.0 * 2) ** 0.25
PI = math.pi


class Buf:
    def __init__(self, name, ap=None):
        self.name = name
        self.ap = ap
        self.w = None
        self.r = {}
        self.dsem = None

    def __getitem__(self, idx):
        return self.ap[idx]


class Eng:
    def __init__(self, name, eng):
        self.name = name
        self.eng = eng
        self.sem = None
        self.count = 0
        self.seen = {}


class Trk:
    def __init__(self, nc):
        self.nc = nc
        self.nsem = 0
        self.E = {n: Eng(n, getattr(nc, a)) for n, a in
                  [("pe", "tensor"), ("act", "scalar"), ("dve", "vector"), ("pool", "gpsimd"), ("sp", "sync")]}
        self.dpool = []
        self.dlive = []
        self.phase_bufs = []
        self.log = {n: [] for n in self.E}
        self.cur_waits = []
        self.new_epoch()

    def _new_sem(self, name):
        self.nsem += 1
        return (self.nsem, self.nc.alloc_semaphore(name=f"{name}_{self.nsem}"))

    def new_epoch(self):
        for e in self.E.values():
            e.sem = self._new_sem("e" + e.name)
            e.count = 0

    def buf(self, name, ap=None):
        b = Buf(name, ap)
        self.phase_bufs.append(b)
        return b

    def _wait(self, E, tok):
        key, h, val, en = tok
        if E.seen.get(key, 0) >= val:
            return
        E.eng.wait_ge(h, val)
        E.seen[key] = val
        self.cur_waits.append((key, val))

    def _deps(self, E, reads, writes):
        for b in reads:
            if b.w is not None:
                if not (b.w[3] == "pe" and E.name == "pe"):
                    self._wait(E, b.w)
        for b in writes:
            if b.w is not None:
                if not (b.w[3] == "pe" and E.name == "pe"):
                    self._wait(E, b.w)
            for tok in b.r.values():
                if not (tok[3] == "pe" and E.name == "pe"):
                    self._wait(E, tok)

    def op(self, en, fn, reads=(), writes=(), inc=True):
        E = self.E[en]
        self._deps(E, reads, writes)
        ins = fn(E.eng)
        self.log[en].append((self.cur_waits, (E.sem[0], E.count + 1, 1) if inc else None))
        self.cur_waits = []
        if inc:
            E.count += 1
            ins.then_inc(E.sem[1], 1)
            tok = (E.sem[0], E.sem[1], E.count, en)
            for b in reads:
                b.r[en] = tok
            for b in writes:
                b.w = tok
                b.r = {}
        return ins

    def dma(self, qn, pairs, reads=(), writes=(), **kw):
        E = self.E[qn]
        self._deps(E, reads, writes)
        owner = (list(writes) + list(reads))[0]
        if owner.dsem is None:
            if self.dpool:
                owner.dsem = self.dpool.pop()
            else:
                k, h = self._new_sem("d")
                owner.dsem = [k, h, 0]
                self.dlive.append(owner.dsem)
        ds = owner.dsem
        self.log[qn].append((self.cur_waits, (ds[0], 16 * (ds[2] + len(pairs)), 16 * len(pairs))))
        self.cur_waits = []
        for (o, i) in pairs:
            E.eng.dma_start(out=o, in_=i, **kw).then_inc(ds[1], 16)
            ds[2] += 1
        tok = (ds[0], ds[1], 16 * ds[2], "dma")
        for b in reads:
            b.r[("dma", ds[0])] = tok
        for b in writes:
            b.w = tok
            b.r = {}

    def barrier(self):
        toks = []
        for n, e in self.E.items():
            if e.count > 0:
                toks.append((e.sem[0], e.sem[1], e.count, n))
        for ds in self.dlive:
            if ds[2] > 0:
                toks.append((ds[0], ds[1], 16 * ds[2], "dma"))
        for E in self.E.values():
            for tok in toks:
                self._wait(E, tok)
            self.log[E.name].append((self.cur_waits, None))
            self.cur_waits = []
        self.new_epoch()
        for b in self.phase_bufs:
            if b.dsem is not None:
                self.dpool.append(b.dsem)
                b.dsem = None
            b.w = None
            b.r = {}
        self.phase_bufs = []


def simulate_sync(T):
    sem = {}
    ptr = {n: 0 for n in T.log}
    total = sum(len(v) for v in T.log.values())
    done = 0
    while done < total:
        prog = False
        for n, ops in T.log.items():
            while ptr[n] < len(ops):
                waits, prod = ops[ptr[n]]
                if all(sem.get(k, 0) >= v for (k, v) in waits):
                    if prod is not None:
                        sem[prod[0]] = sem.get(prod[0], 0) + prod[2]
                    ptr[n] += 1
                    done += 1
                    prog = True
                else:
                    break
        if not prog:
            print("DEADLOCK")
            for n, ops in T.log.items():
                if ptr[n] < len(ops):
                    waits, prod = ops[ptr[n]]
                    print(n, ptr[n], len(ops), [(k, v, sem.get(k, 0)) for (k, v) in waits if sem.get(k, 0) < v], prod)
            return False
    print("SYNC SIM OK", total)
    return True


import os as _os
SBLAG = int(_os.environ.get('SBLAG', '2'))


def build(S, taps=False):
    NT = S // 128
    nc = bass.Bass("TRN2", target_bir_lowering=False)
    T = Trk(nc)

    def din(name, shape, dt=F32):
        return nc.dram_tensor(name, list(shape), dt, kind="ExternalInput").ap()

    def dscr(name, shape, dt):
        return nc.dram_tensor(name, list(shape), dt, kind="Internal").ap()

    x_in = din("x", [S, D])
    p_in = din("p", [2, S, 256])
    a_w_in = din("a_w_in", [D, 2080])
    a_q_norm = din("a_q_norm", [256])
    a_w_uq = din("a_w_uq", [256, 768])
    a_kv_norm = din("a_kv_norm", [256])
    a_w_ukv = din("a_w_ukv", [256, 1024])
    a_w_out = din("a_w_out", [D, D])
    c_w_in = din("c_w_in", [D, 3088])
    c_b_f = din("c_b_f", [16])
    c_w_out = din("c_w_out", [D, D])
    ffn_w1 = din("ffn_w1", [2, D, DFF])
    ffn_w3 = din("ffn_w3", [2, D, DFF])
    ffn_w2 = din("ffn_w2", [2, DFF, D])
    ln1_g = din("ln1_g", [2, D])
    ln1_b = din("ln1_b", [2, D])
    ln2_g = din("ln2_g", [2, D])
    ln2_b = din("ln2_b", [2, D])
    ple_w_proj = din("ple_w_proj", [2, 256, D])
    ple_w_gate = din("ple_w_gate", [2, D, D])
    ple_b_gate = din("ple_b_gate", [2, D])
    NCST = 128 * 5 + NT + 16
    cst_in = din("cst", [128, NCST])
    out_d = nc.dram_tensor("out", [S, D], F32, kind="ExternalOutput").ap()

    QTm = dscr("QTm", [8, 96, S], BF16)
    KTm = dscr("KTm", [8, 96, S], BF16)
    Vm = dscr("Vm", [S, 8, 64], BF16)
    QTs = dscr("QTs", [512, S], BF16)
    KTs = dscr("KTs", [512, S], BF16)
    Vs = dscr("Vs", [S, 8, 64], BF16)
    QTf = dscr("QTf", [1024, S], BF16)
    KTf = dscr("KTf", [1024, S], BF16)
    QDf = dscr("QDf", [3, 16, S], BF16)
    KDf = dscr("KDf", [3, 16, S], BF16)
    Vf = dscr("Vf", [S, 16, 64], BF16)
    Y1 = dscr("Y1", [S, D], F32)
    Y2 = dscr("Y2", [S, D], F32)
    X1 = dscr("X1", [S, D], F32)

    gs = contextlib.ExitStack()

    uniq = [0]

    def sb(es, name, shape, dt, side=None):
        uniq[0] += 1
        if side is None:
            t = es.enter_context(nc.sbuf_tensor(f"{name}_u{uniq[0]}", list(shape), dt))
        else:
            t = es.enter_context(nc.sbuf_tensor(f"{name}_u{uniq[0]}", list(shape), dt, side=side))
        return T.buf(name, t)

    def ps(es, name, shape, dt):
        uniq[0] += 1
        t = es.enter_context(nc.psum_tensor(f"{name}_u{uniq[0]}", list(shape), dt))
        return T.buf(name, t)

    cst = sb(gs, "cst_sb", [128, NCST], F32)
    ident_bf = sb(gs, "ident_bf", [128, 128], BF16)
    mC_bf = sb(gs, "mC_bf", [128, 128], BF16)
    mM_bf = sb(gs, "mM_bf", [128, 128], BF16)
    mS_bf = sb(gs, "mS_bf", [128, 128], BF16)
    ones_f = sb(gs, "ones_f", [128, 512], F32)
    mhalf = sb(gs, "mhalf", [128, 1], F32)
    GBUFS = [cst, ident_bf, mC_bf, mM_bf, mS_bf, ones_f, mhalf]
    ident_f = cst[:, 0:128]
    J_f = cst[:, 128:256]
    mS_f = cst[:, 512:640]
    pos_f = cst[:, 640:640 + NT]
    invf = cst[:, 640 + NT:640 + NT + 16]

    T.dma("sp", [(cst[:, :], cst_in[:, :])], writes=[cst])
    T.op("dve", lambda e: e.tensor_copy(out=ident_bf[:, :], in_=cst[:, 0:128]), [cst], [ident_bf])
    T.op("dve", lambda e: e.tensor_copy(out=mC_bf[:, :], in_=cst[:, 256:384]), [cst], [mC_bf])
    T.op("dve", lambda e: e.tensor_copy(out=mM_bf[:, :], in_=cst[:, 384:512]), [cst], [mM_bf])
    T.op("dve", lambda e: e.tensor_copy(out=mS_bf[:, :], in_=cst[:, 512:640]), [cst], [mS_bf])
    T.op("dve", lambda e: e.memset(ones_f[:, :], 1.0), [], [ones_f])
    T.op("dve", lambda e: e.memset(mhalf[:, :], -0.5), [], [mhalf])

    def load_w(es, name, src, kc, n, q="pool", side=None):
        w = sb(es, name, [128, kc, n], BF16, side=side)
        v = src.rearrange("(c p) n -> p c n", p=128)
        T.dma(q, [(w[:, c, :], v[:, c, :]) for c in range(kc)], writes=[w], max_dma_last_dim=4096)
        return w

    def load_bc(es, name, src, n=D):
        t = sb(es, name, [128, n], F32)
        T.dma("sp", [(t[:, :], src.partition_broadcast(128))], writes=[t])
        return t

    def phase_end():
        T.barrier()
        for b in GBUFS:
            T.phase_bufs.append(b)

    def layer_norm(r, g_bc, b_bc, out, st, eps=1e-5):
        T.op("dve", lambda e: e.bn_stats(out=st[:, 0:6], in_=r[:, 0:512]), [r], [st])
        T.op("dve", lambda e: e.bn_stats(out=st[:, 6:12], in_=r[:, 512:1024]), [r], [st])
        T.op("dve", lambda e: e.bn_aggr(out=st[:, 12:14], in_=st[:, 0:12]), [st], [st])
        T.op("dve", lambda e: e.tensor_scalar(out=st[:, 14:15], in0=st[:, 13:14], scalar1=eps, scalar2=None,
                                               op0=ALU.add), [st], [st])
        T.op("pool", lambda e: e.tensor_tensor(out=st[:, 16:17], in0=st[:, 14:15], in1=mhalf[:, 0:1], op=ALU.pow),
             [st, mhalf], [st])
        T.op("dve", lambda e: e.scalar_tensor_tensor(out=st[:, 17:18], in0=st[:, 12:13], scalar=-1.0,
                                                      in1=st[:, 16:17], op0=ALU.mult, op1=ALU.mult), [st], [st])
        T.op("act", lambda e: e.activation(out=r[:, :], in_=r[:, :], func=AF.Identity,
                                           scale=st[:, 16:17], bias=st[:, 17:18]), [r, st], [r])
        T.op("dve", lambda e: e.tensor_tensor(out=r[:, :], in0=r[:, :], in1=g_bc[:, :], op=ALU.mult), [r, g_bc], [r])
        T.op("pool", lambda e: e.tensor_tensor(out=out[:, :], in0=r[:, :], in1=b_bc[:, :], op=ALU.add), [r, b_bc], [out])

    def transpose_f32_tile(src, dstT, pT, nblk):
        for h0 in range(0, nblk, 4):
            n = min(4, nblk - h0)
            for i in range(n):
                T.op("pe", lambda e, i=i: e.transpose(out=pT[:, i * 128:(i + 1) * 128],
                                                       in_=src[:, (h0 + i) * 128:(h0 + i + 1) * 128],
                                                       identity=ident_f), [src, cst], [pT], inc=(i == n - 1))
            eng = "act" if (h0 // 4) % 2 == 0 else "dve"
            if eng == "act":
                T.op("act", lambda e: e.activation(out=dstT[:, h0:h0 + n, :],
                                                   in_=pT[:, 0:n * 128].rearrange("p (c t) -> p c t", t=128),
                                                   func=AF.Copy), [pT], [dstT])
            else:
                T.op("dve", lambda e: e.tensor_copy(out=dstT[:, h0:h0 + n, :],
                                                    in_=pT[:, 0:n * 128].rearrange("p (c t) -> p c t", t=128)),
                     [pT], [dstT])

    def phase_A0():
        es = contextlib.ExitStack()
        Win = load_w(es, "a_win", a_w_in, 8, 2080)
        Wuq = load_w(es, "a_wuq", a_w_uq, 2, 768)
        Wukv = load_w(es, "a_wukv", a_w_ukv, 2, 1024)
        g4 = sb(es, "g4", [128, 4], F32)
        T.dma("sp", [(g4[:, 0:2], a_q_norm.rearrange("(c p) -> p c", p=128)),
                     (g4[:, 2:4], a_kv_norm.rearrange("(c p) -> p c", p=128))], writes=[g4],
              allow_slow_non_contiguous=True)
        cosT = sb(es, "cosT", [128, NT, 16], F32)
        sinT = sb(es, "sinT", [128, NT, 16], F32)
        ang = sb(es, "ang", [128, NT, 16], F32)
        tmpa = sb(es, "tmpa", [128, NT, 16], F32)
        ki = sb(es, "ki", [128, NT, 16], I32)
        kf = sb(es, "kf", [128, NT, 16], F32)
        T.op("dve", lambda e: e.tensor_tensor(out=ang[:, :, :], in0=pos_f.unsqueeze(2).broadcast_to([128, NT, 16]),
                                               in1=invf.unsqueeze(1).broadcast_to([128, NT, 16]), op=ALU.mult),
             [cst], [ang])
        for (dst, shift) in ((sinT, 0.0), (cosT, PI / 2)):
            T.op("dve", lambda e: e.tensor_scalar(out=tmpa[:, :, :], in0=ang[:, :, :], scalar1=shift,
                                                   scalar2=1.0 / (2 * PI), op0=ALU.add, op1=ALU.mult), [ang], [tmpa])
            T.op("dve", lambda e: e.tensor_copy(out=ki[:, :, :], in_=tmpa[:, :, :]), [tmpa], [ki])
            T.op("dve", lambda e: e.tensor_copy(out=kf[:, :, :], in_=ki[:, :, :]), [ki], [kf])
            T.op("dve", lambda e: e.scalar_tensor_tensor(out=tmpa[:, :, :], in0=kf[:, :, :], scalar=-2 * PI,
                                                          in1=ang[:, :, :], op0=ALU.mult, op1=ALU.add), [kf, ang], [tmpa])
            T.op("dve", lambda e: e.tensor_scalar(out=tmpa[:, :, :], in0=tmpa[:, :, :], scalar1=shift,
                                                   scalar2=3.1415925, op0=ALU.add, op1=ALU.min), [tmpa], [tmpa])
            T.op("dve", lambda e: e.tensor_scalar(out=tmpa[:, :, :], in0=tmpa[:, :, :], scalar1=-3.1415925,
                                                   scalar2=None, op0=ALU.max), [tmpa], [tmpa])
            T.op("act", lambda e: e.activation(out=dst[:, :, :], in_=tmpa[:, :, :], func=AF.Sin), [tmpa], [dst])

        import os
        ksub = int(os.environ.get("KSUB", "99"))
        if ksub == 0:
            phase_end(); es.close(); return
        xt = [sb(es, f"a0_x{i}", [128, D], F32) for i in range(2)]
        xT = sb(es, "a0_xT", [128, 8, 128], BF16)
        xTr = sb(es, "a0_xTr", [128, 8, 128], BF16)
        junk = sb(es, "a0_junk", [128, 256], F32)
        st = sb(es, "a0_st", [128, 16], F32)
        cs = sb(es, "a0_cs", [128, 512], F32)
        cT = sb(es, "a0_cT", [128, 4, 128], BF16)
        qa = sb(es, "a0_qa", [128, 8, 96], BF16)
        ka = sb(es, "a0_ka", [128, 8, 96], BF16)
        va = sb(es, "a0_va", [128, 8, 64], BF16)
        rt = [sb(es, f"a0_rt{i}", [128, 8, 16], F32) for i in range(4)]
        kr = sb(es, "a0_kr", [128, 32], F32)
        krs = sb(es, "a0_krs", [128, 32], F32)
        qr = sb(es, "a0_qr", [128, 8, 32], F32)
        cos8 = sb(es, "a0_cos8", [128, NT, 8, 16], F32)
        sin8 = sb(es, "a0_sin8", [128, NT, 8, 16], F32)
        for h in range(8):
            T.op("dve", lambda e: e.tensor_copy(out=cos8[:, :, h, :], in_=cosT[:, :, :]), [cosT], [cos8])
            T.op("dve", lambda e: e.tensor_copy(out=sin8[:, :, h, :], in_=sinT[:, :, :]), [sinT], [sin8])
        krt = [sb(es, f"a0_krt{i}", [128, 16], F32) for i in range(4)]
        stQ = sb(es, "a0_stQ", [128, 8, 128], BF16)
        stK = sb(es, "a0_stK", [128, 8, 128], BF16)
        stQs = sb(es, "a0_stQs", [128, 4, 128], BF16)
        stKs = sb(es, "a0_stKs", [128, 4, 128], BF16)
        stVs = sb(es, "a0_stVs", [128, 512], BF16)
        pT = ps(es, "a0_pT", [128, 512], F32)
        pC = ps(es, "a0_pC", [128, 512], F32)
        pC2 = ps(es, "a0_pC2", [128, 512], F32)
        pQ = ps(es, "a0_pQ", [128, 1024], F32)
        pKV = ps(es, "a0_pKV", [128, 1024], F32)
        pB = ps(es, "a0_pB", [128, 1024], BF16)

        T.dma("sp", [(xt[0][:, :], x_in[0:128, :])], writes=[xt[0]])
        for tt in range(NT):
            x = xt[tt % 2]
            if tt + 1 < NT:
                nx = xt[(tt + 1) % 2]
                T.dma("sp", [(nx[:, :], x_in[(tt + 1) * 128:(tt + 2) * 128, :])], writes=[nx])
            rtt = NT - 1 - tt
            tsl = slice(tt * 128, (tt + 1) * 128)
            rsl = slice(rtt * 128, (rtt + 1) * 128)
            transpose_f32_tile(x, xT, pT, 8)
            for h0 in (0, 4):
                for i in range(4):
                    T.op("pe", lambda e, i=i: e.matmul(out=pT[:, i * 128:(i + 1) * 128],
                                                        lhsT=x[:, (h0 + i) * 128:(h0 + i + 1) * 128], rhs=J_f,
                                                        start=True, stop=True), [x, cst], [pT], inc=(i == 3))
                T.op("act", lambda e: e.activation(out=xTr[:, h0:h0 + 4, :],
                                                   in_=pT[:, :].rearrange("p (c t) -> p c t", t=128), func=AF.Copy),
                     [pT], [xTr])
            if ksub == 1: continue
            for kc in range(8):
                T.op("pe", lambda e: e.matmul(out=pC[:, 0:512], lhsT=xT[:, kc, :], rhs=Win[:, kc, 0:512],
                                              start=(kc == 0), stop=(kc == 7)), [xT, Win], [pC], inc=(kc == 7))
            for kc in range(8):
                T.op("pe", lambda e: e.matmul(out=pC2[:, 0:32], lhsT=xT[:, kc, :], rhs=Win[:, kc, 512:544],
                                              start=(kc == 0), stop=(kc == 7)), [xT, Win], [pC2], inc=(kc == 7))
            for hp in range(4):
                for kc in range(8):
                    T.op("pe", lambda e: e.matmul(out=pKV[:, hp * 128:(hp + 1) * 128],
                                                  lhsT=Win[:, kc, 544 + hp * 128:544 + (hp + 1) * 128],
                                                  rhs=xT[:, kc, :], start=(kc == 0), stop=(kc == 7)),
                         [xT, Win], [pKV], inc=False)
            for hp in range(4):
                for kc in range(8):
                    T.op("pe", lambda e: e.matmul(out=pKV[:, 512 + hp * 128:512 + (hp + 1) * 128],
                                                  lhsT=Win[:, kc, 1056 + hp * 128:1056 + (hp + 1) * 128],
                                                  rhs=xTr[:, kc, :], start=(kc == 0), stop=(kc == 7)),
                         [xT, xTr, Win], [pKV], inc=(kc == 7 and hp == 3))
            for kc in range(8):
                T.op("pe", lambda e: e.matmul(out=pQ[:, 0:512], lhsT=xTr[:, kc, :], rhs=Win[:, kc, 1568:2080],
                                              start=(kc == 0), stop=(kc == 7)), [xTr, Win], [pQ], inc=(kc == 7))
            T.op("act", lambda e: e.activation(out=junk[:, :], in_=pC[:, 0:256], func=AF.Square,
                                               accum_out=st[:, 0:1]), [pC], [junk, st])
            T.op("act", lambda e: e.activation(out=junk[:, :], in_=pC[:, 256:512], func=AF.Square,
                                               accum_out=st[:, 1:2]), [pC], [junk, st])
            T.op("dve", lambda e: e.tensor_scalar(out=st[:, 2:4], in0=st[:, 0:2], scalar1=1.0 / 256, scalar2=1e-6,
                                                   op0=ALU.mult, op1=ALU.add), [st], [st])
            T.op("act", lambda e: e.activation(out=st[:, 4:6], in_=st[:, 2:4], func=AF.Sqrt), [st], [st])
            T.op("dve", lambda e: e.reciprocal(out=st[:, 6:8], in_=st[:, 4:6]), [st], [st])
            T.op("dve", lambda e: e.tensor_scalar(out=cs[:, 0:256], in0=pC[:, 0:256], scalar1=st[:, 6:7],
                                                   scalar2=None, op0=ALU.mult), [pC, st], [cs])
            T.op("dve", lambda e: e.tensor_scalar(out=cs[:, 256:512], in0=pC[:, 256:512], scalar1=st[:, 7:8],
                                                   scalar2=None, op0=ALU.mult), [pC, st], [cs])
            for i in range(4):
                T.op("pe", lambda e: e.transpose(out=pT[:, i * 128:(i + 1) * 128], in_=cs[:, i * 128:(i + 1) * 128],
                                                 identity=ident_f), [cs, cst], [pT], inc=(i == 3))
            for i in range(4):
                T.op("dve", lambda e: e.tensor_scalar(out=cT[:, i, :], in0=pT[:, i * 128:(i + 1) * 128],
                                                       scalar1=g4[:, i:i + 1], scalar2=None, op0=ALU.mult),
                     [pT, g4], [cT])
            T.op("dve", lambda e: e.tensor_copy(out=stQs[:, :, :],
                                                in_=pKV[:, 0:512].rearrange("p (c t) -> p c t", t=128)), [pKV], [stQs])
            T.dma("sp", [(QTs.rearrange("(c p) t -> p c t", p=128)[:, :, tsl], stQs[:, :, :])], reads=[stQs])
            T.op("act", lambda e: e.activation(out=stKs[:, :, :],
                                               in_=pKV[:, 512:1024].rearrange("p (c t) -> p c t", t=128),
                                               func=AF.Copy), [pKV], [stKs])
            T.dma("sp", [(KTs.rearrange("(c p) t -> p c t", p=128)[:, :, rsl], stKs[:, :, :])], reads=[stKs])
            T.op("dve", lambda e: e.tensor_copy(out=stVs[:, :], in_=pQ[:, 0:512]), [pQ], [stVs])
            T.dma("sp", [(Vs[rsl, :, :], stVs[:, :].rearrange("p (h d) -> p h d", d=64))], reads=[stVs])
            if ksub == 2: continue
            for bq in range(2):
                for kc in range(2):
                    T.op("pe", lambda e: e.matmul(out=pQ[:, bq * 512:bq * 512 + 384], lhsT=cT[:, kc, :],
                                                  rhs=Wuq[:, kc, bq * 384:(bq + 1) * 384],
                                                  start=(kc == 0), stop=(kc == 1)), [cT, Wuq], [pQ],
                         inc=(kc == 1 and bq == 1))
            for (n0, n1) in ((0, 512), (512, 1024)):
                for kc in range(2):
                    T.op("pe", lambda e: e.matmul(out=pKV[:, n0:n1], lhsT=cT[:, 2 + kc, :], rhs=Wukv[:, kc, n0:n1],
                                                  start=(kc == 0), stop=(kc == 1)), [cT, Wukv], [pKV],
                         inc=(kc == 1 and n0 == 512))
            if ksub == 21: continue
            for bq in range(2):
                hs = slice(4 * bq, 4 * bq + 4)
                q3 = pQ[:, bq * 512:bq * 512 + 384].rearrange("p (h d) -> p h d", d=96)
                kv3 = pKV[:, bq * 512:(bq + 1) * 512].rearrange("p (h d) -> p h d", d=128)
                T.op("act", lambda e: e.activation(out=qa[:, hs, 0:64], in_=q3[:, :, 0:64], func=AF.Copy), [pQ], [qa])
                T.op("act", lambda e: e.activation(out=qr[:, hs, :], in_=q3[:, :, 64:96], func=AF.Copy), [pQ], [qr])
                T.op("act", lambda e: e.activation(out=ka[:, hs, 0:64], in_=kv3[:, :, 0:64], func=AF.Copy), [pKV], [ka])
                T.op("act", lambda e: e.activation(out=va[:, hs, :], in_=kv3[:, :, 64:128], func=AF.Copy), [pKV], [va])
            T.op("act", lambda e: e.activation(out=krs[:, :], in_=pC2[:, 0:32], func=AF.Copy), [pC2], [krs])
            if ksub == 22: continue
            cosb = cos8[:, tt, :, :]
            sinb = sin8[:, tt, :, :]
            T.op("dve", lambda e: e.tensor_tensor(out=rt[0][:, :, :], in0=qr[:, :, 0:16], in1=cosb, op=ALU.mult),
                 [qr, cos8], [rt[0]])
            T.op("dve", lambda e: e.tensor_tensor(out=rt[1][:, :, :], in0=qr[:, :, 16:32], in1=sinb, op=ALU.mult),
                 [qr, sin8], [rt[1]])
            T.op("dve", lambda e: e.tensor_tensor(out=rt[2][:, :, :], in0=qr[:, :, 0:16], in1=sinb, op=ALU.mult),
                 [qr, sin8], [rt[2]])
            T.op("dve", lambda e: e.tensor_tensor(out=rt[3][:, :, :], in0=qr[:, :, 16:32], in1=cosb, op=ALU.mult),
                 [qr, cos8], [rt[3]])
            if ksub == 23: continue
            T.op("pool", lambda e: e.tensor_tensor(out=qa[:, :, 64:80], in0=rt[0][:, :, :], in1=rt[1][:, :, :],
                                                   op=ALU.subtract), [rt[0], rt[1]], [qa])
            T.op("pool", lambda e: e.tensor_tensor(out=qa[:, :, 80:96], in0=rt[2][:, :, :], in1=rt[3][:, :, :],
                                                   op=ALU.add), [rt[2], rt[3]], [qa])
            if ksub == 24: continue
            T.op("dve", lambda e: e.tensor_tensor(out=krt[0][:, :], in0=krs[:, 0:16], in1=cosT[:, tt, :], op=ALU.mult),
                 [krs, cosT], [krt[0]])
            T.op("dve", lambda e: e.tensor_tensor(out=krt[1][:, :], in0=krs[:, 16:32], in1=sinT[:, tt, :], op=ALU.mult),
                 [krs, sinT], [krt[1]])
            T.op("dve", lambda e: e.tensor_tensor(out=krt[2][:, :], in0=krs[:, 0:16], in1=sinT[:, tt, :], op=ALU.mult),
                 [krs, sinT], [krt[2]])
            T.op("dve", lambda e: e.tensor_tensor(out=krt[3][:, :], in0=krs[:, 16:32], in1=cosT[:, tt, :], op=ALU.mult),
                 [krs, cosT], [krt[3]])
            T.op("pool", lambda e: e.tensor_tensor(out=kr[:, 0:16], in0=krt[0][:, :], in1=krt[1][:, :],
                                                   op=ALU.subtract), [krt[0], krt[1]], [kr])
            T.op("pool", lambda e: e.tensor_tensor(out=kr[:, 16:32], in0=krt[2][:, :], in1=krt[3][:, :],
                                                   op=ALU.add), [krt[2], krt[3]], [kr])
            T.op("dve", lambda e: e.tensor_scalar(out=ka[:, :, 64:96],
                                                   in0=kr[:, :].unsqueeze(1).broadcast_to([128, 8, 32]),
                                                   scalar1=1.0, scalar2=None, op0=ALU.mult), [kr], [ka])
            if ksub == 25: continue
            T.dma("sp", [(Vm[tsl, :, :], va[:, :, :])], reads=[va])
            if ksub == 3: continue
            for (src, dst, dr) in ((qa, stQ, QTm), (ka, stK, KTm)):
                for h in range(8):
                    T.op("pe", lambda e: e.transpose(out=pB[0:96, h * 128:(h + 1) * 128], in_=src[:, h, :],
                                                     identity=ident_bf[:, :]), [src, ident_bf], [pB], inc=(h == 7))
                T.op("act", lambda e: e.activation(out=dst[0:96, :, :],
                                                   in_=pB[0:96, :].rearrange("p (h t) -> p h t", t=128),
                                                   func=AF.Copy), [pB], [dst])
                T.dma("sp", [(dr.rearrange("h r t -> r h t")[:, :, tsl], dst[0:96, :, :])], reads=[dst])
        phase_end()
        es.close()

    def phase_A1(Win=None):
        es = contextlib.ExitStack()
        if Win is None:
            Win = load_w(es, "c_win", c_w_in, 8, 3088)
        nbf = sb(es, "a1_nbf", [16, 1], F32)
        T.dma("sp", [(nbf[:, :], c_b_f.rearrange("(h o) -> h o", o=1))], writes=[nbf])
        xt = [sb(es, f"a1_x{i}", [128, D], F32) for i in range(2)]
        xTg = [sb(es, f"a1_xT{i}", [128, 8, 512], BF16) for i in range(2)]
        FL = sb(es, "a1_FL", [16, S], F32)
        G = sb(es, "a1_G", [16, S], F32)
        R1 = sb(es, "a1_R1", [16, S], F32)
        dq = [sb(es, f"a1_dq{i}", [16, S], BF16) for i in range(3)]
        dk = [sb(es, f"a1_dk{i}", [16, S], BF16) for i in range(3)]
        stQ = sb(es, "a1_stQ", [128, 8, 512], BF16)
        stK = sb(es, "a1_stK", [128, 8, 512], BF16)
        stV = [sb(es, f"a1_stV{i}", [128, 1024], BF16) for i in range(2)]
        pT = ps(es, "a1_pT", [128, 512], F32)
        pQ = [ps(es, f"a1_pQ{i}", [128, 512], F32) for i in range(2)]
        pK = [ps(es, f"a1_pK{i}", [128, 512], F32) for i in range(2)]
        pV = ps(es, "a1_pV", [128, 1024], F32)
        pF = ps(es, "a1_pF", [128, 512], F32)
        T.op("dve", lambda e: e.tensor_scalar(out=nbf[:, :], in0=nbf[:, :], scalar1=-1.0, scalar2=None, op0=ALU.mult),
             [nbf], [nbf])
        NG = NT // 4
        T.dma("sp", [(xt[0][:, :], X1[0:128, :])], writes=[xt[0]])
        for g in range(NG):
            xT = xTg[g % 2]
            gsl = slice(g * 512, (g + 1) * 512)
            for ti in range(4):
                tt = g * 4 + ti
                x = xt[tt % 2]
                if tt + 1 < NT:
                    nx = xt[(tt + 1) % 2]
                    T.dma("sp", [(nx[:, :], X1[(tt + 1) * 128:(tt + 2) * 128, :])], writes=[nx])
                for h0 in (0, 4):
                    for i in range(4):
                        T.op("pe", lambda e: e.transpose(out=pT[:, i * 128:(i + 1) * 128],
                                                         in_=x[:, (h0 + i) * 128:(h0 + i + 1) * 128],
                                                         identity=ident_f), [x, cst], [pT], inc=(i == 3))
                    if h0 == 0:
                        T.op("act", lambda e: e.activation(out=xT[:, 0:4, ti * 128:(ti + 1) * 128],
                                                           in_=pT[:, :].rearrange("p (c t) -> p c t", t=128),
                                                           func=AF.Copy), [pT], [xT])
                    else:
                        T.op("dve", lambda e: e.tensor_copy(out=xT[:, 4:8, ti * 128:(ti + 1) * 128],
                                                            in_=pT[:, :].rearrange("p (c t) -> p c t", t=128)),
                             [pT], [xT])
                sv = stV[tt % 2]
                for n0 in (0, 512):
                    for kc in range(8):
                        T.op("pe", lambda e: e.matmul(out=pV[:, n0:n0 + 512], lhsT=xT[:, kc, ti * 128:(ti + 1) * 128],
                                                      rhs=Win[:, kc, 2048 + n0:2048 + n0 + 512],
                                                      start=(kc == 0), stop=(kc == 7)), [xT, Win], [pV],
                             inc=(kc == 7 and n0 == 512))
                T.op("act", lambda e: e.activation(out=sv[:, 0:512], in_=pV[:, 0:512], func=AF.Copy), [pV], [sv])
                T.op("dve", lambda e: e.tensor_copy(out=sv[:, 512:1024], in_=pV[:, 512:1024]), [pV], [sv])
                T.dma("sp", [(Vf[tt * 128:(tt + 1) * 128, :, :], sv[:, :].rearrange("p (h d) -> p h d", d=64))],
                      reads=[sv])
            for (pp2, st_, c0, dr) in ((pQ, stQ, 0, QTf), (pK, stK, 1024, KTf)):
                for c in range(8):
                    pb = pp2[c % 2]
                    for kc in range(8):
                        T.op("pe", lambda e: e.matmul(out=pb[:, 0:512],
                                                      lhsT=Win[:, kc, c0 + c * 128:c0 + (c + 1) * 128],
                                                      rhs=xT[:, kc, :], start=(kc == 0), stop=(kc == 7)),
                             [xT, Win], [pb], inc=(kc == 7))
                    if c % 2 == 0:
                        T.op("act", lambda e: e.activation(out=st_[:, c, :], in_=pb[:, 0:512], func=AF.Copy),
                             [pb], [st_])
                    else:
                        T.op("dve", lambda e: e.tensor_copy(out=st_[:, c, :], in_=pb[:, 0:512]), [pb], [st_])
                T.dma("sp", [(dr.rearrange("(c p) t -> p c t", p=128)[:, :, gsl], st_[:, :, :])], reads=[st_])
            for kc in range(8):
                T.op("pe", lambda e: e.matmul(out=pF[0:16, 0:512], lhsT=Win[:, kc, 3072:3088], rhs=xT[:, kc, :],
                                              start=(kc == 0), stop=(kc == 7)), [xT, Win], [pF], inc=(kc == 7))
            T.op("dve", lambda e: e.tensor_copy(out=FL[:, gsl], in_=pF[0:16, 0:512]), [pF], [FL])
        T.op("act", lambda e: e.activation(out=FL[:, :], in_=FL[:, :], func=AF.Exp, scale=-1.0, bias=nbf[:, 0:1]),
             [FL, nbf], [FL])
        T.op("act", lambda e: e.activation(out=FL[:, :], in_=FL[:, :], func=AF.Ln, bias=1.0), [FL], [FL])
        for c in range(S // 512):
            csl = slice(c * 512, (c + 1) * 512)
            ini = 0.0 if c == 0 else G[:, c * 512 - 1:c * 512]
            T.op("dve", lambda e: e.tensor_tensor_scan(out=G[:, csl], data0=ones_f[0:16, 0:512],
                                                        data1=FL[:, csl], initial=ini, op0=ALU.mult, op1=ALU.add),
                 [FL, ones_f, G], [G])
        T.op("dve", lambda e: e.tensor_scalar(out=G[:, :], in0=G[:, :], scalar1=-8.0, scalar2=None, op0=ALU.mult),
             [G], [G])
        cur = G
        for i in range(3):
            T.op("dve", lambda e: e.tensor_copy(out=dq[i][:, :], in_=cur[:, :]), [cur], [dq[i]])
            T.op("dve", lambda e: e.tensor_scalar(out=dk[i][:, :], in0=dq[i][:, :], scalar1=-1.0, scalar2=None,
                                                   op0=ALU.mult), [dq[i]], [dk[i]])
            if i < 2:
                nxt = R1
                T.op("dve", lambda e: e.tensor_tensor(out=nxt[:, :], in0=cur[:, :], in1=dq[i][:, :], op=ALU.subtract),
                     [cur, dq[i]], [nxt])
                cur = nxt
            T.dma("sp", [(QDf[i], dq[i][:, :])], reads=[dq[i]])
            T.dma("sp", [(KDf[i], dk[i][:, :])], reads=[dk[i]])
        phase_end()
        es.close()

    def softmax_heads(es, Oall, heads, kd, scale, mask_bf, loader, QTa, KTa, Va, PT, pS, pO):
        NQB = S // 512
        first = True
        for hi, (h, ocol) in enumerate(heads):
            sl = hi % 2
            if first:
                loader(h, QTa[sl], KTa[sl], Va[sl])
                first = False
            if hi + 1 < len(heads):
                loader(heads[hi + 1][0], QTa[1 - sl], KTa[1 - sl], Va[1 - sl])
            Q, K, V = QTa[sl], KTa[sl], Va[sl]
            tiles = []
            for j in range(NQB):
                for kt in range(4 * j + 4):
                    tiles.append((j, kt))
            rn = sb_small["rn"]

            NS = len(pS)

            def qk(idx):
                j, kt = tiles[idx]
                i = max(0, kt - 4 * j)
                c0 = 128 * i
                pb = pS[idx % NS]
                T.op("pe", lambda e: e.matmul(out=pb[:, c0:512], lhsT=K[0:kd, kt * 128:(kt + 1) * 128],
                                              rhs=Q[0:kd, j * 512 + c0:(j + 1) * 512], start=True, stop=True),
                     [K, Q], [pb])

            for i0 in range(min(NS - 1, len(tiles))):
                qk(i0)
            for idx, (j, kt) in enumerate(tiles):
                if idx + NS - 1 < len(tiles):
                    qk(idx + NS - 1)
                i = max(0, kt - 4 * j)
                c0 = 128 * i
                pb = pS[idx % NS]
                pt = PT[idx % 3]
                T.op("act", lambda e: e.activation(out=pt[:, c0:512], in_=pb[:, c0:512], func=AF.Exp, scale=scale),
                     [pb], [pt])
                if kt >= 4 * j:
                    T.op("dve", lambda e: e.tensor_tensor(out=pt[:, c0:c0 + 128], in0=pt[:, c0:c0 + 128],
                                                           in1=mask_bf[:, :], op=ALU.mult), [pt, mask_bf], [pt])
                for sub in range(i, 4):
                    last = (kt == 4 * j + sub)
                    T.op("pe", lambda e: e.matmul(out=pO[sub][:, 0:65], lhsT=pt[:, sub * 128:(sub + 1) * 128],
                                                  rhs=V[:, kt, :], start=(kt == 0), stop=last),
                         [pt, V], [pO[sub]], inc=(last or sub == 3))
                    if last:
                        T.op("dve", lambda e: e.reciprocal(out=rn[:, sub:sub + 1], in_=pO[sub][:, 64:65]),
                             [pO[sub]], [rn])
                        T.op("dve", lambda e: e.tensor_scalar(out=Oall[:, j * 4 + sub, ocol:ocol + 64],
                                                               in0=pO[sub][:, 0:64], scalar1=rn[:, sub:sub + 1],
                                                               scalar2=None, op0=ALU.mult), [pO[sub], rn], [Oall])

    sb_small = {}

    def sb_heads(es, Oall, QTa, KTa, Va, wk, pS, pO, pTb):
        E_, L_, C_, X_, att, attT = wk
        for h in range(8):
            sl = h % 2

            def loader(hh, s_):
                T.dma("sp", [(QTa[s_][0:64, :], QTs[hh * 64:(hh + 1) * 64, :]),
                             (KTa[s_][0:64, :], KTs[hh * 64:(hh + 1) * 64, :])], writes=[QTa[s_], KTa[s_]])
                T.dma("sp", [(Va[s_][:, :, 0:64], Vs.rearrange("(t p) h d -> p t h d", p=128)[:, :, hh, :])],
                      writes=[Va[s_]])
            if h == 0:
                loader(0, 0)
            if h + 1 < 8:
                loader(h + 1, 1 - sl)
            Q, K, V = QTa[sl], KTa[sl], Va[sl]
            chunks = []
            for qt in range(NT):
                r0 = NT - 1 - qt
                n = qt + 1
                nchunk = (n + 3) // 4
                for c in range(nchunk):
                    cnt = min(4, n - 4 * c)
                    chunks.append((qt, c, r0 + 4 * c, cnt, c == nchunk - 1))
            NCH = len(chunks)

            def qk(idx):
                qt, c, t0, cnt, lastc = chunks[idx]
                w = 128 * cnt
                pb = pS[idx % 3]
                T.op("pe", lambda e: e.matmul(out=pb[:, 0:w], lhsT=Q[0:64, qt * 128:(qt + 1) * 128],
                                              rhs=K[0:64, t0 * 128:t0 * 128 + w], start=True, stop=True),
                     [Q, K], [pb])

            def f_E(idx):
                qt, c, t0, cnt, lastc = chunks[idx]
                w = 128 * cnt
                T.op("act", lambda e: e.activation(out=E_[idx % 4][:, 0:w], in_=pS[idx % 3][:, 0:w], func=AF.Exp,
                                                   scale=0.125), [pS[idx % 3]], [E_[idx % 4]])

            def f_L(idx):
                qt, c, t0, cnt, lastc = chunks[idx]
                w = 128 * cnt
                Lb = L_[idx % 2]
                T.op("act", lambda e: e.activation(out=Lb[:, 0:w], in_=E_[idx % 4][:, 0:w], func=AF.Ln, bias=1.0),
                     [E_[idx % 4]], [Lb])
                if c == 0:
                    T.op("pool", lambda e: e.tensor_tensor(out=Lb[:, 0:128], in0=Lb[:, 0:128], in1=mS_f,
                                                           op=ALU.mult), [Lb, cst], [Lb])

            def f_scan(idx):
                qt, c, t0, cnt, lastc = chunks[idx]
                w = 128 * cnt
                Lb, Cb = L_[idx % 2], C_[idx % 3]
                if c == 0:
                    T.op("dve", lambda e: e.tensor_tensor_scan(out=Cb[:, 0:w], data0=ones_f[:, 0:w],
                                                                data1=Lb[:, 0:w], initial=0.0,
                                                                op0=ALU.mult, op1=ALU.add), [Lb, ones_f], [Cb])
                else:
                    Cp = C_[(idx - 1) % 3]
                    T.op("dve", lambda e: e.tensor_tensor_scan(out=Cb[:, 0:w], data0=ones_f[:, 0:w],
                                                                data1=Lb[:, 0:w], initial=Cp[:, 511:512],
                                                                op0=ALU.mult, op1=ALU.add), [Lb, ones_f, Cp], [Cb])

            def b_X(idx):
                qt, c, t0, cnt, lastc = chunks[idx]
                w = 128 * cnt
                T.op("act", lambda e: e.activation(out=X_[idx % 2][:, 0:w], in_=C_[idx % 3][:, 0:w], func=AF.Exp,
                                                   scale=-1.0), [C_[idx % 3]], [X_[idx % 2]])

            def b_att(idx):
                qt, c, t0, cnt, lastc = chunks[idx]
                w = 128 * cnt
                ab = att[idx % 2]
                T.op("dve", lambda e: e.tensor_tensor(out=ab[:, 0:w], in0=E_[idx % 4][:, 0:w], in1=X_[idx % 2][:, 0:w],
                                                      op=ALU.mult), [E_[idx % 4], X_[idx % 2]], [ab])
                if c == 0:
                    T.op("pool", lambda e: e.tensor_tensor(out=ab[:, 0:128], in0=ab[:, 0:128], in1=mS_bf[:, :],
                                                           op=ALU.mult), [ab, mS_bf], [ab])

            pTv = [pTb[:, 0:512], pO[3][:, :].bitcast(BF16)[:, 0:512]]
            pTb2 = [pTb, pO[3]]

            def b_T(idx):
                qt, c, t0, cnt, lastc = chunks[idx]
                ab = att[idx % 2]
                ptb = pTb2[idx % 2]
                pv = pTv[idx % 2]
                for m in range(cnt):
                    T.op("pe", lambda e: e.transpose(out=pv[:, m * 128:(m + 1) * 128],
                                                     in_=ab[:, m * 128:(m + 1) * 128], identity=ident_bf[:, :]),
                         [ab, ident_bf], [ptb], inc=(m == cnt - 1))

            def b_evac(idx):
                qt, c, t0, cnt, lastc = chunks[idx]
                w = 128 * cnt
                pv = pTv[idx % 2]
                if idx % 2 == 0:
                    T.op("act", lambda e: e.activation(out=attT[idx % 2][:, 0:w], in_=pv[:, 0:w], func=AF.Copy),
                         [pTb2[idx % 2]], [attT[idx % 2]])
                else:
                    T.op("dve", lambda e: e.tensor_copy(out=attT[idx % 2][:, 0:w], in_=pv[:, 0:w]),
                         [pTb2[idx % 2]], [attT[idx % 2]])

            def b_PV(idx):
                qt, c, t0, cnt, lastc = chunks[idx]
                aT = attT[idx % 2]
                po = pO[qt % 2]
                for m in range(cnt):
                    lastm = lastc and (m == cnt - 1)
                    T.op("pe", lambda e: e.matmul(out=po[:, 0:64], lhsT=aT[:, m * 128:(m + 1) * 128],
                                                  rhs=V[:, t0 + m, 0:64], start=(c == 0 and m == 0), stop=lastm),
                         [aT, V], [po], inc=(m == cnt - 1))
                if lastc:
                    T.op("act", lambda e: e.activation(out=Oall[:, qt, 512 + h * 64:512 + (h + 1) * 64],
                                                       in_=po[:, 0:64], func=AF.Copy), [po], [Oall])

            for i in range(min(3, NCH)):
                qk(i)
            for i in range(min(2, NCH)):
                f_E(i)
                f_L(i)
                f_scan(i)
            for idx in range(NCH + 2):
                if idx + 3 < NCH:
                    qk(idx + 3)
                if idx + 2 < NCH:
                    f_E(idx + 2)
                if idx < NCH:
                    b_X(idx)
                if idx + 2 < NCH:
                    f_L(idx + 2)
                if idx < NCH:
                    b_att(idx)
                if idx + 2 < NCH:
                    f_scan(idx + 2)
                if SBLAG == 2:
                    if 2 <= idx:
                        b_PV(idx - 2)
                    if idx < NCH:
                        b_T(idx)
                    if 1 <= idx <= NCH:
                        b_evac(idx - 1)
                else:
                    if 1 <= idx <= NCH:
                        b_PV(idx - 1)
                    if idx < NCH:
                        b_T(idx)
                        b_evac(idx)

    def phase_B0(es_outer, hook=None):
        Oall = sb(es_outer, "Oall", [128, NT, D], BF16)
        GBUFS.append(Oall)
        if hook is not None:
            hook(es_outer)
        es = contextlib.ExitStack()
        QTa = [sb(es, f"b_QTa{i}", [128, S], BF16) for i in range(2)]
        KTa = [sb(es, f"b_KTa{i}", [128, S], BF16) for i in range(2)]
        Va = [sb(es, f"b_Va{i}", [128, NT, 65], BF16) for i in range(2)]
        PT = [sb(es, f"b_PT{i}", [128, 512], BF16) for i in range(3)]
        sb_small["rn"] = sb(es, "b_rn", [128, 4], F32)
        E_ = [sb(es, f"b_E{i}", [128, 512], F32) for i in range(4)]
        L_ = [sb(es, f"b_L{i}", [128, 512], F32) for i in range(2)]
        C_ = [sb(es, f"b_C{i}", [128, 512], F32) for i in range(3)]
        X_ = [sb(es, f"b_X{i}", [128, 512], F32) for i in range(2)]
        att = [sb(es, f"b_att{i}", [128, 512], BF16) for i in range(2)]
        attT = [sb(es, f"b_attT{i}", [128, 512], BF16) for i in range(2)]
        pS = [ps(es, f"b_pS{i}", [128, 512], F32) for i in range(3)]
        pO = [ps(es, f"b_pO{i}", [128, 512], F32) for i in range(4)]
        pTb = ps(es, "b_pTb", [128, 1024], BF16)
        for i in range(2):
            T.op("pool", lambda e: e.memset(Va[i][:, :, 64:65], 1.0), [], [Va[i]])

        def loader(h, Q, K, V):
            T.dma("sp", [(Q[0:96, :], QTm[h]), (K[0:96, :], KTm[h])], writes=[Q, K])
            T.dma("sp", [(V[:, :, 0:64], Vm.rearrange("(t p) h d -> p t h d", p=128)[:, :, h, :])], writes=[V])

        softmax_heads(es, Oall, [(h, h * 64) for h in range(8)], 96, 96.0 ** -0.5, mM_bf, loader,
                      QTa, KTa, Va, PT, pS, pO)
        sb_heads(es, Oall, QTa, KTa, Va, (E_, L_, C_, X_, att, attT), pS, pO, pTb)
        phase_end()
        es.close()
        return Oall

    def phase_B1(es_outer, hook=None):
        Oall = sb(es_outer, "Oall1", [128, NT, D], BF16)
        GBUFS.append(Oall)
        if hook is not None:
            hook(es_outer)
        es = contextlib.ExitStack()
        QTa = [sb(es, f"b1_QTa{i}", [128, S], BF16) for i in range(2)]
        KTa = [sb(es, f"b1_KTa{i}", [128, S], BF16) for i in range(2)]
        Va = [sb(es, f"b1_Va{i}", [128, NT, 65], BF16) for i in range(2)]
        PT = [sb(es, f"b1_PT{i}", [128, 512], BF16) for i in range(3)]
        sb_small["rn"] = sb(es, "b1_rn", [128, 4], F32)
        pS = [ps(es, f"b1_pS{i}", [128, 512], F32) for i in range(4)]
        pO = [ps(es, f"b1_pO{i}", [128, 512], F32) for i in range(4)]
        for i in range(2):
            T.op("pool", lambda e: e.memset(Va[i][:, :, 64:65], 1.0), [], [Va[i]])
            T.op("pool", lambda e: e.memset(QTa[i][64:70, :], 1.0), [], [QTa[i]])
            T.op("pool", lambda e: e.memset(KTa[i][64:70, :], 1.0), [], [KTa[i]])

        def loader(h, Q, K, V):
            T.dma("sp", [(Q[0:64, :], QTf[h * 64:(h + 1) * 64, :]), (Q[64:67, :], QDf[:, h, :]),
                         (K[0:64, :], KTf[h * 64:(h + 1) * 64, :]), (K[67:70, :], KDf[:, h, :])], writes=[Q, K])
            T.dma("sp", [(V[:, :, 0:64], Vf.rearrange("(t p) h d -> p t h d", p=128)[:, :, h, :])], writes=[V])

        softmax_heads(es, Oall, [(h, h * 64) for h in range(16)], 70, 0.125, mC_bf, loader,
                      QTa, KTa, Va, PT, pS, pO)
        phase_end()
        es.close()
        return Oall

    def preload_C1(es, li, w_out):
        Wout = load_w(es, "c1_wout", w_out, 8, D)
        g_bc = load_bc(es, "c1_g", ln1_g[li])
        b_bc = load_bc(es, "c1_b", ln1_b[li])
        GBUFS.extend([Wout, g_bc, b_bc])
        return [Wout, g_bc, b_bc]

    def phase_C1(li, Oall, pre, xsrc):
        es = contextlib.ExitStack()
        Wout, g_bc, b_bc = pre
        xt = [sb(es, f"c1_x{i}", [128, D], F32) for i in range(2)]
        rr = [sb(es, f"c1_r{i}", [128, D], F32) for i in range(2)]
        OTs = [sb(es, f"c1_OT{i}", [128, 8, 128], BF16) for i in range(2)]
        st = sb(es, "c1_st", [128, 32], F32)
        pB = ps(es, "c1_pB", [128, 1024], BF16)
        pMs = [ps(es, f"c1_pM{i}", [128, 1024], F32) for i in range(2)]

        def front(tt):
            OT, pM = OTs[tt % 2], pMs[tt % 2]
            for c in range(8):
                T.op("pe", lambda e: e.transpose(out=pB[:, c * 128:(c + 1) * 128], in_=Oall[:, tt, c * 128:(c + 1) * 128],
                                                 identity=ident_bf[:, :]), [Oall, ident_bf], [pB], inc=(c == 7))
            T.op("act", lambda e: e.activation(out=OT[:, :, :], in_=pB[:, :].rearrange("p (c t) -> p c t", t=128),
                                               func=AF.Copy), [pB], [OT])
            for n0 in (0, 512):
                for kc in range(8):
                    T.op("pe", lambda e: e.matmul(out=pM[:, n0:n0 + 512], lhsT=OT[:, kc, :], rhs=Wout[:, kc, n0:n0 + 512],
                                                  start=(kc == 0), stop=(kc == 7)), [OT, Wout], [pM],
                         inc=(kc == 7 and n0 == 512))

        def back(tt):
            x, r, pM = xt[tt % 2], rr[tt % 2], pMs[tt % 2]
            for n0 in (0, 512):
                T.op("dve", lambda e: e.scalar_tensor_tensor(out=r[:, n0:n0 + 512], in0=x[:, n0:n0 + 512], scalar=ALPHA,
                                                              in1=pM[:, n0:n0 + 512], op0=ALU.mult, op1=ALU.add),
                     [x, pM], [r])
            layer_norm(r, g_bc, b_bc, r, st)
            T.dma("sp", [(Y1[tt * 128:(tt + 1) * 128, :], r[:, :])], reads=[r])

        T.dma("sp", [(xt[0][:, :], xsrc[0:128, :])], writes=[xt[0]])
        front(0)
        for tt in range(NT):
            if tt + 1 < NT:
                nx = xt[(tt + 1) % 2]
                T.dma("sp", [(nx[:, :], xsrc[(tt + 1) * 128:(tt + 2) * 128, :])], writes=[nx])
                front(tt + 1)
            back(tt)
        phase_end()
        es.close()

    def phase_C2(li, W13):
        es = contextlib.ExitStack()
        W1, W3 = W13
        W2 = load_w(es, "c2_w2", ffn_w2[li], NFC, D)
        g_bc = load_bc(es, "c2_g", ln2_g[li])
        b_bc = load_bc(es, "c2_b", ln2_b[li])
        yt = [sb(es, f"c2_y{i}", [128, D], F32) for i in range(2)]
        rr = [sb(es, f"c2_r{i}", [128, D], F32) for i in range(2)]
        yTs = [sb(es, f"c2_yT{i}", [128, 8, 128], BF16) for i in range(2)]
        sl_ = [sb(es, f"c2_s{i}", [128, 512], F32) for i in range(2)]
        ggs = [sb(es, f"c2_g2{i}", [128, DFF], BF16) for i in range(2)]
        gT = sb(es, "c2_gT", [128, NFC, 128], BF16)
        st = sb(es, "c2_st", [128, 32], F32)
        pT = ps(es, "c2_pT", [128, 512], F32)
        pH1 = [ps(es, f"c2_pH1{i}", [128, 512], F32) for i in range(2)]
        pH3 = [ps(es, f"c2_pH3{i}", [128, 512], F32) for i in range(2)]
        pB = ps(es, "c2_pB", [128, 1024], BF16)
        pO = ps(es, "c2_pO", [128, 1024], F32)
        blocks = [(n0, min(512, DFF - n0)) for n0 in range(0, DFF, 512)]

        def emit_H(tt):
            yT, gg = yTs[tt % 2], ggs[tt % 2]
            for bi, (n0, w) in enumerate(blocks):
                p1, p3, s_ = pH1[bi % 2], pH3[bi % 2], sl_[bi % 2]
                for kc in range(8):
                    T.op("pe", lambda e: e.matmul(out=p1[:, 0:w], lhsT=yT[:, kc, :], rhs=W1[:, kc, n0:n0 + w],
                                                  start=(kc == 0), stop=(kc == 7)), [yT, W1], [p1], inc=(kc == 7))
                for kc in range(8):
                    T.op("pe", lambda e: e.matmul(out=p3[:, 0:w], lhsT=yT[:, kc, :], rhs=W3[:, kc, n0:n0 + w],
                                                  start=(kc == 0), stop=(kc == 7)), [yT, W3], [p3], inc=(kc == 7))
                T.op("act", lambda e: e.activation(out=s_[:, 0:w], in_=p1[:, 0:w], func=AF.Silu), [p1], [s_])
                T.op("dve", lambda e: e.tensor_tensor(out=gg[:, n0:n0 + w], in0=s_[:, 0:w], in1=p3[:, 0:w],
                                                       op=ALU.mult), [s_, p3], [gg])

        def emit_tail(tt):
            y, r, gg = yt[tt % 2], rr[tt % 2], ggs[tt % 2]
            for f0 in range(0, NFC, 8):
                n = min(8, NFC - f0)
                for i in range(n):
                    T.op("pe", lambda e: e.transpose(out=pB[:, i * 128:(i + 1) * 128],
                                                     in_=gg[:, (f0 + i) * 128:(f0 + i + 1) * 128],
                                                     identity=ident_bf[:, :]), [gg, ident_bf], [pB], inc=(i == n - 1))
                T.op("act", lambda e: e.activation(out=gT[:, f0:f0 + n, :],
                                                   in_=pB[:, 0:n * 128].rearrange("p (c t) -> p c t", t=128),
                                                   func=AF.Copy), [pB], [gT])
            for n0 in (0, 512):
                for fc in range(NFC):
                    T.op("pe", lambda e: e.matmul(out=pO[:, n0:n0 + 512], lhsT=gT[:, fc, :], rhs=W2[:, fc, n0:n0 + 512],
                                                  start=(fc == 0), stop=(fc == NFC - 1)), [gT, W2], [pO],
                         inc=(fc == NFC - 1 and n0 == 512))
            for n0 in (0, 512):
                T.op("dve", lambda e: e.scalar_tensor_tensor(out=r[:, n0:n0 + 512], in0=y[:, n0:n0 + 512], scalar=ALPHA,
                                                              in1=pO[:, n0:n0 + 512], op0=ALU.mult, op1=ALU.add),
                     [y, pO], [r])
            layer_norm(r, g_bc, b_bc, r, st)
            T.dma("sp", [(Y2[tt * 128:(tt + 1) * 128, :], r[:, :])], reads=[r])

        T.dma("sp", [(yt[0][:, :], Y1[0:128, :])], writes=[yt[0]])
        transpose_f32_tile(yt[0], yTs[0], pT, 8)
        for tt in range(NT):
            if tt + 1 < NT:
                ny = yt[(tt + 1) % 2]
                T.dma("sp", [(ny[:, :], Y1[(tt + 1) * 128:(tt + 2) * 128, :])], writes=[ny])
            emit_H(tt)
            if tt + 1 < NT:
                transpose_f32_tile(yt[(tt + 1) % 2], yTs[(tt + 1) % 2], pT, 8)
            emit_tail(tt)
        phase_end()
        es.close()

    def phase_C3(li, dst):
        es = contextlib.ExitStack()
        Wg = load_w(es, "c3_wg", ple_w_gate[li], 8, D)
        Wp = load_w(es, "c3_wp", ple_w_proj[li], 2, D)
        bg_bc = load_bc(es, "c3_bg", ple_b_gate[li])
        yt = [sb(es, f"c3_y{i}", [128, D], F32) for i in range(3)]
        pt_ = [sb(es, f"c3_p{i}", [128, 256], F32) for i in range(3)]
        oo = [sb(es, f"c3_o{i}", [128, D], F32) for i in range(2)]
        tgs = [sb(es, f"c3_tg{i}", [128, D], F32) for i in range(2)]
        yTs = [sb(es, f"c3_yT{i}", [128, 8, 128], BF16) for i in range(2)]
        pTs = [sb(es, f"c3_pT{i}", [128, 2, 128], BF16) for i in range(2)]
        pT = ps(es, "c3_pTp", [128, 512], F32)
        pG = ps(es, "c3_pG", [128, 1024], F32)
        pP = ps(es, "c3_pP", [128, 1024], F32)

        def loads(tt):
            T.dma("sp", [(yt[tt % 3][:, :], Y2[tt * 128:(tt + 1) * 128, :])], writes=[yt[tt % 3]])
            T.dma("sp", [(pt_[tt % 3][:, :], p_in[li, tt * 128:(tt + 1) * 128, :])], writes=[pt_[tt % 3]])

        def frontA(tt):
            transpose_f32_tile(yt[tt % 3], yTs[tt % 2], pT, 8)
            transpose_f32_tile(pt_[tt % 3], pTs[tt % 2], pT, 2)

        def frontB(tt):
            yT, pT_ = yTs[tt % 2], pTs[tt % 2]
            for n0 in (0, 512):
                for kc in range(8):
                    T.op("pe", lambda e: e.matmul(out=pG[:, n0:n0 + 512], lhsT=yT[:, kc, :], rhs=Wg[:, kc, n0:n0 + 512],
                                                  start=(kc == 0), stop=(kc == 7)), [yT, Wg], [pG],
                         inc=(kc == 7 and n0 == 512))
            for n0 in (0, 512):
                for kc in range(2):
                    T.op("pe", lambda e: e.matmul(out=pP[:, n0:n0 + 512], lhsT=pT_[:, kc, :], rhs=Wp[:, kc, n0:n0 + 512],
                                                  start=(kc == 0), stop=(kc == 1)), [pT_, Wp], [pP],
                         inc=(kc == 1 and n0 == 512))

        def back1(tt):
            tg = tgs[tt % 2]
            for n0 in (0, 512):
                T.op("dve", lambda e: e.tensor_tensor(out=tg[:, n0:n0 + 512], in0=pG[:, n0:n0 + 512],
                                                       in1=bg_bc[:, n0:n0 + 512], op=ALU.add), [pG, bg_bc], [tg])
            T.op("act", lambda e: e.activation(out=tg[:, :], in_=tg[:, :], func=AF.Sigmoid), [tg], [tg])
            for n0 in (0, 512):
                T.op("dve", lambda e: e.tensor_tensor(out=tg[:, n0:n0 + 512], in0=tg[:, n0:n0 + 512],
                                                       in1=pP[:, n0:n0 + 512], op=ALU.mult), [tg, pP], [tg])

        def back2(tt):
            tg, y, o = tgs[tt % 2], yt[tt % 3], oo[tt % 2]
            T.op("pool", lambda e: e.tensor_tensor(out=o[:, :], in0=tg[:, :], in1=y[:, :], op=ALU.add),
                 [tg, y], [o])
            T.dma("sp", [(dst[tt * 128:(tt + 1) * 128, :], o[:, :])], reads=[o])

        loads(0)
        if NT > 1:
            loads(1)
        frontA(0)
        frontB(0)
        for tt in range(NT):
            if tt + 2 < NT:
                loads(tt + 2)
            if tt + 1 < NT:
                frontA(tt + 1)
            back1(tt)
            if tt + 1 < NT:
                frontB(tt + 1)
            back2(tt)
        phase_end()
        es.close()

    import os
    nstop = int(os.environ.get("KSTOP", "99"))
    phase_end()
    def _run():
        n = 0
        def chk():
            nonlocal n
            n += 1
            return n > nstop
        if chk(): return
        phase_A0()
        for li in range(2):
            eo = contextlib.ExitStack()
            c1w = []

            def hook(es_, li=li):
                c1w.extend(preload_C1(es_, li, a_w_out if li == 0 else c_w_out))

            Oall = phase_B0(eo, hook) if li == 0 else phase_B1(eo, hook)
            er = contextlib.ExitStack()
            W13 = [load_w(er, "c2_w1", ffn_w1[li], 8, DFF, side="right"),
                   load_w(er, "c2_w3", ffn_w3[li], 8, DFF, side="right")]
            GBUFS.extend(W13)
            phase_C1(li, Oall, c1w, x_in if li == 0 else X1)
            for b_ in c1w + [Oall]:
                GBUFS.remove(b_)
            eo.close()
            phase_C2(li, W13)
            for b_ in W13:
                GBUFS.remove(b_)
            er.close()
            if li == 0:
                era = contextlib.ExitStack()
                WinA1 = load_w(era, "c_win", c_w_in, 8, 3088, side="right")
                GBUFS.append(WinA1)
                phase_C3(0, X1)
                phase_A1(WinA1)
                GBUFS.remove(WinA1)
                era.close()
            else:
                phase_C3(1, out_d)
    _run()
    if os.environ.get("KSIM"):
        simulate_sync(T)
    gs.close()
    return nc


def make_consts(S):
    NT = S // 128
    p = np.arange(128)[:, None]
    f = np.arange(128)[None, :]
    c = np.zeros((128, 128 * 5 + NT + 16), np.float32)
    c[:, 0:128] = (p == f)
    c[:, 128:256] = (p + f == 127)
    c[:, 256:384] = (p <= f)
    c[:, 384:512] = ((p // 64) <= (f // 64))
    c[:, 512:640] = (p + f >= 128)
    c[:, 640:640 + NT] = (np.arange(NT)[None, :] * 128 + p).astype(np.float32)
    inv = (1.0 / (np.float32(10000.0) ** (np.arange(0, 32, 2, dtype=np.float32) / np.float32(32)))).astype(np.float32)
    c[:, 640 + NT:] = inv[None, :]
    return c


_CACHE = {}


def run(inputs, S):
    B = inputs["x"].shape[0]
    if S not in _CACHE:
        _CACHE[S] = build(S)
    nc = _CACHE[S]
    cst = make_consts(S)
    f32 = lambda a: np.ascontiguousarray(np.asarray(a, dtype=np.float32))
    shared = {
        "a_w_in": f32(inputs["a_w_in"][0]), "a_q_norm": f32(inputs["a_q_norm"][0]),
        "a_w_uq": f32(inputs["a_w_uq"][0]), "a_kv_norm": f32(inputs["a_kv_norm"][0]),
        "a_w_ukv": f32(inputs["a_w_ukv"][0]), "a_w_out": f32(inputs["a_w_out"][0]),
        "c_w_in": f32(inputs["c_w_in"][0]), "c_b_f": f32(inputs["c_b_f"][0]),
        "c_w_out": f32(inputs["c_w_out"][0]),
        "ffn_w1": f32(inputs["ffn_w1"]), "ffn_w3": f32(inputs["ffn_w3"]), "ffn_w2": f32(inputs["ffn_w2"]),
        "ln1_g": f32(inputs["ln1_g"]), "ln1_b": f32(inputs["ln1_b"]),
        "ln2_g": f32(inputs["ln2_g"]), "ln2_b": f32(inputs["ln2_b"]),
        "ple_w_proj": f32(inputs["ple_w_proj"]), "ple_w_gate": f32(inputs["ple_w_gate"]),
        "ple_b_gate": f32(inputs["ple_b_gate"]), "cst": cst,
    }
    xs = f32(inputs["x"])
    ps_ = f32(inputs["p"])
    in_maps = []
    for b in range(B):
        m = dict(shared)
        m["x"] = np.ascontiguousarray(xs[b])
        m["p"] = np.ascontiguousarray(ps_[:, b])
        in_maps.append(m)
    res = run_bass_kernel_spmd(nc, in_maps, core_ids=list(range(B)))
    return np.stack([np.asarray(r["out"], dtype=np.float32) for r in res.results], axis=0)


def kernel(**inputs):
    return run(inputs, int(inputs["x"].shape[1]))
```

```python
import contextlib
import math
import numpy as np
import concourse.bass as bass
import concourse.mybir as mybir
from concourse.bass_utils import run_bass_kernel_spmd

F32 = mybir.dt.float32
BF16 = mybir.dt.bfloat16
I32 = mybir.dt.int32
AF = mybir.ActivationFunctionType
ALU = mybir.AluOpType

D = 1024
DFF = 2816
NFC = DFF // 128
ALPHA = (2.0 * 2) ** 0.25
PI = math.pi


class Buf:
    def __init__(self, name, ap=None):
        self.name = name
        self.ap = ap
        self.w = None
        self.r = {}
        self.dsem = None

    def __getitem__(self, idx):
        return self.ap[idx]


class Eng:
    def __init__(self, name, eng):
        self.name = name
        self.eng = eng
        self.sem = None
        self.count = 0
        self.seen = {}


class Trk:
    def __init__(self, nc):
        self.nc = nc
        self.nsem = 0
        self.E = {n: Eng(n, getattr(nc, a)) for n, a in
                  [("pe", "tensor"), ("act", "scalar"), ("dve", "vector"), ("pool", "gpsimd"), ("sp", "sync")]}
        self.dpool = []
        self.dlive = []
        self.phase_bufs = []
        self.log = {n: [] for n in self.E}
        self.cur_waits = []
        self.new_epoch()

    def _new_sem(self, name):
        self.nsem += 1
        return (self.nsem, self.nc.alloc_semaphore(name=f"{name}_{self.nsem}"))

    def new_epoch(self):
        for e in self.E.values():
            e.sem = self._new_sem("e" + e.name)
            e.count = 0

    def buf(self, name, ap=None):
        b = Buf(name, ap)
        self.phase_bufs.append(b)
        return b

    def _wait(self, E, tok):
        key, h, val, en = tok
        if E.seen.get(key, 0) >= val:
            return
        E.eng.wait_ge(h, val)
        E.seen[key] = val
        self.cur_waits.append((key, val))

    def _deps(self, E, reads, writes):
        for b in reads:
            if b.w is not None:
                if not (b.w[3] == "pe" and E.name == "pe"):
                    self._wait(E, b.w)
        for b in writes:
            if b.w is not None:
                if not (b.w[3] == "pe" and E.name == "pe"):
                    self._wait(E, b.w)
            for tok in b.r.values():
                if not (tok[3] == "pe" and E.name == "pe"):
                    self._wait(E, tok)

    def op(self, en, fn, reads=(), writes=(), inc=True):
        E = self.E[en]
        self._deps(E, reads, writes)
        ins = fn(E.eng)
        self.log[en].append((self.cur_waits, (E.sem[0], E.count + 1, 1) if inc else None))
        self.cur_waits = []
        if inc:
            E.count += 1
            ins.then_inc(E.sem[1], 1)
            tok = (E.sem[0], E.sem[1], E.count, en)
            for b in reads:
                b.r[en] = tok
            for b in writes:
                b.w = tok
                b.r = {}
        return ins

    def dma(self, qn, pairs, reads=(), writes=(), **kw):
        E = self.E[qn]
        self._deps(E, reads, writes)
        owner = (list(writes) + list(reads))[0]
        if owner.dsem is None:
            if self.dpool:
                owner.dsem = self.dpool.pop()
            else:
                k, h = self._new_sem("d")
                owner.dsem = [k, h, 0]
                self.dlive.append(owner.dsem)
        ds = owner.dsem
        self.log[qn].append((self.cur_waits, (ds[0], 16 * (ds[2] + len(pairs)), 16 * len(pairs))))
        self.cur_waits = []
        for (o, i) in pairs:
            E.eng.dma_start(out=o, in_=i, **kw).then_inc(ds[1], 16)
            ds[2] += 1
        tok = (ds[0], ds[1], 16 * ds[2], "dma")
        for b in reads:
            b.r[("dma", ds[0])] = tok
        for b in writes:
            b.w = tok
            b.r = {}

    def barrier(self):
        toks = []
        for n, e in self.E.items():
            if e.count > 0:
                toks.append((e.sem[0], e.sem[1], e.count, n))
        for ds in self.dlive:
            if ds[2] > 0:
                toks.append((ds[0], ds[1], 16 * ds[2], "dma"))
        for E in self.E.values():
            for tok in toks:
                self._wait(E, tok)
            self.log[E.name].append((self.cur_waits, None))
            self.cur_waits = []
        self.new_epoch()
        for b in self.phase_bufs:
            if b.dsem is not None:
                self.dpool.append(b.dsem)
                b.dsem = None
            b.w = None
            b.r = {}
        self.phase_bufs = []


def simulate_sync(T):
    sem = {}
    ptr = {n: 0 for n in T.log}
    total = sum(len(v) for v in T.log.values())
    done = 0
    while done < total:
        prog = False
        for n, ops in T.log.items():
            while ptr[n] < len(ops):
                waits, prod = ops[ptr[n]]
                if all(sem.get(k, 0) >= v for (k, v) in waits):
                    if prod is not None:
                        sem[prod[0]] = sem.get(prod[0], 0) + prod[2]
                    ptr[n] += 1
                    done += 1
                    prog = True
                else:
                    break
        if not prog:
            print("DEADLOCK")
            for n, ops in T.log.items():
                if ptr[n] < len(ops):
                    waits, prod = ops[ptr[n]]
                    print(n, ptr[n], len(ops), [(k, v, sem.get(k, 0)) for (k, v) in waits if sem.get(k, 0) < v], prod)
            return False
    print("SYNC SIM OK", total)
    return True


import os as _os
SBLAG = int(_os.environ.get('SBLAG', '2'))


def build(S, taps=False):
    NT = S // 128
    nc = bass.Bass("TRN2", target_bir_lowering=False)
    T = Trk(nc)

    def din(name, shape, dt=F32):
        return nc.dram_tensor(name, list(shape), dt, kind="ExternalInput").ap()

    def dscr(name, shape, dt):
        return nc.dram_tensor(name, list(shape), dt, kind="Internal").ap()

    x_in = din("x", [S, D])
    p_in = din("p", [2, S, 256])
    a_w_in = din("a_w_in", [D, 2080])
    a_q_norm = din("a_q_norm", [256])
    a_w_uq = din("a_w_uq", [256, 768])
    a_kv_norm = din("a_kv_norm", [256])
    a_w_ukv = din("a_w_ukv", [256, 1024])
    a_w_out = din("a_w_out", [D, D])
    c_w_in = din("c_w_in", [D, 3088])
    c_b_f = din("c_b_f", [16])
    c_w_out = din("c_w_out", [D, D])
    ffn_w1 = din("ffn_w1", [2, D, DFF])
    ffn_w3 = din("ffn_w3", [2, D, DFF])
    ffn_w2 = din("ffn_w2", [2, DFF, D])
    ln1_g = din("ln1_g", [2, D])
    ln1_b = din("ln1_b", [2, D])
    ln2_g = din("ln2_g", [2, D])
    ln2_b = din("ln2_b", [2, D])
    ple_w_proj = din("ple_w_proj", [2, 256, D])
    ple_w_gate = din("ple_w_gate", [2, D, D])
    ple_b_gate = din("ple_b_gate", [2, D])
    NCST = 128 * 5 + NT + 16
    cst_in = din("cst", [128, NCST])
    out_d = nc.dram_tensor("out", [S, D], F32, kind="ExternalOutput").ap()

    QTm = dscr("QTm", [8, 96, S], BF16)
    KTm = dscr("KTm", [8, 96, S], BF16)
    Vm = dscr("Vm", [S, 8, 64], BF16)
    QTs = dscr("QTs", [512, S], BF16)
    KTs = dscr("KTs", [512, S], BF16)
    Vs = dscr("Vs", [S, 8, 64], BF16)
    QTf = dscr("QTf", [1024, S], BF16)
    KTf = dscr("KTf", [1024, S], BF16)
    QDf = dscr("QDf", [3, 16, S], BF16)
    KDf = dscr("KDf", [3, 16, S], BF16)
    Vf = dscr("Vf", [S, 16, 64], BF16)
    Y1 = dscr("Y1", [S, D], F32)
    Y2 = dscr("Y2", [S, D], F32)
    X1 = dscr("X1", [S, D], F32)

    gs = contextlib.ExitStack()

    uniq = [0]

    def sb(es, name, shape, dt, side=None):
        uniq[0] += 1
        if side is None:
            t = es.enter_context(nc.sbuf_tensor(f"{name}_u{uniq[0]}", list(shape), dt))
        else:
            t = es.enter_context(nc.sbuf_tensor(f"{name}_u{uniq[0]}", list(shape), dt, side=side))
        return T.buf(name, t)

    def ps(es, name, shape, dt):
        uniq[0] += 1
        t = es.enter_context(nc.psum_tensor(f"{name}_u{uniq[0]}", list(shape), dt))
        return T.buf(name, t)

    cst = sb(gs, "cst_sb", [128, NCST], F32)
    ident_bf = sb(gs, "ident_bf", [128, 128], BF16)
    mC_bf = sb(gs, "mC_bf", [128, 128], BF16)
    mM_bf = sb(gs, "mM_bf", [128, 128], BF16)
    mS_bf = sb(gs, "mS_bf", [128, 128], BF16)
    ones_f = sb(gs, "ones_f", [128, 512], F32)
    mhalf = sb(gs, "mhalf", [128, 1], F32)
    GBUFS = [cst, ident_bf, mC_bf, mM_bf, mS_bf, ones_f, mhalf]
    ident_f = cst[:, 0:128]
    J_f = cst[:, 128:256]
    mS_f = cst[:, 512:640]
    pos_f = cst[:, 640:640 + NT]
    invf = cst[:, 640 + NT:640 + NT + 16]

    T.dma("sp", [(cst[:, :], cst_in[:, :])], writes=[cst])
    T.op("dve", lambda e: e.tensor_copy(out=ident_bf[:, :], in_=cst[:, 0:128]), [cst], [ident_bf])
    T.op("dve", lambda e: e.tensor_copy(out=mC_bf[:, :], in_=cst[:, 256:384]), [cst], [mC_bf])
    T.op("dve", lambda e: e.tensor_copy(out=mM_bf[:, :], in_=cst[:, 384:512]), [cst], [mM_bf])
    T.op("dve", lambda e: e.tensor_copy(out=mS_bf[:, :], in_=cst[:, 512:640]), [cst], [mS_bf])
    T.op("dve", lambda e: e.memset(ones_f[:, :], 1.0), [], [ones_f])
    T.op("dve", lambda e: e.memset(mhalf[:, :], -0.5), [], [mhalf])

    def load_w(es, name, src, kc, n, q="pool", side=None):
        w = sb(es, name, [128, kc, n], BF16, side=side)
        v = src.rearrange("(c p) n -> p c n", p=128)
        T.dma(q, [(w[:, c, :], v[:, c, :]) for c in range(kc)], writes=[w], max_dma_last_dim=4096)
        return w

    def load_bc(es, name, src, n=D):
        t = sb(es, name, [128, n], F32)
        T.dma("sp", [(t[:, :], src.partition_broadcast(128))], writes=[t])
        return t

    def phase_end():
        T.barrier()
        for b in GBUFS:
            T.phase_bufs.append(b)

    def layer_norm(r, g_bc, b_bc, out, st, eps=1e-5):
        T.op("dve", lambda e: e.bn_stats(out=st[:, 0:6], in_=r[:, 0:512]), [r], [st])
        T.op("dve", lambda e: e.bn_stats(out=st[:, 6:12], in_=r[:, 512:1024]), [r], [st])
        T.op("dve", lambda e: e.bn_aggr(out=st[:, 12:14], in_=st[:, 0:12]), [st], [st])
        T.op("dve", lambda e: e.tensor_scalar(out=st[:, 14:15], in0=st[:, 13:14], scalar1=eps, scalar2=None,
                                               op0=ALU.add), [st], [st])
        T.op("pool", lambda e: e.tensor_tensor(out=st[:, 16:17], in0=st[:, 14:15], in1=mhalf[:, 0:1], op=ALU.pow),
             [st, mhalf], [st])
        T.op("dve", lambda e: e.scalar_tensor_tensor(out=st[:, 17:18], in0=st[:, 12:13], scalar=-1.0,
                                                      in1=st[:, 16:17], op0=ALU.mult, op1=ALU.mult), [st], [st])
        T.op("act", lambda e: e.activation(out=r[:, :], in_=r[:, :], func=AF.Identity,
                                           scale=st[:, 16:17], bias=st[:, 17:18]), [r, st], [r])
        T.op("dve", lambda e: e.tensor_tensor(out=r[:, :], in0=r[:, :], in1=g_bc[:, :], op=ALU.mult), [r, g_bc], [r])
        T.op("pool", lambda e: e.tensor_tensor(out=out[:, :], in0=r[:, :], in1=b_bc[:, :], op=ALU.add), [r, b_bc], [out])

    def transpose_f32_tile(src, dstT, pT, nblk):
        for h0 in range(0, nblk, 4):
            n = min(4, nblk - h0)
            for i in range(n):
                T.op("pe", lambda e, i=i: e.transpose(out=pT[:, i * 128:(i + 1) * 128],
                                                       in_=src[:, (h0 + i) * 128:(h0 + i + 1) * 128],
                                                       identity=ident_f), [src, cst], [pT], inc=(i == n - 1))
            eng = "act" if (h0 // 4) % 2 == 0 else "dve"
            if eng == "act":
                T.op("act", lambda e: e.activation(out=dstT[:, h0:h0 + n, :],
                                                   in_=pT[:, 0:n * 128].rearrange("p (c t) -> p c t", t=128),
                                                   func=AF.Copy), [pT], [dstT])
            else:
                T.op("dve", lambda e: e.tensor_copy(out=dstT[:, h0:h0 + n, :],
                                                    in_=pT[:, 0:n * 128].rearrange("p (c t) -> p c t", t=128)),
                     [pT], [dstT])

    def phase_A0():
        es = contextlib.ExitStack()
        Win = load_w(es, "a_win", a_w_in, 8, 2080)
        Wuq = load_w(es, "a_wuq", a_w_uq, 2, 768)
        Wukv = load_w(es, "a_wukv", a_w_ukv, 2, 1024)
        g4 = sb(es, "g4", [128, 4], F32)
        T.dma("sp", [(g4[:, 0:2], a_q_norm.rearrange("(c p) -> p c", p=128)),
                     (g4[:, 2:4], a_kv_norm.rearrange("(c p) -> p c", p=128))], writes=[g4],
              allow_slow_non_contiguous=True)
        cosT = sb(es, "cosT", [128, NT, 16], F32)
        sinT = sb(es, "sinT", [128, NT, 16], F32)
        ang = sb(es, "ang", [128, NT, 16], F32)
        tmpa = sb(es, "tmpa", [128, NT, 16], F32)
        ki = sb(es, "ki", [128, NT, 16], I32)
        kf = sb(es, "kf", [128, NT, 16], F32)
        T.op("dve", lambda e: e.tensor_tensor(out=ang[:, :, :], in0=pos_f.unsqueeze(2).broadcast_to([128, NT, 16]),
                                               in1=invf.unsqueeze(1).broadcast_to([128, NT, 16]), op=ALU.mult),
             [cst], [ang])
        for (dst, shift) in ((sinT, 0.0), (cosT, PI / 2)):
            T.op("dve", lambda e: e.tensor_scalar(out=tmpa[:, :, :], in0=ang[:, :, :], scalar1=shift,
                                                   scalar2=1.0 / (2 * PI), op0=ALU.add, op1=ALU.mult), [ang], [tmpa])
            T.op("dve", lambda e: e.tensor_copy(out=ki[:, :, :], in_=tmpa[:, :, :]), [tmpa], [ki])
            T.op("dve", lambda e: e.tensor_copy(out=kf[:, :, :], in_=ki[:, :, :]), [ki], [kf])
            T.op("dve", lambda e: e.scalar_tensor_tensor(out=tmpa[:, :, :], in0=kf[:, :, :], scalar=-2 * PI,
                                                          in1=ang[:, :, :], op0=ALU.mult, op1=ALU.add), [kf, ang], [tmpa])
            T.op("dve", lambda e: e.tensor_scalar(out=tmpa[:, :, :], in0=tmpa[:, :, :], scalar1=shift,
                                                   scalar2=3.1415925, op0=ALU.add, op1=ALU.min), [tmpa], [tmpa])
            T.op("dve", lambda e: e.tensor_scalar(out=tmpa[:, :, :], in0=tmpa[:, :, :], scalar1=-3.1415925,
                                                   scalar2=None, op0=ALU.max), [tmpa], [tmpa])
            T.op("act", lambda e: e.activation(out=dst[:, :, :], in_=tmpa[:, :, :], func=AF.Sin), [tmpa], [dst])

        import os
        ksub = int(os.environ.get("KSUB", "99"))
        if ksub == 0:
            phase_end(); es.close(); return
        xt = [sb(es, f"a0_x{i}", [128, D], F32) for i in range(2)]
        xT = sb(es, "a0_xT", [128, 8, 128], BF16)
        xTr = sb(es, "a0_xTr", [128, 8, 128], BF16)
        junk = sb(es, "a0_junk", [128, 256], F32)
        st = sb(es, "a0_st", [128, 16], F32)
        cs = sb(es, "a0_cs", [128, 512], F32)
        cT = sb(es, "a0_cT", [128, 4, 128], BF16)
        qa = sb(es, "a0_qa", [128, 8, 96], BF16)
        ka = sb(es, "a0_ka", [128, 8, 96], BF16)
        va = sb(es, "a0_va", [128, 8, 64], BF16)
        rt = [sb(es, f"a0_rt{i}", [128, 8, 16], F32) for i in range(4)]
        kr = sb(es, "a0_kr", [128, 32], F32)
        krs = sb(es, "a0_krs", [128, 32], F32)
        qr = sb(es, "a0_qr", [128, 8, 32], F32)
        cos8 = sb(es, "a0_cos8", [128, NT, 8, 16], F32)
        sin8 = sb(es, "a0_sin8", [128, NT, 8, 16], F32)
        for h in range(8):
            T.op("dve", lambda e: e.tensor_copy(out=cos8[:, :, h, :], in_=cosT[:, :, :]), [cosT], [cos8])
            T.op("dve", lambda e: e.tensor_copy(out=sin8[:, :, h, :], in_=sinT[:, :, :]), [sinT], [sin8])
        krt = [sb(es, f"a0_krt{i}", [128, 16], F32) for i in range(4)]
        stQ = sb(es, "a0_stQ", [128, 8, 128], BF16)
        stK = sb(es, "a0_stK", [128, 8, 128], BF16)
        stQs = sb(es, "a0_stQs", [128, 4, 128], BF16)
        stKs = sb(es, "a0_stKs", [128, 4, 128], BF16)
        stVs = sb(es, "a0_stVs", [128, 512], BF16)
        pT = ps(es, "a0_pT", [128, 512], F32)
        pC = ps(es, "a0_pC", [128, 512], F32)
        pC2 = ps(es, "a0_pC2", [128, 512], F32)
        pQ = ps(es, "a0_pQ", [128, 1024], F32)
        pKV = ps(es, "a0_pKV", [128, 1024], F32)
        pB = ps(es, "a0_pB", [128, 1024], BF16)

        def stA(tt):
            x = xt[tt % 2]
            rtt = NT - 1 - tt
            tsl = slice(tt * 128, (tt + 1) * 128)
            rsl = slice(rtt * 128, (rtt + 1) * 128)
            transpose_f32_tile(x, xT, pT, 8)
            for h0 in (0, 4):
                for i in range(4):
                    T.op("pe", lambda e, i=i: e.matmul(out=pT[:, i * 128:(i + 1) * 128],
                                                        lhsT=x[:, (h0 + i) * 128:(h0 + i + 1) * 128], rhs=J_f,
                                                        start=True, stop=True), [x, cst], [pT], inc=(i == 3))
                T.op("act", lambda e: e.activation(out=xTr[:, h0:h0 + 4, :],
                                                   in_=pT[:, :].rearrange("p (c t) -> p c t", t=128), func=AF.Copy),
                     [pT], [xTr])
            for kc in range(8):
                T.op("pe", lambda e: e.matmul(out=pC[:, 0:512], lhsT=xT[:, kc, :], rhs=Win[:, kc, 0:512],
                                              start=(kc == 0), stop=(kc == 7)), [xT, Win], [pC], inc=(kc == 7))
            for kc in range(8):
                T.op("pe", lambda e: e.matmul(out=pC2[:, 0:32], lhsT=xT[:, kc, :], rhs=Win[:, kc, 512:544],
                                              start=(kc == 0), stop=(kc == 7)), [xT, Win], [pC2], inc=(kc == 7))
            for hp in range(4):
                for kc in range(8):
                    T.op("pe", lambda e: e.matmul(out=pKV[:, hp * 128:(hp + 1) * 128],
                                                  lhsT=Win[:, kc, 544 + hp * 128:544 + (hp + 1) * 128],
                                                  rhs=xT[:, kc, :], start=(kc == 0), stop=(kc == 7)),
                         [xT, Win], [pKV], inc=False)
            for hp in range(4):
                for kc in range(8):
                    T.op("pe", lambda e: e.matmul(out=pKV[:, 512 + hp * 128:512 + (hp + 1) * 128],
                                                  lhsT=Win[:, kc, 1056 + hp * 128:1056 + (hp + 1) * 128],
                                                  rhs=xTr[:, kc, :], start=(kc == 0), stop=(kc == 7)),
                         [xT, xTr, Win], [pKV], inc=(kc == 7 and hp == 3))
            for kc in range(8):
                T.op("pe", lambda e: e.matmul(out=pQ[:, 0:512], lhsT=xTr[:, kc, :], rhs=Win[:, kc, 1568:2080],
                                              start=(kc == 0), stop=(kc == 7)), [xTr, Win], [pQ], inc=(kc == 7))

        def stB(tt):
            x = xt[tt % 2]
            rtt = NT - 1 - tt
            tsl = slice(tt * 128, (tt + 1) * 128)
            rsl = slice(rtt * 128, (rtt + 1) * 128)
            T.op("act", lambda e: e.activation(out=junk[:, :], in_=pC[:, 0:256], func=AF.Square,
                                               accum_out=st[:, 0:1]), [pC], [junk, st])
            T.op("act", lambda e: e.activation(out=junk[:, :], in_=pC[:, 256:512], func=AF.Square,
                                               accum_out=st[:, 1:2]), [pC], [junk, st])
            T.op("dve", lambda e: e.tensor_scalar(out=st[:, 2:4], in0=st[:, 0:2], scalar1=1.0 / 256, scalar2=1e-6,
                                                   op0=ALU.mult, op1=ALU.add), [st], [st])
            T.op("act", lambda e: e.activation(out=st[:, 4:6], in_=st[:, 2:4], func=AF.Sqrt), [st], [st])
            T.op("dve", lambda e: e.reciprocal(out=st[:, 6:8], in_=st[:, 4:6]), [st], [st])
            T.op("dve", lambda e: e.tensor_scalar(out=cs[:, 0:256], in0=pC[:, 0:256], scalar1=st[:, 6:7],
                                                   scalar2=None, op0=ALU.mult), [pC, st], [cs])
            T.op("dve", lambda e: e.tensor_scalar(out=cs[:, 256:512], in0=pC[:, 256:512], scalar1=st[:, 7:8],
                                                   scalar2=None, op0=ALU.mult), [pC, st], [cs])
            for i in range(4):
                T.op("pe", lambda e: e.transpose(out=pT[:, i * 128:(i + 1) * 128], in_=cs[:, i * 128:(i + 1) * 128],
                                                 identity=ident_f), [cs, cst], [pT], inc=(i == 3))
            for i in range(4):
                T.op("dve", lambda e: e.tensor_scalar(out=cT[:, i, :], in0=pT[:, i * 128:(i + 1) * 128],
                                                       scalar1=g4[:, i:i + 1], scalar2=None, op0=ALU.mult),
                     [pT, g4], [cT])
            T.op("dve", lambda e: e.tensor_copy(out=stQs[:, :, :],
                                                in_=pKV[:, 0:512].rearrange("p (c t) -> p c t", t=128)), [pKV], [stQs])
            T.dma("sp", [(QTs.rearrange("(c p) t -> p c t", p=128)[:, :, tsl], stQs[:, :, :])], reads=[stQs])
            T.op("act", lambda e: e.activation(out=stKs[:, :, :],
                                               in_=pKV[:, 512:1024].rearrange("p (c t) -> p c t", t=128),
                                               func=AF.Copy), [pKV], [stKs])
            T.dma("sp", [(KTs.rearrange("(c p) t -> p c t", p=128)[:, :, rsl], stKs[:, :, :])], reads=[stKs])
            T.op("dve", lambda e: e.tensor_copy(out=stVs[:, :], in_=pQ[:, 0:512]), [pQ], [stVs])
            T.dma("sp", [(Vs[rsl, :, :], stVs[:, :].rearrange("p (h d) -> p h d", d=64))], reads=[stVs])
            for bq in range(2):
                for kc in range(2):
                    T.op("pe", lambda e: e.matmul(out=pQ[:, bq * 512:bq * 512 + 384], lhsT=cT[:, kc, :],
                                                  rhs=Wuq[:, kc, bq * 384:(bq + 1) * 384],
                                                  start=(kc == 0), stop=(kc == 1)), [cT, Wuq], [pQ],
                         inc=(kc == 1 and bq == 1))
            for (n0, n1) in ((0, 512), (512, 1024)):
                for kc in range(2):
                    T.op("pe", lambda e: e.matmul(out=pKV[:, n0:n1], lhsT=cT[:, 2 + kc, :], rhs=Wukv[:, kc, n0:n1],
                                                  start=(kc == 0), stop=(kc == 1)), [cT, Wukv], [pKV],
                         inc=(kc == 1 and n0 == 512))
            for bq in range(2):
                hs = slice(4 * bq, 4 * bq + 4)
                q3 = pQ[:, bq * 512:bq * 512 + 384].rearrange("p (h d) -> p h d", d=96)
                kv3 = pKV[:, bq * 512:(bq + 1) * 512].rearrange("p (h d) -> p h d", d=128)
                T.op("act", lambda e: e.activation(out=qa[:, hs, 0:64], in_=q3[:, :, 0:64], func=AF.Copy), [pQ], [qa])
                T.op("act", lambda e: e.activation(out=qr[:, hs, :], in_=q3[:, :, 64:96], func=AF.Copy), [pQ], [qr])
                T.op("act", lambda e: e.activation(out=ka[:, hs, 0:64], in_=kv3[:, :, 0:64], func=AF.Copy), [pKV], [ka])
                T.op("act", lambda e: e.activation(out=va[:, hs, :], in_=kv3[:, :, 64:128], func=AF.Copy), [pKV], [va])
            T.op("act", lambda e: e.activation(out=krs[:, :], in_=pC2[:, 0:32], func=AF.Copy), [pC2], [krs])
            cosb = cos8[:, tt, :, :]
            sinb = sin8[:, tt, :, :]
            T.op("dve", lambda e: e.tensor_tensor(out=rt[0][:, :, :], in0=qr[:, :, 0:16], in1=cosb, op=ALU.mult),
                 [qr, cos8], [rt[0]])
            T.op("dve", lambda e: e.tensor_tensor(out=rt[1][:, :, :], in0=qr[:, :, 16:32], in1=sinb, op=ALU.mult),
                 [qr, sin8], [rt[1]])
            T.op("dve", lambda e: e.tensor_tensor(out=rt[2][:, :, :], in0=qr[:, :, 0:16], in1=sinb, op=ALU.mult),
                 [qr, sin8], [rt[2]])
            T.op("dve", lambda e: e.tensor_tensor(out=rt[3][:, :, :], in0=qr[:, :, 16:32], in1=cosb, op=ALU.mult),
                 [qr, cos8], [rt[3]])
            T.op("pool", lambda e: e.tensor_tensor(out=qa[:, :, 64:80], in0=rt[0][:, :, :], in1=rt[1][:, :, :],
                                                   op=ALU.subtract), [rt[0], rt[1]], [qa])
            T.op("pool", lambda e: e.tensor_tensor(out=qa[:, :, 80:96], in0=rt[2][:, :, :], in1=rt[3][:, :, :],
                                                   op=ALU.add), [rt[2], rt[3]], [qa])
            T.op("dve", lambda e: e.tensor_tensor(out=krt[0][:, :], in0=krs[:, 0:16], in1=cosT[:, tt, :], op=ALU.mult),
                 [krs, cosT], [krt[0]])
            T.op("dve", lambda e: e.tensor_tensor(out=krt[1][:, :], in0=krs[:, 16:32], in1=sinT[:, tt, :], op=ALU.mult),
                 [krs, sinT], [krt[1]])
            T.op("dve", lambda e: e.tensor_tensor(out=krt[2][:, :], in0=krs[:, 0:16], in1=sinT[:, tt, :], op=ALU.mult),
                 [krs, sinT], [krt[2]])
            T.op("dve", lambda e: e.tensor_tensor(out=krt[3][:, :], in0=krs[:, 16:32], in1=cosT[:, tt, :], op=ALU.mult),
                 [krs, cosT], [krt[3]])
            T.op("pool", lambda e: e.tensor_tensor(out=kr[:, 0:16], in0=krt[0][:, :], in1=krt[1][:, :],
                                                   op=ALU.subtract), [krt[0], krt[1]], [kr])
            T.op("pool", lambda e: e.tensor_tensor(out=kr[:, 16:32], in0=krt[2][:, :], in1=krt[3][:, :],
                                                   op=ALU.add), [krt[2], krt[3]], [kr])
            T.op("dve", lambda e: e.tensor_scalar(out=ka[:, :, 64:96],
                                                   in0=kr[:, :].unsqueeze(1).broadcast_to([128, 8, 32]),
                                                   scalar1=1.0, scalar2=None, op0=ALU.mult), [kr], [ka])

        def stC(tt):
            x = xt[tt % 2]
            rtt = NT - 1 - tt
            tsl = slice(tt * 128, (tt + 1) * 128)
            rsl = slice(rtt * 128, (rtt + 1) * 128)
            T.dma("sp", [(Vm[tsl, :, :], va[:, :, :])], reads=[va])
            for (src, dst, dr) in ((qa, stQ, QTm), (ka, stK, KTm)):
                for h in range(8):
                    T.op("pe", lambda e: e.transpose(out=pB[0:96, h * 128:(h + 1) * 128], in_=src[:, h, :],
                                                     identity=ident_bf[:, :]), [src, ident_bf], [pB], inc=(h == 7))
                T.op("act", lambda e: e.activation(out=dst[0:96, :, :],
                                                   in_=pB[0:96, :].rearrange("p (h t) -> p h t", t=128),
                                                   func=AF.Copy), [pB], [dst])
                T.dma("sp", [(dr.rearrange("h r t -> r h t")[:, :, tsl], dst[0:96, :, :])], reads=[dst])

        T.dma("sp", [(xt[0][:, :], x_in[0:128, :])], writes=[xt[0]])
        if NT > 1:
            T.dma("sp", [(xt[1][:, :], x_in[128:256, :])], writes=[xt[1]])
        stA(0)
        for tt in range(NT):
            stB(tt)
            if tt + 1 < NT:
                stA(tt + 1)
            if tt + 2 < NT:
                nx = xt[tt % 2]
                T.dma("sp", [(nx[:, :], x_in[(tt + 2) * 128:(tt + 3) * 128, :])], writes=[nx])
            stC(tt)
        phase_end()
        es.close()

    def phase_A1(Win=None):
        es = contextlib.ExitStack()
        if Win is None:
            Win = load_w(es, "c_win", c_w_in, 8, 3088)
        nbf = sb(es, "a1_nbf", [16, 1], F32)
        T.dma("sp", [(nbf[:, :], c_b_f.rearrange("(h o) -> h o", o=1))], writes=[nbf])
        xt = [sb(es, f"a1_x{i}", [128, D], F32) for i in range(2)]
        xTg = [sb(es, f"a1_xT{i}", [128, 8, 512], BF16) for i in range(2)]
        FL = sb(es, "a1_FL", [16, S], F32)
        G = sb(es, "a1_G", [16, S], F32)
        R1 = sb(es, "a1_R1", [16, S], F32)
        dq = [sb(es, f"a1_dq{i}", [16, S], BF16) for i in range(3)]
        dk = [sb(es, f"a1_dk{i}", [16, S], BF16) for i in range(3)]
        stQ = sb(es, "a1_stQ", [128, 8, 512], BF16)
        stK = sb(es, "a1_stK", [128, 8, 512], BF16)
        stV = [sb(es, f"a1_stV{i}", [128, 1024], BF16) for i in range(2)]
        pT = ps(es, "a1_pT", [128, 512], F32)
        pQ = [ps(es, f"a1_pQ{i}", [128, 512], F32) for i in range(2)]
        pK = [ps(es, f"a1_pK{i}", [128, 512], F32) for i in range(2)]
        pV = ps(es, "a1_pV", [128, 1024], F32)
        pF = ps(es, "a1_pF", [128, 512], F32)
        T.op("dve", lambda e: e.tensor_scalar(out=nbf[:, :], in0=nbf[:, :], scalar1=-1.0, scalar2=None, op0=ALU.mult),
             [nbf], [nbf])
        NG = NT // 4
        T.dma("sp", [(xt[0][:, :], X1[0:128, :])], writes=[xt[0]])
        for g in range(NG):
            xT = xTg[g % 2]
            gsl = slice(g * 512, (g + 1) * 512)
            for ti in range(4):
                tt = g * 4 + ti
                x = xt[tt % 2]
                if tt + 1 < NT:
                    nx = xt[(tt + 1) % 2]
                    T.dma("sp", [(nx[:, :], X1[(tt + 1) * 128:(tt + 2) * 128, :])], writes=[nx])
                for h0 in (0, 4):
                    for i in range(4):
                        T.op("pe", lambda e: e.transpose(out=pT[:, i * 128:(i + 1) * 128],
                                                         in_=x[:, (h0 + i) * 128:(h0 + i + 1) * 128],
                                                         identity=ident_f), [x, cst], [pT], inc=(i == 3))
                    if h0 == 0:
                        T.op("act", lambda e: e.activation(out=xT[:, 0:4, ti * 128:(ti + 1) * 128],
                                                           in_=pT[:, :].rearrange("p (c t) -> p c t", t=128),
                                                           func=AF.Copy), [pT], [xT])
                    else:
                        T.op("dve", lambda e: e.tensor_copy(out=xT[:, 4:8, ti * 128:(ti + 1) * 128],
                                                            in_=pT[:, :].rearrange("p (c t) -> p c t", t=128)),
                             [pT], [xT])
                sv = stV[tt % 2]
                for n0 in (0, 512):
                    for kc in range(8):
                        T.op("pe", lambda e: e.matmul(out=pV[:, n0:n0 + 512], lhsT=xT[:, kc, ti * 128:(ti + 1) * 128],
                                                      rhs=Win[:, kc, 2048 + n0:2048 + n0 + 512],
                                                      start=(kc == 0), stop=(kc == 7)), [xT, Win], [pV],
                             inc=(kc == 7 and n0 == 512))
                T.op("act", lambda e: e.activation(out=sv[:, 0:512], in_=pV[:, 0:512], func=AF.Copy), [pV], [sv])
                T.op("dve", lambda e: e.tensor_copy(out=sv[:, 512:1024], in_=pV[:, 512:1024]), [pV], [sv])
                T.dma("sp", [(Vf[tt * 128:(tt + 1) * 128, :, :], sv[:, :].rearrange("p (h d) -> p h d", d=64))],
                      reads=[sv])
            for (pp2, st_, c0, dr) in ((pQ, stQ, 0, QTf), (pK, stK, 1024, KTf)):
                for c in range(8):
                    pb = pp2[c % 2]
                    for kc in range(8):
                        T.op("pe", lambda e: e.matmul(out=pb[:, 0:512],
                                                      lhsT=Win[:, kc, c0 + c * 128:c0 + (c + 1) * 128],
                                                      rhs=xT[:, kc, :], start=(kc == 0), stop=(kc == 7)),
                             [xT, Win], [pb], inc=(kc == 7))
                    if c % 2 == 0:
                        T.op("act", lambda e: e.activation(out=st_[:, c, :], in_=pb[:, 0:512], func=AF.Copy),
                             [pb], [st_])
                    else:
                        T.op("dve", lambda e: e.tensor_copy(out=st_[:, c, :], in_=pb[:, 0:512]), [pb], [st_])
                T.dma("sp", [(dr.rearrange("(c p) t -> p c t", p=128)[:, :, gsl], st_[:, :, :])], reads=[st_])
            for kc in range(8):
                T.op("pe", lambda e: e.matmul(out=pF[0:16, 0:512], lhsT=Win[:, kc, 3072:3088], rhs=xT[:, kc, :],
                                              start=(kc == 0), stop=(kc == 7)), [xT, Win], [pF], inc=(kc == 7))
            T.op("dve", lambda e: e.tensor_copy(out=FL[:, gsl], in_=pF[0:16, 0:512]), [pF], [FL])
        T.op("act", lambda e: e.activation(out=FL[:, :], in_=FL[:, :], func=AF.Exp, scale=-1.0, bias=nbf[:, 0:1]),
             [FL, nbf], [FL])
        T.op("act", lambda e: e.activation(out=FL[:, :], in_=FL[:, :], func=AF.Ln, bias=1.0), [FL], [FL])
        for c in range(S // 512):
            csl = slice(c * 512, (c + 1) * 512)
            ini = 0.0 if c == 0 else G[:, c * 512 - 1:c * 512]
            T.op("dve", lambda e: e.tensor_tensor_scan(out=G[:, csl], data0=ones_f[0:16, 0:512],
                                                        data1=FL[:, csl], initial=ini, op0=ALU.mult, op1=ALU.add),
                 [FL, ones_f, G], [G])
        T.op("dve", lambda e: e.tensor_scalar(out=G[:, :], in0=G[:, :], scalar1=-8.0, scalar2=None, op0=ALU.mult),
             [G], [G])
        cur = G
        for i in range(3):
            T.op("dve", lambda e: e.tensor_copy(out=dq[i][:, :], in_=cur[:, :]), [cur], [dq[i]])
            T.op("dve", lambda e: e.tensor_scalar(out=dk[i][:, :], in0=dq[i][:, :], scalar1=-1.0, scalar2=None,
                                                   op0=ALU.mult), [dq[i]], [dk[i]])
            if i < 2:
                nxt = R1
                T.op("dve", lambda e: e.tensor_tensor(out=nxt[:, :], in0=cur[:, :], in1=dq[i][:, :], op=ALU.subtract),
                     [cur, dq[i]], [nxt])
                cur = nxt
            T.dma("sp", [(QDf[i], dq[i][:, :])], reads=[dq[i]])
            T.dma("sp", [(KDf[i], dk[i][:, :])], reads=[dk[i]])
        phase_end()
        es.close()

    def softmax_heads(es, Oall, heads, kd, scale, mask_bf, loader, QTa, KTa, Va, PT, pS, pO):
        NQB = S // 512
        first = True
        for hi, (h, ocol) in enumerate(heads):
            sl = hi % 2
            if first:
                loader(h, QTa[sl], KTa[sl], Va[sl])
                first = False
            if hi + 1 < len(heads):
                loader(heads[hi + 1][0], QTa[1 - sl], KTa[1 - sl], Va[1 - sl])
            Q, K, V = QTa[sl], KTa[sl], Va[sl]
            tiles = []
            for j in range(NQB):
                for kt in range(4 * j + 4):
                    tiles.append((j, kt))
            rn = sb_small["rn"]

            NS = len(pS)

            def qk(idx):
                j, kt = tiles[idx]
                i = max(0, kt - 4 * j)
                c0 = 128 * i
                pb = pS[idx % NS]
                T.op("pe", lambda e: e.matmul(out=pb[:, c0:512], lhsT=K[0:kd, kt * 128:(kt + 1) * 128],
                                              rhs=Q[0:kd, j * 512 + c0:(j + 1) * 512], start=True, stop=True),
                     [K, Q], [pb])

            for i0 in range(min(NS - 1, len(tiles))):
                qk(i0)
            for idx, (j, kt) in enumerate(tiles):
                if idx + NS - 1 < len(tiles):
                    qk(idx + NS - 1)
                i = max(0, kt - 4 * j)
                c0 = 128 * i
                pb = pS[idx % NS]
                pt = PT[idx % 3]
                T.op("act", lambda e: e.activation(out=pt[:, c0:512], in_=pb[:, c0:512], func=AF.Exp, scale=scale),
                     [pb], [pt])
                if kt >= 4 * j:
                    T.op("dve", lambda e: e.tensor_tensor(out=pt[:, c0:c0 + 128], in0=pt[:, c0:c0 + 128],
                                                           in1=mask_bf[:, :], op=ALU.mult), [pt, mask_bf], [pt])
                for sub in range(i, 4):
                    last = (kt == 4 * j + sub)
                    T.op("pe", lambda e: e.matmul(out=pO[sub][:, 0:65], lhsT=pt[:, sub * 128:(sub + 1) * 128],
                                                  rhs=V[:, kt, :], start=(kt == 0), stop=last),
                         [pt, V], [pO[sub]], inc=(last or sub == 3))
                    if last:
                        T.op("dve", lambda e: e.reciprocal(out=rn[:, sub:sub + 1], in_=pO[sub][:, 64:65]),
                             [pO[sub]], [rn])
                        T.op("dve", lambda e: e.tensor_scalar(out=Oall[:, j * 4 + sub, ocol:ocol + 64],
                                                               in0=pO[sub][:, 0:64], scalar1=rn[:, sub:sub + 1],
                                                               scalar2=None, op0=ALU.mult), [pO[sub], rn], [Oall])

    sb_small = {}

    def sb_heads(es, Oall, QTa, KTa, Va, wk, pS, pO, pTb):
        E_, L_, C_, X_, att, attT = wk
        for h in range(8):
            sl = h % 2

            def loader(hh, s_):
                T.dma("sp", [(QTa[s_][0:64, :], QTs[hh * 64:(hh + 1) * 64, :]),
                             (KTa[s_][0:64, :], KTs[hh * 64:(hh + 1) * 64, :])], writes=[QTa[s_], KTa[s_]])
                T.dma("sp", [(Va[s_][:, :, 0:64], Vs.rearrange("(t p) h d -> p t h d", p=128)[:, :, hh, :])],
                      writes=[Va[s_]])
            if h == 0:
                loader(0, 0)
            if h + 1 < 8:
                loader(h + 1, 1 - sl)
            Q, K, V = QTa[sl], KTa[sl], Va[sl]
            chunks = []
            for qt in range(NT):
                r0 = NT - 1 - qt
                n = qt + 1
                nchunk = (n + 3) // 4
                for c in range(nchunk):
                    cnt = min(4, n - 4 * c)
                    chunks.append((qt, c, r0 + 4 * c, cnt, c == nchunk - 1))
            NCH = len(chunks)

            def qk(idx):
                qt, c, t0, cnt, lastc = chunks[idx]
                w = 128 * cnt
                pb = pS[idx % 3]
                T.op("pe", lambda e: e.matmul(out=pb[:, 0:w], lhsT=Q[0:64, qt * 128:(qt + 1) * 128],
                                              rhs=K[0:64, t0 * 128:t0 * 128 + w], start=True, stop=True),
                     [Q, K], [pb])

            def f_E(idx):
                qt, c, t0, cnt, lastc = chunks[idx]
                w = 128 * cnt
                T.op("act", lambda e: e.activation(out=E_[idx % 4][:, 0:w], in_=pS[idx % 3][:, 0:w], func=AF.Exp,
                                                   scale=0.125), [pS[idx % 3]], [E_[idx % 4]])

            def f_L(idx):
                qt, c, t0, cnt, lastc = chunks[idx]
                w = 128 * cnt
                Lb = L_[idx % 2]
                T.op("act", lambda e: e.activation(out=Lb[:, 0:w], in_=E_[idx % 4][:, 0:w], func=AF.Ln, bias=1.0),
                     [E_[idx % 4]], [Lb])
                if c == 0:
                    T.op("pool", lambda e: e.tensor_tensor(out=Lb[:, 0:128], in0=Lb[:, 0:128], in1=mS_f,
                                                           op=ALU.mult), [Lb, cst], [Lb])

            def f_scan(idx):
                qt, c, t0, cnt, lastc = chunks[idx]
                w = 128 * cnt
                Lb, Cb = L_[idx % 2], C_[idx % 3]
                if c == 0:
                    T.op("dve", lambda e: e.tensor_tensor_scan(out=Cb[:, 0:w], data0=ones_f[:, 0:w],
                                                                data1=Lb[:, 0:w], initial=0.0,
                                                                op0=ALU.mult, op1=ALU.add), [Lb, ones_f], [Cb])
                else:
                    Cp = C_[(idx - 1) % 3]
                    T.op("dve", lambda e: e.tensor_tensor_scan(out=Cb[:, 0:w], data0=ones_f[:, 0:w],
                                                                data1=Lb[:, 0:w], initial=Cp[:, 511:512],
                                                                op0=ALU.mult, op1=ALU.add), [Lb, ones_f, Cp], [Cb])

            def b_X(idx):
                qt, c, t0, cnt, lastc = chunks[idx]
                w = 128 * cnt
                T.op("act", lambda e: e.activation(out=X_[idx % 2][:, 0:w], in_=C_[idx % 3][:, 0:w], func=AF.Exp,
                                                   scale=-1.0), [C_[idx % 3]], [X_[idx % 2]])

            def b_att(idx):
                qt, c, t0, cnt, lastc = chunks[idx]
                w = 128 * cnt
                ab = att[idx % 2]
                T.op("dve", lambda e: e.tensor_tensor(out=ab[:, 0:w], in0=E_[idx % 4][:, 0:w], in1=X_[idx % 2][:, 0:w],
                                                      op=ALU.mult), [E_[idx % 4], X_[idx % 2]], [ab])
                if c == 0:
                    T.op("pool", lambda e: e.tensor_tensor(out=ab[:, 0:128], in0=ab[:, 0:128], in1=mS_bf[:, :],
                                                           op=ALU.mult), [ab, mS_bf], [ab])

            pTv = [pTb[:, 0:512], pO[3][:, :].bitcast(BF16)[:, 0:512]]
            pTb2 = [pTb, pO[3]]

            def b_T(idx):
                qt, c, t0, cnt, lastc = chunks[idx]
                ab = att[idx % 2]
                ptb = pTb2[idx % 2]
                pv = pTv[idx % 2]
                for m in range(cnt):
                    T.op("pe", lambda e: e.transpose(out=pv[:, m * 128:(m + 1) * 128],
                                                     in_=ab[:, m * 128:(m + 1) * 128], identity=ident_bf[:, :]),
                         [ab, ident_bf], [ptb], inc=(m == cnt - 1))

            def b_evac(idx):
                qt, c, t0, cnt, lastc = chunks[idx]
                w = 128 * cnt
                pv = pTv[idx % 2]
                if idx % 2 == 0:
                    T.op("act", lambda e: e.activation(out=attT[idx % 2][:, 0:w], in_=pv[:, 0:w], func=AF.Copy),
                         [pTb2[idx % 2]], [attT[idx % 2]])
                else:
                    T.op("dve", lambda e: e.tensor_copy(out=attT[idx % 2][:, 0:w], in_=pv[:, 0:w]),
                         [pTb2[idx % 2]], [attT[idx % 2]])

            def b_PV(idx):
                qt, c, t0, cnt, lastc = chunks[idx]
                aT = attT[idx % 2]
                po = pO[qt % 2]
                for m in range(cnt):
                    lastm = lastc and (m == cnt - 1)
                    T.op("pe", lambda e: e.matmul(out=po[:, 0:64], lhsT=aT[:, m * 128:(m + 1) * 128],
                                                  rhs=V[:, t0 + m, 0:64], start=(c == 0 and m == 0), stop=lastm),
                         [aT, V], [po], inc=(m == cnt - 1))
                if lastc:
                    T.op("act", lambda e: e.activation(out=Oall[:, qt, 512 + h * 64:512 + (h + 1) * 64],
                                                       in_=po[:, 0:64], func=AF.Copy), [po], [Oall])

            for i in range(min(3, NCH)):
                qk(i)
            for i in range(min(2, NCH)):
                f_E(i)
                f_L(i)
                f_scan(i)
            for idx in range(NCH + 2):
                if idx + 3 < NCH:
                    qk(idx + 3)
                if idx + 2 < NCH:
                    f_E(idx + 2)
                if idx < NCH:
                    b_X(idx)
                if idx + 2 < NCH:
                    f_L(idx + 2)
                if idx < NCH:
                    b_att(idx)
                if idx + 2 < NCH:
                    f_scan(idx + 2)
                if SBLAG == 2:
                    if 2 <= idx:
                        b_PV(idx - 2)
                    if idx < NCH:
                        b_T(idx)
                    if 1 <= idx <= NCH:
                        b_evac(idx - 1)
                else:
                    if 1 <= idx <= NCH:
                        b_PV(idx - 1)
                    if idx < NCH:
                        b_T(idx)
                        b_evac(idx)

    def phase_B0(es_outer, hook=None):
        Oall = sb(es_outer, "Oall", [128, NT, D], BF16)
        GBUFS.append(Oall)
        if hook is not None:
            hook(es_outer)
        es = contextlib.ExitStack()
        QTa = [sb(es, f"b_QTa{i}", [128, S], BF16) for i in range(2)]
        KTa = [sb(es, f"b_KTa{i}", [128, S], BF16) for i in range(2)]
        Va = [sb(es, f"b_Va{i}", [128, NT, 65], BF16) for i in range(2)]
        PT = [sb(es, f"b_PT{i}", [128, 512], BF16) for i in range(3)]
        sb_small["rn"] = sb(es, "b_rn", [128, 4], F32)
        E_ = [sb(es, f"b_E{i}", [128, 512], F32) for i in range(4)]
        L_ = [sb(es, f"b_L{i}", [128, 512], F32) for i in range(2)]
        C_ = [sb(es, f"b_C{i}", [128, 512], F32) for i in range(3)]
        X_ = [sb(es, f"b_X{i}", [128, 512], F32) for i in range(2)]
        att = [sb(es, f"b_att{i}", [128, 512], BF16) for i in range(2)]
        attT = [sb(es, f"b_attT{i}", [128, 512], BF16) for i in range(2)]
        pS = [ps(es, f"b_pS{i}", [128, 512], F32) for i in range(3)]
        pO = [ps(es, f"b_pO{i}", [128, 512], F32) for i in range(4)]
        pTb = ps(es, "b_pTb", [128, 1024], BF16)
        for i in range(2):
            T.op("pool", lambda e: e.memset(Va[i][:, :, 64:65], 1.0), [], [Va[i]])

        def loader(h, Q, K, V):
            T.dma("sp", [(Q[0:96, :], QTm[h]), (K[0:96, :], KTm[h])], writes=[Q, K])
            T.dma("sp", [(V[:, :, 0:64], Vm.rearrange("(t p) h d -> p t h d", p=128)[:, :, h, :])], writes=[V])

        softmax_heads(es, Oall, [(h, h * 64) for h in range(8)], 96, 96.0 ** -0.5, mM_bf, loader,
                      QTa, KTa, Va, PT, pS, pO)
        sb_heads(es, Oall, QTa, KTa, Va, (E_, L_, C_, X_, att, attT), pS, pO, pTb)
        phase_end()
        es.close()
        return Oall

    def phase_B1(es_outer, hook=None):
        Oall = sb(es_outer, "Oall1", [128, NT, D], BF16)
        GBUFS.append(Oall)
        if hook is not None:
            hook(es_outer)
        es = contextlib.ExitStack()
        QTa = [sb(es, f"b1_QTa{i}", [128, S], BF16) for i in range(2)]
        KTa = [sb(es, f"b1_KTa{i}", [128, S], BF16) for i in range(2)]
        Va = [sb(es, f"b1_Va{i}", [128, NT, 65], BF16) for i in range(2)]
        PT = [sb(es, f"b1_PT{i}", [128, 512], BF16) for i in range(3)]
        sb_small["rn"] = sb(es, "b1_rn", [128, 4], F32)
        pS = [ps(es, f"b1_pS{i}", [128, 512], F32) for i in range(4)]
        pO = [ps(es, f"b1_pO{i}", [128, 512], F32) for i in range(4)]
        for i in range(2):
            T.op("pool", lambda e: e.memset(Va[i][:, :, 64:65], 1.0), [], [Va[i]])
            T.op("pool", lambda e: e.memset(QTa[i][64:70, :], 1.0), [], [QTa[i]])
            T.op("pool", lambda e: e.memset(KTa[i][64:70, :], 1.0), [], [KTa[i]])

        def loader(h, Q, K, V):
            T.dma("sp", [(Q[0:64, :], QTf[h * 64:(h + 1) * 64, :]), (Q[64:67, :], QDf[:, h, :]),
                         (K[0:64, :], KTf[h * 64:(h + 1) * 64, :]), (K[67:70, :], KDf[:, h, :])], writes=[Q, K])
            T.dma("sp", [(V[:, :, 0:64], Vf.rearrange("(t p) h d -> p t h d", p=128)[:, :, h, :])], writes=[V])

        softmax_heads(es, Oall, [(h, h * 64) for h in range(16)], 70, 0.125, mC_bf, loader,
                      QTa, KTa, Va, PT, pS, pO)
        phase_end()
        es.close()
        return Oall

    def preload_C1(es, li, w_out):
        Wout = load_w(es, "c1_wout", w_out, 8, D)
        g_bc = load_bc(es, "c1_g", ln1_g[li])
        b_bc = load_bc(es, "c1_b", ln1_b[li])
        GBUFS.extend([Wout, g_bc, b_bc])
        return [Wout, g_bc, b_bc]

    def phase_C1(li, Oall, pre, xsrc):
        es = contextlib.ExitStack()
        Wout, g_bc, b_bc = pre
        xt = [sb(es, f"c1_x{i}", [128, D], F32) for i in range(2)]
        rr = [sb(es, f"c1_r{i}", [128, D], F32) for i in range(2)]
        OTs = [sb(es, f"c1_OT{i}", [128, 8, 128], BF16) for i in range(2)]
        st = sb(es, "c1_st", [128, 32], F32)
        pB = ps(es, "c1_pB", [128, 1024], BF16)
        pMs = [ps(es, f"c1_pM{i}", [128, 1024], F32) for i in range(2)]

        def front(tt):
            OT, pM = OTs[tt % 2], pMs[tt % 2]
            for c in range(8):
                T.op("pe", lambda e: e.transpose(out=pB[:, c * 128:(c + 1) * 128], in_=Oall[:, tt, c * 128:(c + 1) * 128],
                                                 identity=ident_bf[:, :]), [Oall, ident_bf], [pB], inc=(c == 7))
            T.op("act", lambda e: e.activation(out=OT[:, :, :], in_=pB[:, :].rearrange("p (c t) -> p c t", t=128),
                                               func=AF.Copy), [pB], [OT])
            for n0 in (0, 512):
                for kc in range(8):
                    T.op("pe", lambda e: e.matmul(out=pM[:, n0:n0 + 512], lhsT=OT[:, kc, :], rhs=Wout[:, kc, n0:n0 + 512],
                                                  start=(kc == 0), stop=(kc == 7)), [OT, Wout], [pM],
                         inc=(kc == 7 and n0 == 512))

        def back(tt):
            x, r, pM = xt[tt % 2], rr[tt % 2], pMs[tt % 2]
            for n0 in (0, 512):
                T.op("dve", lambda e: e.scalar_tensor_tensor(out=r[:, n0:n0 + 512], in0=x[:, n0:n0 + 512], scalar=ALPHA,
                                                              in1=pM[:, n0:n0 + 512], op0=ALU.mult, op1=ALU.add),
                     [x, pM], [r])
            layer_norm(r, g_bc, b_bc, r, st)
            T.dma("sp", [(Y1[tt * 128:(tt + 1) * 128, :], r[:, :])], reads=[r])

        T.dma("sp", [(xt[0][:, :], xsrc[0:128, :])], writes=[xt[0]])
        front(0)
        for tt in range(NT):
            if tt + 1 < NT:
                nx = xt[(tt + 1) % 2]
                T.dma("sp", [(nx[:, :], xsrc[(tt + 1) * 128:(tt + 2) * 128, :])], writes=[nx])
                front(tt + 1)
            back(tt)
        phase_end()
        es.close()

    def phase_C2(li, W13):
        es = contextlib.ExitStack()
        W1, W3 = W13
        W2 = load_w(es, "c2_w2", ffn_w2[li], NFC, D)
        g_bc = load_bc(es, "c2_g", ln2_g[li])
        b_bc = load_bc(es, "c2_b", ln2_b[li])
        yt = [sb(es, f"c2_y{i}", [128, D], F32) for i in range(2)]
        rr = [sb(es, f"c2_r{i}", [128, D], F32) for i in range(2)]
        yTs = [sb(es, f"c2_yT{i}", [128, 8, 128], BF16) for i in range(2)]
        sl_ = [sb(es, f"c2_s{i}", [128, 512], F32) for i in range(2)]
        ggs = [sb(es, f"c2_g2{i}", [128, DFF], BF16) for i in range(2)]
        gT = sb(es, "c2_gT", [128, NFC, 128], BF16)
        st = sb(es, "c2_st", [128, 32], F32)
        pT = ps(es, "c2_pT", [128, 512], F32)
        pH1 = [ps(es, f"c2_pH1{i}", [128, 512], F32) for i in range(2)]
        pH3 = [ps(es, f"c2_pH3{i}", [128, 512], F32) for i in range(2)]
        pB = ps(es, "c2_pB", [128, 1024], BF16)
        pO = ps(es, "c2_pO", [128, 1024], F32)
        blocks = [(n0, min(512, DFF - n0)) for n0 in range(0, DFF, 512)]

        def emit_H(tt):
            yT, gg = yTs[tt % 2], ggs[tt % 2]
            for bi, (n0, w) in enumerate(blocks):
                p1, p3, s_ = pH1[bi % 2], pH3[bi % 2], sl_[bi % 2]
                for kc in range(8):
                    T.op("pe", lambda e: e.matmul(out=p1[:, 0:w], lhsT=yT[:, kc, :], rhs=W1[:, kc, n0:n0 + w],
                                                  start=(kc == 0), stop=(kc == 7)), [yT, W1], [p1], inc=(kc == 7))
                for kc in range(8):
                    T.op("pe", lambda e: e.matmul(out=p3[:, 0:w], lhsT=yT[:, kc, :], rhs=W3[:, kc, n0:n0 + w],
                                                  start=(kc == 0), stop=(kc == 7)), [yT, W3], [p3], inc=(kc == 7))
                T.op("act", lambda e: e.activation(out=s_[:, 0:w], in_=p1[:, 0:w], func=AF.Silu), [p1], [s_])
                T.op("dve", lambda e: e.tensor_tensor(out=gg[:, n0:n0 + w], in0=s_[:, 0:w], in1=p3[:, 0:w],
                                                       op=ALU.mult), [s_, p3], [gg])

        def emit_tail(tt):
            y, r, gg = yt[tt % 2], rr[tt % 2], ggs[tt % 2]
            for f0 in range(0, NFC, 8):
                n = min(8, NFC - f0)
                for i in range(n):
                    T.op("pe", lambda e: e.transpose(out=pB[:, i * 128:(i + 1) * 128],
                                                     in_=gg[:, (f0 + i) * 128:(f0 + i + 1) * 128],
                                                     identity=ident_bf[:, :]), [gg, ident_bf], [pB], inc=(i == n - 1))
                T.op("act", lambda e: e.activation(out=gT[:, f0:f0 + n, :],
                                                   in_=pB[:, 0:n * 128].rearrange("p (c t) -> p c t", t=128),
                                                   func=AF.Copy), [pB], [gT])
            for n0 in (0, 512):
                for fc in range(NFC):
                    T.op("pe", lambda e: e.matmul(out=pO[:, n0:n0 + 512], lhsT=gT[:, fc, :], rhs=W2[:, fc, n0:n0 + 512],
                                                  start=(fc == 0), stop=(fc == NFC - 1)), [gT, W2], [pO],
                         inc=(fc == NFC - 1 and n0 == 512))
            for n0 in (0, 512):
                T.op("dve", lambda e: e.scalar_tensor_tensor(out=r[:, n0:n0 + 512], in0=y[:, n0:n0 + 512], scalar=ALPHA,
                                                              in1=pO[:, n0:n0 + 512], op0=ALU.mult, op1=ALU.add),
                     [y, pO], [r])
            layer_norm(r, g_bc, b_bc, r, st)
            T.dma("sp", [(Y2[tt * 128:(tt + 1) * 128, :], r[:, :])], reads=[r])

        T.dma("sp", [(yt[0][:, :], Y1[0:128, :])], writes=[yt[0]])
        transpose_f32_tile(yt[0], yTs[0], pT, 8)
        for tt in range(NT):
            if tt + 1 < NT:
                ny = yt[(tt + 1) % 2]
                T.dma("sp", [(ny[:, :], Y1[(tt + 1) * 128:(tt + 2) * 128, :])], writes=[ny])
            emit_H(tt)
            if tt + 1 < NT:
                transpose_f32_tile(yt[(tt + 1) % 2], yTs[(tt + 1) % 2], pT, 8)
            emit_tail(tt)
        phase_end()
        es.close()

    def phase_C3(li, dst):
        es = contextlib.ExitStack()
        Wg = load_w(es, "c3_wg", ple_w_gate[li], 8, D)
        Wp = load_w(es, "c3_wp", ple_w_proj[li], 2, D)
        bg_bc = load_bc(es, "c3_bg", ple_b_gate[li])
        yt = [sb(es, f"c3_y{i}", [128, D], F32) for i in range(3)]
        pt_ = [sb(es, f"c3_p{i}", [128, 256], F32) for i in range(3)]
        oo = [sb(es, f"c3_o{i}", [128, D], F32) for i in range(2)]
        tgs = [sb(es, f"c3_tg{i}", [128, D], F32) for i in range(2)]
        yTs = [sb(es, f"c3_yT{i}", [128, 8, 128], BF16) for i in range(2)]
        pTs = [sb(es, f"c3_pT{i}", [128, 2, 128], BF16) for i in range(2)]
        pT = ps(es, "c3_pTp", [128, 512], F32)
        pG = ps(es, "c3_pG", [128, 1024], F32)
        pP = ps(es, "c3_pP", [128, 1024], F32)

        def loads(tt):
            T.dma("sp", [(yt[tt % 3][:, :], Y2[tt * 128:(tt + 1) * 128, :])], writes=[yt[tt % 3]])
            T.dma("sp", [(pt_[tt % 3][:, :], p_in[li, tt * 128:(tt + 1) * 128, :])], writes=[pt_[tt % 3]])

        def frontA(tt):
            transpose_f32_tile(yt[tt % 3], yTs[tt % 2], pT, 8)
            transpose_f32_tile(pt_[tt % 3], pTs[tt % 2], pT, 2)

        def frontB(tt):
            yT, pT_ = yTs[tt % 2], pTs[tt % 2]
            for n0 in (0, 512):
                for kc in range(8):
                    T.op("pe", lambda e: e.matmul(out=pG[:, n0:n0 + 512], lhsT=yT[:, kc, :], rhs=Wg[:, kc, n0:n0 + 512],
                                                  start=(kc == 0), stop=(kc == 7)), [yT, Wg], [pG],
                         inc=(kc == 7 and n0 == 512))
            for n0 in (0, 512):
                for kc in range(2):
                    T.op("pe", lambda e: e.matmul(out=pP[:, n0:n0 + 512], lhsT=pT_[:, kc, :], rhs=Wp[:, kc, n0:n0 + 512],
                                                  start=(kc == 0), stop=(kc == 1)), [pT_, Wp], [pP],
                         inc=(kc == 1 and n0 == 512))

        def back1(tt):
            tg = tgs[tt % 2]
            for n0 in (0, 512):
                T.op("dve", lambda e: e.tensor_tensor(out=tg[:, n0:n0 + 512], in0=pG[:, n0:n0 + 512],
                                                       in1=bg_bc[:, n0:n0 + 512], op=ALU.add), [pG, bg_bc], [tg])
            T.op("act", lambda e: e.activation(out=tg[:, :], in_=tg[:, :], func=AF.Sigmoid), [tg], [tg])
            for n0 in (0, 512):
                T.op("dve", lambda e: e.tensor_tensor(out=tg[:, n0:n0 + 512], in0=tg[:, n0:n0 + 512],
                                                       in1=pP[:, n0:n0 + 512], op=ALU.mult), [tg, pP], [tg])

        def back2(tt):
            tg, y, o = tgs[tt % 2], yt[tt % 3], oo[tt % 2]
            T.op("pool", lambda e: e.tensor_tensor(out=o[:, :], in0=tg[:, :], in1=y[:, :], op=ALU.add),
                 [tg, y], [o])
            T.dma("sp", [(dst[tt * 128:(tt + 1) * 128, :], o[:, :])], reads=[o])

        loads(0)
        if NT > 1:
            loads(1)
        frontA(0)
        frontB(0)
        for tt in range(NT):
            if tt + 2 < NT:
                loads(tt + 2)
            if tt + 1 < NT:
                frontA(tt + 1)
            back1(tt)
            if tt + 1 < NT:
                frontB(tt + 1)
            back2(tt)
        phase_end()
        es.close()

    import os
    nstop = int(os.environ.get("KSTOP", "99"))
    phase_end()
    def _run():
        n = 0
        def chk():
            nonlocal n
            n += 1
            return n > nstop
        if chk(): return
        phase_A0()
        for li in range(2):
            eo = contextlib.ExitStack()
            c1w = []

            def hook(es_, li=li):
                c1w.extend(preload_C1(es_, li, a_w_out if li == 0 else c_w_out))

            Oall = phase_B0(eo, hook) if li == 0 else phase_B1(eo, hook)
            er = contextlib.ExitStack()
            W13 = [load_w(er, "c2_w1", ffn_w1[li], 8, DFF, side="right"),
                   load_w(er, "c2_w3", ffn_w3[li], 8, DFF, side="right")]
            GBUFS.extend(W13)
            phase_C1(li, Oall, c1w, x_in if li == 0 else X1)
            for b_ in c1w + [Oall]:
                GBUFS.remove(b_)
            eo.close()
            phase_C2(li, W13)
            for b_ in W13:
                GBUFS.remove(b_)
            er.close()
            if li == 0:
                era = contextlib.ExitStack()
                WinA1 = load_w(era, "c_win", c_w_in, 8, 3088, side="right")
                GBUFS.append(WinA1)
                phase_C3(0, X1)
                phase_A1(WinA1)
                GBUFS.remove(WinA1)
                era.close()
            else:
                phase_C3(1, out_d)
    _run()
    if os.environ.get("KSIM"):
        simulate_sync(T)
    gs.close()
    return nc


def make_consts(S):
    NT = S // 128
    p = np.arange(128)[:, None]
    f = np.arange(128)[None, :]
    c = np.zeros((128, 128 * 5 + NT + 16), np.float32)
    c[:, 0:128] = (p == f)
    c[:, 128:256] = (p + f == 127)
    c[:, 256:384] = (p <= f)
    c[:, 384:512] = ((p // 64) <= (f // 64))
    c[:, 512:640] = (p + f >= 128)
    c[:, 640:640 + NT] = (np.arange(NT)[None, :] * 128 + p).astype(np.float32)
    inv = (1.0 / (np.float32(10000.0) ** (np.arange(0, 32, 2, dtype=np.float32) / np.float32(32)))).astype(np.float32)
    c[:, 640 + NT:] = inv[None, :]
    return c


_CACHE = {}


def run(inputs, S):
    B = inputs["x"].shape[0]
    if S not in _CACHE:
        _CACHE[S] = build(S)
    nc = _CACHE[S]
    cst = make_consts(S)
    f32 = lambda a: np.ascontiguousarray(np.asarray(a, dtype=np.float32))
    shared = {
        "a_w_in": f32(inputs["a_w_in"][0]), "a_q_norm": f32(inputs["a_q_norm"][0]),
        "a_w_uq": f32(inputs["a_w_uq"][0]), "a_kv_norm": f32(inputs["a_kv_norm"][0]),
        "a_w_ukv": f32(inputs["a_w_ukv"][0]), "a_w_out": f32(inputs["a_w_out"][0]),
        "c_w_in": f32(inputs["c_w_in"][0]), "c_b_f": f32(inputs["c_b_f"][0]),
        "c_w_out": f32(inputs["c_w_out"][0]),
        "ffn_w1": f32(inputs["ffn_w1"]), "ffn_w3": f32(inputs["ffn_w3"]), "ffn_w2": f32(inputs["ffn_w2"]),
        "ln1_g": f32(inputs["ln1_g"]), "ln1_b": f32(inputs["ln1_b"]),
        "ln2_g": f32(inputs["ln2_g"]), "ln2_b": f32(inputs["ln2_b"]),
        "ple_w_proj": f32(inputs["ple_w_proj"]), "ple_w_gate": f32(inputs["ple_w_gate"]),
        "ple_b_gate": f32(inputs["ple_b_gate"]), "cst": cst,
    }
    xs = f32(inputs["x"])
    ps_ = f32(inputs["p"])
    in_maps = []
    for b in range(B):
        m = dict(shared)
        m["x"] = np.ascontiguousarray(xs[b])
        m["p"] = np.ascontiguousarray(ps_[:, b])
        in_maps.append(m)
    res = run_bass_kernel_spmd(nc, in_maps, core_ids=list(range(B)))
    return np.stack([np.asarray(r["out"], dtype=np.float32) for r in res.results], axis=0)


def kernel(**inputs):
    return run(inputs, int(inputs["x"].shape[1]))
```

```python
import contextlib
import math
import numpy as np
import concourse.bass as bass
import concourse.mybir as mybir
from concourse.bass_utils import run_bass_kernel_spmd

F32 = mybir.dt.float32
BF16 = mybir.dt.bfloat16
I32 = mybir.dt.int32
AF = mybir.ActivationFunctionType
ALU = mybir.AluOpType

D = 1024
DFF = 2816
NFC = DFF // 128
ALPHA = (2.0 * 2) ** 0.25
PI = math.pi


class Buf:
    def __init__(self, name, ap=None):
        self.name = name
        self.ap = ap
        self.w = None
        self.r = {}
        self.dsem = None

    def __getitem__(self, idx):
        return self.ap[idx]


class Eng:
    def __init__(self, name, eng):
        self.name = name
        self.eng = eng
        self.sem = None
        self.count = 0
        self.seen = {}


class Trk:
    def __init__(self, nc):
        self.nc = nc
        self.nsem = 0
        self.E = {n: Eng(n, getattr(nc, a)) for n, a in
                  [("pe", "tensor"), ("act", "scalar"), ("dve", "vector"), ("pool", "gpsimd"), ("sp", "sync")]}
        self.dpool = []
        self.dlive = []
        self.phase_bufs = []
        self.log = {n: [] for n in self.E}
        self.cur_waits = []
        self.new_epoch()

    def _new_sem(self, name):
        self.nsem += 1
        return (self.nsem, self.nc.alloc_semaphore(name=f"{name}_{self.nsem}"))

    def new_epoch(self):
        for e in self.E.values():
            e.sem = self._new_sem("e" + e.name)
            e.count = 0

    def buf(self, name, ap=None):
        b = Buf(name, ap)
        self.phase_bufs.append(b)
        return b

    def _wait(self, E, tok):
        key, h, val, en = tok
        if E.seen.get(key, 0) >= val:
            return
        E.eng.wait_ge(h, val)
        E.seen[key] = val
        self.cur_waits.append((key, val))

    def _deps(self, E, reads, writes):
        for b in reads:
            if b.w is not None:
                for tw in (b.w if isinstance(b.w, list) else [b.w]):
                    if not (tw[3] == "pe" and E.name == "pe"):
                        self._wait(E, tw)
        for b in writes:
            if b.w is not None:
                for tw in (b.w if isinstance(b.w, list) else [b.w]):
                    if not (tw[3] == "pe" and E.name == "pe"):
                        self._wait(E, tw)
            for tok in b.r.values():
                if not (tok[3] == "pe" and E.name == "pe"):
                    self._wait(E, tok)

    def op(self, en, fn, reads=(), writes=(), inc=True):
        E = self.E[en]
        self._deps(E, reads, writes)
        ins = fn(E.eng)
        self.log[en].append((self.cur_waits, (E.sem[0], E.count + 1, 1) if inc else None))
        self.cur_waits = []
        if inc:
            E.count += 1
            ins.then_inc(E.sem[1], 1)
            tok = (E.sem[0], E.sem[1], E.count, en)
            for b in reads:
                b.r[en] = tok
            for b in writes:
                b.w = tok
                b.r = {}
        return ins

    def dma(self, qn, pairs, reads=(), writes=(), **kw):
        E = self.E[qn]
        self._deps(E, reads, writes)
        owner = (list(writes) + list(reads))[0]
        if owner.dsem is None:
            if self.dpool:
                owner.dsem = self.dpool.pop()
            else:
                k, h = self._new_sem("d")
                owner.dsem = [k, h, 0]
                self.dlive.append(owner.dsem)
        ds = owner.dsem
        self.log[qn].append((self.cur_waits, (ds[0], 16 * (ds[2] + len(pairs)), 16 * len(pairs))))
        self.cur_waits = []
        for (o, i) in pairs:
            E.eng.dma_start(out=o, in_=i, **kw).then_inc(ds[1], 16)
            ds[2] += 1
        tok = (ds[0], ds[1], 16 * ds[2], "dma")
        for b in reads:
            b.r[("dma", ds[0])] = tok
        for b in writes:
            b.w = tok
            b.r = {}

    def dma_fresh(self, qn, pairs, writes, **kw):
        E = self.E[qn]
        self._deps(E, (), writes)
        toks = []
        for (o, i) in pairs:
            k, h = self._new_sem("w")
            ds = [k, h, 1]
            self.dlive.append(ds)
            self.log[qn].append((self.cur_waits, (k, 16, 16)))
            self.cur_waits = []
            E.eng.dma_start(out=o, in_=i, **kw).then_inc(h, 16)
            toks.append((k, h, 16, "dma"))
        for b in writes:
            b.w = toks
            b.r = {}

    def barrier(self):
        toks = []
        for n, e in self.E.items():
            if e.count > 0:
                toks.append((e.sem[0], e.sem[1], e.count, n))
        for ds in self.dlive:
            if ds[2] > 0:
                toks.append((ds[0], ds[1], 16 * ds[2], "dma"))
        for E in self.E.values():
            for tok in toks:
                self._wait(E, tok)
            self.log[E.name].append((self.cur_waits, None))
            self.cur_waits = []
        self.nbar = getattr(self, "nbar", 0) + 1
        if self.nbar % 2 == 0:
            self.new_epoch()
        for b in self.phase_bufs:
            if b.dsem is not None:
                self.dpool.append(b.dsem)
                b.dsem = None
            b.w = None
            b.r = {}
        self.phase_bufs = []


def simulate_sync(T):
    sem = {}
    ptr = {n: 0 for n in T.log}
    total = sum(len(v) for v in T.log.values())
    done = 0
    while done < total:
        prog = False
        for n, ops in T.log.items():
            while ptr[n] < len(ops):
                waits, prod = ops[ptr[n]]
                if all(sem.get(k, 0) >= v for (k, v) in waits):
                    if prod is not None:
                        sem[prod[0]] = sem.get(prod[0], 0) + prod[2]
                    ptr[n] += 1
                    done += 1
                    prog = True
                else:
                    break
        if not prog:
            print("DEADLOCK")
            for n, ops in T.log.items():
                if ptr[n] < len(ops):
                    waits, prod = ops[ptr[n]]
                    print(n, ptr[n], len(ops), [(k, v, sem.get(k, 0)) for (k, v) in waits if sem.get(k, 0) < v], prod)
            return False
    print("SYNC SIM OK", total)
    return True


import os as _os
SBLAG = int(_os.environ.get('SBLAG', '2'))


def build(S, taps=False):
    NT = S // 128
    nc = bass.Bass("TRN2", target_bir_lowering=False)
    T = Trk(nc)

    def din(name, shape, dt=F32):
        return nc.dram_tensor(name, list(shape), dt, kind="ExternalInput").ap()

    def dscr(name, shape, dt):
        return nc.dram_tensor(name, list(shape), dt, kind="Internal").ap()

    x_in = din("x", [S, D])
    p_in = din("p", [2, S, 256])
    a_w_in = din("a_w_in", [D, 2080])
    a_q_norm = din("a_q_norm", [256])
    a_w_uq = din("a_w_uq", [256, 768])
    a_kv_norm = din("a_kv_norm", [256])
    a_w_ukv = din("a_w_ukv", [256, 1024])
    a_w_out = din("a_w_out", [D, D])
    c_w_in = din("c_w_in", [D, 3088])
    c_b_f = din("c_b_f", [16])
    c_w_out = din("c_w_out", [D, D])
    ffn_w1 = din("ffn_w1", [2, D, DFF])
    ffn_w3 = din("ffn_w3", [2, D, DFF])
    ffn_w2 = din("ffn_w2", [2, DFF, D])
    ln1_g = din("ln1_g", [2, D])
    ln1_b = din("ln1_b", [2, D])
    ln2_g = din("ln2_g", [2, D])
    ln2_b = din("ln2_b", [2, D])
    ple_w_proj = din("ple_w_proj", [2, 256, D])
    ple_w_gate = din("ple_w_gate", [2, D, D])
    ple_b_gate = din("ple_b_gate", [2, D])
    NCST = 128 * 5 + NT + 16
    cst_in = din("cst", [128, NCST])
    out_d = nc.dram_tensor("out", [S, D], F32, kind="ExternalOutput").ap()

    QTm = dscr("QTm", [8, 96, S], BF16)
    KTm = dscr("KTm", [8, 96, S], BF16)
    Vm = dscr("Vm", [S, 8, 64], BF16)
    QTs = dscr("QTs", [512, S], BF16)
    KTs = dscr("KTs", [512, S], BF16)
    Vs = dscr("Vs", [S, 8, 64], BF16)
    QTf = dscr("QTf", [1024, S], BF16)
    KTf = dscr("KTf", [1024, S], BF16)
    QDf = dscr("QDf", [3, 16, S], BF16)
    KDf = dscr("KDf", [3, 16, S], BF16)
    Vf = dscr("Vf", [S, 16, 64], BF16)
    Y1 = dscr("Y1", [S, D], F32)
    Y2 = dscr("Y2", [S, D], F32)
    X1 = dscr("X1", [S, D], F32)

    gs = contextlib.ExitStack()

    uniq = [0]

    def sb(es, name, shape, dt, side=None):
        uniq[0] += 1
        if side is None:
            t = es.enter_context(nc.sbuf_tensor(f"{name}_u{uniq[0]}", list(shape), dt))
        else:
            t = es.enter_context(nc.sbuf_tensor(f"{name}_u{uniq[0]}", list(shape), dt, side=side))
        return T.buf(name, t)

    def ps(es, name, shape, dt):
        uniq[0] += 1
        t = es.enter_context(nc.psum_tensor(f"{name}_u{uniq[0]}", list(shape), dt))
        return T.buf(name, t)

    cst = sb(gs, "cst_sb", [128, NCST], F32)
    ident_bf = sb(gs, "ident_bf", [128, 128], BF16)
    mC_bf = sb(gs, "mC_bf", [128, 128], BF16)
    mM_bf = sb(gs, "mM_bf", [128, 128], BF16)
    mS_bf = sb(gs, "mS_bf", [128, 128], BF16)
    ones_f = sb(gs, "ones_f", [128, 512], F32)
    mhalf = sb(gs, "mhalf", [128, 1], F32)
    GBUFS = [cst, ident_bf, mC_bf, mM_bf, mS_bf, ones_f, mhalf]
    ident_f = cst[:, 0:128]
    J_f = cst[:, 128:256]
    mS_f = cst[:, 512:640]
    pos_f = cst[:, 640:640 + NT]
    invf = cst[:, 640 + NT:640 + NT + 16]

    T.dma("sp", [(cst[:, :], cst_in[:, :])], writes=[cst])
    T.op("dve", lambda e: e.tensor_copy(out=ident_bf[:, :], in_=cst[:, 0:128]), [cst], [ident_bf])
    T.op("dve", lambda e: e.tensor_copy(out=mC_bf[:, :], in_=cst[:, 256:384]), [cst], [mC_bf])
    T.op("dve", lambda e: e.tensor_copy(out=mM_bf[:, :], in_=cst[:, 384:512]), [cst], [mM_bf])
    T.op("dve", lambda e: e.tensor_copy(out=mS_bf[:, :], in_=cst[:, 512:640]), [cst], [mS_bf])
    T.op("dve", lambda e: e.memset(ones_f[:, :], 1.0), [], [ones_f])
    T.op("dve", lambda e: e.memset(mhalf[:, :], -0.5), [], [mhalf])

    def load_w(es, name, src, kc, n, q="pool", side=None):
        w = sb(es, name, [128, kc, n], BF16, side=side)
        v = src.rearrange("(c p) n -> p c n", p=128)
        T.dma_fresh(q, [(w[:, :, n0:min(n, n0 + 1024)], v[:, :, n0:min(n, n0 + 1024)]) for n0 in range(0, n, 1024)],
                    writes=[w])
        return w

    def load_bc(es, name, src, n=D):
        t = sb(es, name, [128, n], F32)
        T.dma("sp", [(t[:, :], src.partition_broadcast(128))], writes=[t])
        return t

    def phase_end():
        T.barrier()
        for b in GBUFS:
            T.phase_bufs.append(b)

    def layer_norm(r, g_bc, b_bc, out, st, eps=1e-5):
        T.op("dve", lambda e: e.bn_stats(out=st[:, 0:6], in_=r[:, 0:512]), [r], [st])
        T.op("dve", lambda e: e.bn_stats(out=st[:, 6:12], in_=r[:, 512:1024]), [r], [st])
        T.op("dve", lambda e: e.bn_aggr(out=st[:, 12:14], in_=st[:, 0:12]), [st], [st])
        T.op("dve", lambda e: e.tensor_scalar(out=st[:, 14:15], in0=st[:, 13:14], scalar1=eps, scalar2=None,
                                               op0=ALU.add), [st], [st])
        T.op("pool", lambda e: e.tensor_tensor(out=st[:, 16:17], in0=st[:, 14:15], in1=mhalf[:, 0:1], op=ALU.pow),
             [st, mhalf], [st])
        T.op("dve", lambda e: e.scalar_tensor_tensor(out=st[:, 17:18], in0=st[:, 12:13], scalar=-1.0,
                                                      in1=st[:, 16:17], op0=ALU.mult, op1=ALU.mult), [st], [st])
        T.op("act", lambda e: e.activation(out=r[:, :], in_=r[:, :], func=AF.Identity,
                                           scale=st[:, 16:17], bias=st[:, 17:18]), [r, st], [r])
        T.op("dve", lambda e: e.tensor_tensor(out=r[:, :], in0=r[:, :], in1=g_bc[:, :], op=ALU.mult), [r, g_bc], [r])
        T.op("pool", lambda e: e.tensor_tensor(out=out[:, :], in0=r[:, :], in1=b_bc[:, :], op=ALU.add), [r, b_bc], [out])

    def transpose_f32_tile(src, dstT, pT, nblk):
        for h0 in range(0, nblk, 4):
            n = min(4, nblk - h0)
            for i in range(n):
                T.op("pe", lambda e, i=i: e.transpose(out=pT[:, i * 128:(i + 1) * 128],
                                                       in_=src[:, (h0 + i) * 128:(h0 + i + 1) * 128],
                                                       identity=ident_f), [src, cst], [pT], inc=(i == n - 1))
            eng = "act" if (h0 // 4) % 2 == 0 else "dve"
            if eng == "act":
                T.op("act", lambda e: e.activation(out=dstT[:, h0:h0 + n, :],
                                                   in_=pT[:, 0:n * 128].rearrange("p (c t) -> p c t", t=128),
                                                   func=AF.Copy), [pT], [dstT])
            else:
                T.op("dve", lambda e: e.tensor_copy(out=dstT[:, h0:h0 + n, :],
                                                    in_=pT[:, 0:n * 128].rearrange("p (c t) -> p c t", t=128)),
                     [pT], [dstT])

    def phase_A0():
        es = contextlib.ExitStack()
        Win = load_w(es, "a_win", a_w_in, 8, 2080)
        Wuq = load_w(es, "a_wuq", a_w_uq, 2, 768)
        Wukv = load_w(es, "a_wukv", a_w_ukv, 2, 1024)
        g4 = sb(es, "g4", [128, 4], F32)
        T.dma("sp", [(g4[:, 0:2], a_q_norm.rearrange("(c p) -> p c", p=128)),
                     (g4[:, 2:4], a_kv_norm.rearrange("(c p) -> p c", p=128))], writes=[g4],
              allow_slow_non_contiguous=True)
        cosT = sb(es, "cosT", [128, NT, 16], F32)
        sinT = sb(es, "sinT", [128, NT, 16], F32)
        ang = sb(es, "ang", [128, NT, 16], F32)
        tmpa = sb(es, "tmpa", [128, NT, 16], F32)
        ki = sb(es, "ki", [128, NT, 16], I32)
        kf = sb(es, "kf", [128, NT, 16], F32)
        T.op("dve", lambda e: e.tensor_tensor(out=ang[:, :, :], in0=pos_f.unsqueeze(2).broadcast_to([128, NT, 16]),
                                               in1=invf.unsqueeze(1).broadcast_to([128, NT, 16]), op=ALU.mult),
             [cst], [ang])
        for (dst, shift) in ((sinT, 0.0), (cosT, PI / 2)):
            T.op("dve", lambda e: e.tensor_scalar(out=tmpa[:, :, :], in0=ang[:, :, :], scalar1=shift,
                                                   scalar2=1.0 / (2 * PI), op0=ALU.add, op1=ALU.mult), [ang], [tmpa])
            T.op("dve", lambda e: e.tensor_copy(out=ki[:, :, :], in_=tmpa[:, :, :]), [tmpa], [ki])
            T.op("dve", lambda e: e.tensor_copy(out=kf[:, :, :], in_=ki[:, :, :]), [ki], [kf])
            T.op("dve", lambda e: e.scalar_tensor_tensor(out=tmpa[:, :, :], in0=kf[:, :, :], scalar=-2 * PI,
                                                          in1=ang[:, :, :], op0=ALU.mult, op1=ALU.add), [kf, ang], [tmpa])
            T.op("dve", lambda e: e.tensor_scalar(out=tmpa[:, :, :], in0=tmpa[:, :, :], scalar1=shift,
                                                   scalar2=3.1415925, op0=ALU.add, op1=ALU.min), [tmpa], [tmpa])
            T.op("dve", lambda e: e.tensor_scalar(out=tmpa[:, :, :], in0=tmpa[:, :, :], scalar1=-3.1415925,
                                                   scalar2=None, op0=ALU.max), [tmpa], [tmpa])
            T.op("act", lambda e: e.activation(out=dst[:, :, :], in_=tmpa[:, :, :], func=AF.Sin), [tmpa], [dst])

        import os
        ksub = int(os.environ.get("KSUB", "99"))
        if ksub == 0:
            phase_end(); es.close(); return
        xt = [sb(es, f"a0_x{i}", [128, D], F32) for i in range(2)]
        xT = sb(es, "a0_xT", [128, 8, 128], BF16)
        xTr = sb(es, "a0_xTr", [128, 8, 128], BF16)
        junk = sb(es, "a0_junk", [128, 256], F32)
        st = sb(es, "a0_st", [128, 16], F32)
        cs = sb(es, "a0_cs", [128, 512], F32)
        cT = sb(es, "a0_cT", [128, 4, 128], BF16)
        qa = sb(es, "a0_qa", [128, 8, 96], BF16)
        ka = sb(es, "a0_ka", [128, 8, 96], BF16)
        va = sb(es, "a0_va", [128, 8, 64], BF16)
        rt = [sb(es, f"a0_rt{i}", [128, 8, 16], F32) for i in range(4)]
        kr = sb(es, "a0_kr", [128, 32], F32)
        krs = sb(es, "a0_krs", [128, 32], F32)
        qr = sb(es, "a0_qr", [128, 8, 32], F32)
        cos8 = sb(es, "a0_cos8", [128, NT, 8, 16], F32)
        sin8 = sb(es, "a0_sin8", [128, NT, 8, 16], F32)
        for h in range(8):
            T.op("dve", lambda e: e.tensor_copy(out=cos8[:, :, h, :], in_=cosT[:, :, :]), [cosT], [cos8])
            T.op("dve", lambda e: e.tensor_copy(out=sin8[:, :, h, :], in_=sinT[:, :, :]), [sinT], [sin8])
        krt = [sb(es, f"a0_krt{i}", [128, 16], F32) for i in range(4)]
        stQ = sb(es, "a0_stQ", [128, 8, 128], BF16)
        stK = sb(es, "a0_stK", [128, 8, 128], BF16)
        stQs = sb(es, "a0_stQs", [128, 4, 128], BF16)
        stKs = sb(es, "a0_stKs", [128, 4, 128], BF16)
        stVs = sb(es, "a0_stVs", [128, 512], BF16)
        pT = ps(es, "a0_pT", [128, 512], F32)
        pC = ps(es, "a0_pC", [128, 512], F32)
        pC2 = ps(es, "a0_pC2", [128, 512], F32)
        pQ = ps(es, "a0_pQ", [128, 1024], F32)
        pKV = ps(es, "a0_pKV", [128, 1024], F32)
        pB = ps(es, "a0_pB", [128, 1024], BF16)

        T.dma("sp", [(xt[0][:, :], x_in[0:128, :])], writes=[xt[0]])
        for tt in range(NT):
            x = xt[tt % 2]
            if tt + 1 < NT:
                nx = xt[(tt + 1) % 2]
                T.dma("sp", [(nx[:, :], x_in[(tt + 1) * 128:(tt + 2) * 128, :])], writes=[nx])
            rtt = NT - 1 - tt
            tsl = slice(tt * 128, (tt + 1) * 128)
            rsl = slice(rtt * 128, (rtt + 1) * 128)
            transpose_f32_tile(x, xT, pT, 8)
            for h0 in (0, 4):
                for i in range(4):
                    T.op("pe", lambda e, i=i: e.matmul(out=pT[:, i * 128:(i + 1) * 128],
                                                        lhsT=x[:, (h0 + i) * 128:(h0 + i + 1) * 128], rhs=J_f,
                                                        start=True, stop=True), [x, cst], [pT], inc=(i == 3))
                T.op("act", lambda e: e.activation(out=xTr[:, h0:h0 + 4, :],
                                                   in_=pT[:, :].rearrange("p (c t) -> p c t", t=128), func=AF.Copy),
                     [pT], [xTr])
            if ksub == 1: continue
            for kc in range(8):
                T.op("pe", lambda e: e.matmul(out=pC[:, 0:512], lhsT=xT[:, kc, :], rhs=Win[:, kc, 0:512],
                                              start=(kc == 0), stop=(kc == 7)), [xT, Win], [pC], inc=(kc == 7))
            for kc in range(8):
                T.op("pe", lambda e: e.matmul(out=pC2[:, 0:32], lhsT=xT[:, kc, :], rhs=Win[:, kc, 512:544],
                                              start=(kc == 0), stop=(kc == 7)), [xT, Win], [pC2], inc=(kc == 7))
            for hp in range(4):
                for kc in range(8):
                    T.op("pe", lambda e: e.matmul(out=pKV[:, hp * 128:(hp + 1) * 128],
                                                  lhsT=Win[:, kc, 544 + hp * 128:544 + (hp + 1) * 128],
                                                  rhs=xT[:, kc, :], start=(kc == 0), stop=(kc == 7)),
                         [xT, Win], [pKV], inc=False)
            for hp in range(4):
                for kc in range(8):
                    T.op("pe", lambda e: e.matmul(out=pKV[:, 512 + hp * 128:512 + (hp + 1) * 128],
                                                  lhsT=Win[:, kc, 1056 + hp * 128:1056 + (hp + 1) * 128],
                                                  rhs=xTr[:, kc, :], start=(kc == 0), stop=(kc == 7)),
                         [xT, xTr, Win], [pKV], inc=(kc == 7 and hp == 3))
            for kc in range(8):
                T.op("pe", lambda e: e.matmul(out=pQ[:, 0:512], lhsT=xTr[:, kc, :], rhs=Win[:, kc, 1568:2080],
                                              start=(kc == 0), stop=(kc == 7)), [xTr, Win], [pQ], inc=(kc == 7))
            T.op("act", lambda e: e.activation(out=junk[:, :], in_=pC[:, 0:256], func=AF.Square,
                                               accum_out=st[:, 0:1]), [pC], [junk, st])
            T.op("act", lambda e: e.activation(out=junk[:, :], in_=pC[:, 256:512], func=AF.Square,
                                               accum_out=st[:, 1:2]), [pC], [junk, st])
            T.op("dve", lambda e: e.tensor_scalar(out=st[:, 2:4], in0=st[:, 0:2], scalar1=1.0 / 256, scalar2=1e-6,
                                                   op0=ALU.mult, op1=ALU.add), [st], [st])
            T.op("act", lambda e: e.activation(out=st[:, 4:6], in_=st[:, 2:4], func=AF.Sqrt), [st], [st])
            T.op("dve", lambda e: e.reciprocal(out=st[:, 6:8], in_=st[:, 4:6]), [st], [st])
            T.op("dve", lambda e: e.tensor_scalar(out=cs[:, 0:256], in0=pC[:, 0:256], scalar1=st[:, 6:7],
                                                   scalar2=None, op0=ALU.mult), [pC, st], [cs])
            T.op("dve", lambda e: e.tensor_scalar(out=cs[:, 256:512], in0=pC[:, 256:512], scalar1=st[:, 7:8],
                                                   scalar2=None, op0=ALU.mult), [pC, st], [cs])
            for i in range(4):
                T.op("pe", lambda e: e.transpose(out=pT[:, i * 128:(i + 1) * 128], in_=cs[:, i * 128:(i + 1) * 128],
                                                 identity=ident_f), [cs, cst], [pT], inc=(i == 3))
            for i in range(4):
                T.op("dve", lambda e: e.tensor_scalar(out=cT[:, i, :], in0=pT[:, i * 128:(i + 1) * 128],
                                                       scalar1=g4[:, i:i + 1], scalar2=None, op0=ALU.mult),
                     [pT, g4], [cT])
            T.op("dve", lambda e: e.tensor_copy(out=stQs[:, :, :],
                                                in_=pKV[:, 0:512].rearrange("p (c t) -> p c t", t=128)), [pKV], [stQs])
            T.dma("sp", [(QTs.rearrange("(c p) t -> p c t", p=128)[:, :, tsl], stQs[:, :, :])], reads=[stQs])
            T.op("act", lambda e: e.activation(out=stKs[:, :, :],
                                               in_=pKV[:, 512:1024].rearrange("p (c t) -> p c t", t=128),
                                               func=AF.Copy), [pKV], [stKs])
            T.dma("sp", [(KTs.rearrange("(c p) t -> p c t", p=128)[:, :, rsl], stKs[:, :, :])], reads=[stKs])
            T.op("dve", lambda e: e.tensor_copy(out=stVs[:, :], in_=pQ[:, 0:512]), [pQ], [stVs])
            T.dma("sp", [(Vs[rsl, :, :], stVs[:, :].rearrange("p (h d) -> p h d", d=64))], reads=[stVs])
            if ksub == 2: continue
            for bq in range(2):
                for kc in range(2):
                    T.op("pe", lambda e: e.matmul(out=pQ[:, bq * 512:bq * 512 + 384], lhsT=cT[:, kc, :],
                                                  rhs=Wuq[:, kc, bq * 384:(bq + 1) * 384],
                                                  start=(kc == 0), stop=(kc == 1)), [cT, Wuq], [pQ],
                         inc=(kc == 1 and bq == 1))
            for (n0, n1) in ((0, 512), (512, 1024)):
                for kc in range(2):
                    T.op("pe", lambda e: e.matmul(out=pKV[:, n0:n1], lhsT=cT[:, 2 + kc, :], rhs=Wukv[:, kc, n0:n1],
                                                  start=(kc == 0), stop=(kc == 1)), [cT, Wukv], [pKV],
                         inc=(kc == 1 and n0 == 512))
            if ksub == 21: continue
            for bq in range(2):
                hs = slice(4 * bq, 4 * bq + 4)
                q3 = pQ[:, bq * 512:bq * 512 + 384].rearrange("p (h d) -> p h d", d=96)
                kv3 = pKV[:, bq * 512:(bq + 1) * 512].rearrange("p (h d) -> p h d", d=128)
                T.op("act", lambda e: e.activation(out=qa[:, hs, 0:64], in_=q3[:, :, 0:64], func=AF.Copy), [pQ], [qa])
                T.op("act", lambda e: e.activation(out=qr[:, hs, :], in_=q3[:, :, 64:96], func=AF.Copy), [pQ], [qr])
                T.op("act", lambda e: e.activation(out=ka[:, hs, 0:64], in_=kv3[:, :, 0:64], func=AF.Copy), [pKV], [ka])
                T.op("act", lambda e: e.activation(out=va[:, hs, :], in_=kv3[:, :, 64:128], func=AF.Copy), [pKV], [va])
            T.op("act", lambda e: e.activation(out=krs[:, :], in_=pC2[:, 0:32], func=AF.Copy), [pC2], [krs])
            if ksub == 22: continue
            cosb = cos8[:, tt, :, :]
            sinb = sin8[:, tt, :, :]
            T.op("dve", lambda e: e.tensor_tensor(out=rt[0][:, :, :], in0=qr[:, :, 0:16], in1=cosb, op=ALU.mult),
                 [qr, cos8], [rt[0]])
            T.op("dve", lambda e: e.tensor_tensor(out=rt[1][:, :, :], in0=qr[:, :, 16:32], in1=sinb, op=ALU.mult),
                 [qr, sin8], [rt[1]])
            T.op("dve", lambda e: e.tensor_tensor(out=rt[2][:, :, :], in0=qr[:, :, 0:16], in1=sinb, op=ALU.mult),
                 [qr, sin8], [rt[2]])
            T.op("dve", lambda e: e.tensor_tensor(out=rt[3][:, :, :], in0=qr[:, :, 16:32], in1=cosb, op=ALU.mult),
                 [qr, cos8], [rt[3]])
            if ksub == 23: continue
            T.op("pool", lambda e: e.tensor_tensor(out=qa[:, :, 64:80], in0=rt[0][:, :, :], in1=rt[1][:, :, :],
                                                   op=ALU.subtract), [rt[0], rt[1]], [qa])
            T.op("pool", lambda e: e.tensor_tensor(out=qa[:, :, 80:96], in0=rt[2][:, :, :], in1=rt[3][:, :, :],
                                                   op=ALU.add), [rt[2], rt[3]], [qa])
            if ksub == 24: continue
            T.op("dve", lambda e: e.tensor_tensor(out=krt[0][:, :], in0=krs[:, 0:16], in1=cosT[:, tt, :], op=ALU.mult),
                 [krs, cosT], [krt[0]])
            T.op("dve", lambda e: e.tensor_tensor(out=krt[1][:, :], in0=krs[:, 16:32], in1=sinT[:, tt, :], op=ALU.mult),
                 [krs, sinT], [krt[1]])
            T.op("dve", lambda e: e.tensor_tensor(out=krt[2][:, :], in0=krs[:, 0:16], in1=sinT[:, tt, :], op=ALU.mult),
                 [krs, sinT], [krt[2]])
            T.op("dve", lambda e: e.tensor_tensor(out=krt[3][:, :], in0=krs[:, 16:32], in1=cosT[:, tt, :], op=ALU.mult),
                 [krs, cosT], [krt[3]])
            T.op("pool", lambda e: e.tensor_tensor(out=kr[:, 0:16], in0=krt[0][:, :], in1=krt[1][:, :],
                                                   op=ALU.subtract), [krt[0], krt[1]], [kr])
            T.op("pool", lambda e: e.tensor_tensor(out=kr[:, 16:32], in0=krt[2][:, :], in1=krt[3][:, :],
                                                   op=ALU.add), [krt[2], krt[3]], [kr])
            T.op("dve", lambda e: e.tensor_scalar(out=ka[:, :, 64:96],
                                                   in0=kr[:, :].unsqueeze(1).broadcast_to([128, 8, 32]),
                                                   scalar1=1.0, scalar2=None, op0=ALU.mult), [kr], [ka])
            if ksub == 25: continue
            T.dma("sp", [(Vm[tsl, :, :], va[:, :, :])], reads=[va])
            if ksub == 3: continue
            for (src, dst, dr) in ((qa, stQ, QTm), (ka, stK, KTm)):
                for h in range(8):
                    T.op("pe", lambda e: e.transpose(out=pB[0:96, h * 128:(h + 1) * 128], in_=src[:, h, :],
                                                     identity=ident_bf[:, :]), [src, ident_bf], [pB], inc=(h == 7))
                T.op("act", lambda e: e.activation(out=dst[0:96, :, :],
                                                   in_=pB[0:96, :].rearrange("p (h t) -> p h t", t=128),
                                                   func=AF.Copy), [pB], [dst])
                T.dma("sp", [(dr.rearrange("h r t -> r h t")[:, :, tsl], dst[0:96, :, :])], reads=[dst])
        phase_end()
        es.close()

    def phase_A1(Win=None):
        es = contextlib.ExitStack()
        if Win is None:
            Win = load_w(es, "c_win", c_w_in, 8, 3088)
        nbf = sb(es, "a1_nbf", [16, 1], F32)
        T.dma("sp", [(nbf[:, :], c_b_f.rearrange("(h o) -> h o", o=1))], writes=[nbf])
        xt = [sb(es, f"a1_x{i}", [128, D], F32) for i in range(2)]
        xTg = [sb(es, f"a1_xT{i}", [128, 8, 512], BF16) for i in range(2)]
        FL = sb(es, "a1_FL", [16, S], F32)
        G = sb(es, "a1_G", [16, S], F32)
        R1 = sb(es, "a1_R1", [16, S], F32)
        dq = [sb(es, f"a1_dq{i}", [16, S], BF16) for i in range(3)]
        dk = [sb(es, f"a1_dk{i}", [16, S], BF16) for i in range(3)]
        stQ = sb(es, "a1_stQ", [128, 8, 512], BF16)
        stK = sb(es, "a1_stK", [128, 8, 512], BF16)
        stV = [sb(es, f"a1_stV{i}", [128, 1024], BF16) for i in range(2)]
        pT = ps(es, "a1_pT", [128, 512], F32)
        pQ = [ps(es, f"a1_pQ{i}", [128, 512], F32) for i in range(2)]
        pK = [ps(es, f"a1_pK{i}", [128, 512], F32) for i in range(2)]
        pV = ps(es, "a1_pV", [128, 1024], F32)
        pF = ps(es, "a1_pF", [128, 512], F32)
        T.op("dve", lambda e: e.tensor_scalar(out=nbf[:, :], in0=nbf[:, :], scalar1=-1.0, scalar2=None, op0=ALU.mult),
             [nbf], [nbf])
        NG = NT // 4
        T.dma("sp", [(xt[0][:, :], X1[0:128, :])], writes=[xt[0]])
        for g in range(NG):
            xT = xTg[g % 2]
            gsl = slice(g * 512, (g + 1) * 512)
            for ti in range(4):
                tt = g * 4 + ti
                x = xt[tt % 2]
                if tt + 1 < NT:
                    nx = xt[(tt + 1) % 2]
                    T.dma("sp", [(nx[:, :], X1[(tt + 1) * 128:(tt + 2) * 128, :])], writes=[nx])
                for h0 in (0, 4):
                    for i in range(4):
                        T.op("pe", lambda e: e.transpose(out=pT[:, i * 128:(i + 1) * 128],
                                                         in_=x[:, (h0 + i) * 128:(h0 + i + 1) * 128],
                                                         identity=ident_f), [x, cst], [pT], inc=(i == 3))
                    if h0 == 0:
                        T.op("act", lambda e: e.activation(out=xT[:, 0:4, ti * 128:(ti + 1) * 128],
                                                           in_=pT[:, :].rearrange("p (c t) -> p c t", t=128),
                                                           func=AF.Copy), [pT], [xT])
                    else:
                        T.op("dve", lambda e: e.tensor_copy(out=xT[:, 4:8, ti * 128:(ti + 1) * 128],
                                                            in_=pT[:, :].rearrange("p (c t) -> p c t", t=128)),
                             [pT], [xT])
                sv = stV[tt % 2]
                for n0 in (0, 512):
                    for kc in range(8):
                        T.op("pe", lambda e: e.matmul(out=pV[:, n0:n0 + 512], lhsT=xT[:, kc, ti * 128:(ti + 1) * 128],
                                                      rhs=Win[:, kc, 2048 + n0:2048 + n0 + 512],
                                                      start=(kc == 0), stop=(kc == 7)), [xT, Win], [pV],
                             inc=(kc == 7 and n0 == 512))
                T.op("act", lambda e: e.activation(out=sv[:, 0:512], in_=pV[:, 0:512], func=AF.Copy), [pV], [sv])
                T.op("dve", lambda e: e.tensor_copy(out=sv[:, 512:1024], in_=pV[:, 512:1024]), [pV], [sv])
                T.dma("sp", [(Vf[tt * 128:(tt + 1) * 128, :, :], sv[:, :].rearrange("p (h d) -> p h d", d=64))],
                      reads=[sv])
            for (pp2, st_, c0, dr) in ((pQ, stQ, 0, QTf), (pK, stK, 1024, KTf)):
                for c in range(8):
                    pb = pp2[c % 2]
                    for kc in range(8):
                        T.op("pe", lambda e: e.matmul(out=pb[:, 0:512],
                                                      lhsT=Win[:, kc, c0 + c * 128:c0 + (c + 1) * 128],
                                                      rhs=xT[:, kc, :], start=(kc == 0), stop=(kc == 7)),
                             [xT, Win], [pb], inc=(kc == 7))
                    if c % 2 == 0:
                        T.op("act", lambda e: e.activation(out=st_[:, c, :], in_=pb[:, 0:512], func=AF.Copy),
                             [pb], [st_])
                    else:
                        T.op("dve", lambda e: e.tensor_copy(out=st_[:, c, :], in_=pb[:, 0:512]), [pb], [st_])
                T.dma("sp", [(dr.rearrange("(c p) t -> p c t", p=128)[:, :, gsl], st_[:, :, :])], reads=[st_])
            for kc in range(8):
                T.op("pe", lambda e: e.matmul(out=pF[0:16, 0:512], lhsT=Win[:, kc, 3072:3088], rhs=xT[:, kc, :],
                                              start=(kc == 0), stop=(kc == 7)), [xT, Win], [pF], inc=(kc == 7))
            T.op("dve", lambda e: e.tensor_copy(out=FL[:, gsl], in_=pF[0:16, 0:512]), [pF], [FL])
        T.op("act", lambda e: e.activation(out=FL[:, :], in_=FL[:, :], func=AF.Exp, scale=-1.0, bias=nbf[:, 0:1]),
             [FL, nbf], [FL])
        T.op("act", lambda e: e.activation(out=FL[:, :], in_=FL[:, :], func=AF.Ln, bias=1.0), [FL], [FL])
        for c in range(S // 512):
            csl = slice(c * 512, (c + 1) * 512)
            ini = 0.0 if c == 0 else G[:, c * 512 - 1:c * 512]
            T.op("dve", lambda e: e.tensor_tensor_scan(out=G[:, csl], data0=ones_f[0:16, 0:512],
                                                        data1=FL[:, csl], initial=ini, op0=ALU.mult, op1=ALU.add),
                 [FL, ones_f, G], [G])
        T.op("dve", lambda e: e.tensor_scalar(out=G[:, :], in0=G[:, :], scalar1=-8.0, scalar2=None, op0=ALU.mult),
             [G], [G])
        cur = G
        for i in range(3):
            T.op("dve", lambda e: e.tensor_copy(out=dq[i][:, :], in_=cur[:, :]), [cur], [dq[i]])
            T.op("dve", lambda e: e.tensor_scalar(out=dk[i][:, :], in0=dq[i][:, :], scalar1=-1.0, scalar2=None,
                                                   op0=ALU.mult), [dq[i]], [dk[i]])
            if i < 2:
                nxt = R1
                T.op("dve", lambda e: e.tensor_tensor(out=nxt[:, :], in0=cur[:, :], in1=dq[i][:, :], op=ALU.subtract),
                     [cur, dq[i]], [nxt])
                cur = nxt
            T.dma("sp", [(QDf[i], dq[i][:, :])], reads=[dq[i]])
            T.dma("sp", [(KDf[i], dk[i][:, :])], reads=[dk[i]])
        phase_end()
        es.close()

    def softmax_heads(es, Oall, heads, kd, scale, mask_bf, loader, QTa, KTa, Va, PT, pS, pO):
        NQB = S // 512
        first = True
        for hi, (h, ocol) in enumerate(heads):
            sl = hi % 2
            if first:
                loader(h, QTa[sl], KTa[sl], Va[sl])
                first = False
            if hi + 1 < len(heads):
                loader(heads[hi + 1][0], QTa[1 - sl], KTa[1 - sl], Va[1 - sl])
            Q, K, V = QTa[sl], KTa[sl], Va[sl]
            tiles = []
            for j in range(NQB):
                for kt in range(4 * j + 4):
                    tiles.append((j, kt))
            rn = sb_small["rn"]

            NS = len(pS)

            def qk(idx):
                j, kt = tiles[idx]
                i = max(0, kt - 4 * j)
                c0 = 128 * i
                pb = pS[idx % NS]
                T.op("pe", lambda e: e.matmul(out=pb[:, c0:512], lhsT=K[0:kd, kt * 128:(kt + 1) * 128],
                                              rhs=Q[0:kd, j * 512 + c0:(j + 1) * 512], start=True, stop=True),
                     [K, Q], [pb])

            for i0 in range(min(NS - 1, len(tiles))):
                qk(i0)
            for idx, (j, kt) in enumerate(tiles):
                if idx + NS - 1 < len(tiles):
                    qk(idx + NS - 1)
                i = max(0, kt - 4 * j)
                c0 = 128 * i
                pb = pS[idx % NS]
                pt = PT[idx % 3]
                T.op("act", lambda e: e.activation(out=pt[:, c0:512], in_=pb[:, c0:512], func=AF.Exp, scale=scale),
                     [pb], [pt])
                if kt >= 4 * j:
                    T.op("dve", lambda e: e.tensor_tensor(out=pt[:, c0:c0 + 128], in0=pt[:, c0:c0 + 128],
                                                           in1=mask_bf[:, :], op=ALU.mult), [pt, mask_bf], [pt])
                for sub in range(i, 4):
                    last = (kt == 4 * j + sub)
                    T.op("pe", lambda e: e.matmul(out=pO[sub][:, 0:65], lhsT=pt[:, sub * 128:(sub + 1) * 128],
                                                  rhs=V[:, kt, :], start=(kt == 0), stop=last),
                         [pt, V], [pO[sub]], inc=(last or sub == 3))
                    if last:
                        T.op("dve", lambda e: e.reciprocal(out=rn[:, sub:sub + 1], in_=pO[sub][:, 64:65]),
                             [pO[sub]], [rn])
                        T.op("dve", lambda e: e.tensor_scalar(out=Oall[:, j * 4 + sub, ocol:ocol + 64],
                                                               in0=pO[sub][:, 0:64], scalar1=rn[:, sub:sub + 1],
                                                               scalar2=None, op0=ALU.mult), [pO[sub], rn], [Oall])

    sb_small = {}

    def sb_heads(es, Oall, QTa, KTa, Va, wk, pS, pO, pTb):
        E_, L_, C_, X_, att, attT = wk
        for h in range(8):
            sl = h % 2

            def loader(hh, s_):
                T.dma("sp", [(QTa[s_][0:64, :], QTs[hh * 64:(hh + 1) * 64, :]),
                             (KTa[s_][0:64, :], KTs[hh * 64:(hh + 1) * 64, :])], writes=[QTa[s_], KTa[s_]])
                T.dma("sp", [(Va[s_][:, :, 0:64], Vs.rearrange("(t p) h d -> p t h d", p=128)[:, :, hh, :])],
                      writes=[Va[s_]])
            if h == 0:
                loader(0, 0)
            if h + 1 < 8:
                loader(h + 1, 1 - sl)
            Q, K, V = QTa[sl], KTa[sl], Va[sl]
            chunks = []
            for qt in range(NT):
                r0 = NT - 1 - qt
                n = qt + 1
                nchunk = (n + 3) // 4
                for c in range(nchunk):
                    cnt = min(4, n - 4 * c)
                    chunks.append((qt, c, r0 + 4 * c, cnt, c == nchunk - 1))
            NCH = len(chunks)

            def qk(idx):
                qt, c, t0, cnt, lastc = chunks[idx]
                w = 128 * cnt
                pb = pS[idx % 3]
                T.op("pe", lambda e: e.matmul(out=pb[:, 0:w], lhsT=Q[0:64, qt * 128:(qt + 1) * 128],
                                              rhs=K[0:64, t0 * 128:t0 * 128 + w], start=True, stop=True),
                     [Q, K], [pb])

            def f_E(idx):
                qt, c, t0, cnt, lastc = chunks[idx]
                w = 128 * cnt
                T.op("act", lambda e: e.activation(out=E_[idx % 4][:, 0:w], in_=pS[idx % 3][:, 0:w], func=AF.Exp,
                                                   scale=0.125), [pS[idx % 3]], [E_[idx % 4]])

            def f_L(idx):
                qt, c, t0, cnt, lastc = chunks[idx]
                w = 128 * cnt
                Lb = L_[idx % 2]
                T.op("act", lambda e: e.activation(out=Lb[:, 0:w], in_=E_[idx % 4][:, 0:w], func=AF.Ln, bias=1.0),
                     [E_[idx % 4]], [Lb])
                if c == 0:
                    T.op("pool", lambda e: e.tensor_tensor(out=Lb[:, 0:128], in0=Lb[:, 0:128], in1=mS_f,
                                                           op=ALU.mult), [Lb, cst], [Lb])

            def f_scan(idx):
                qt, c, t0, cnt, lastc = chunks[idx]
                w = 128 * cnt
                Lb, Cb = L_[idx % 2], C_[idx % 3]
                if c == 0:
                    T.op("dve", lambda e: e.tensor_tensor_scan(out=Cb[:, 0:w], data0=ones_f[:, 0:w],
                                                                data1=Lb[:, 0:w], initial=0.0,
                                                                op0=ALU.mult, op1=ALU.add), [Lb, ones_f], [Cb])
                else:
                    Cp = C_[(idx - 1) % 3]
                    T.op("dve", lambda e: e.tensor_tensor_scan(out=Cb[:, 0:w], data0=ones_f[:, 0:w],
                                                                data1=Lb[:, 0:w], initial=Cp[:, 511:512],
                                                                op0=ALU.mult, op1=ALU.add), [Lb, ones_f, Cp], [Cb])

            def b_X(idx):
                qt, c, t0, cnt, lastc = chunks[idx]
                w = 128 * cnt
                T.op("act", lambda e: e.activation(out=X_[idx % 2][:, 0:w], in_=C_[idx % 3][:, 0:w], func=AF.Exp,
                                                   scale=-1.0), [C_[idx % 3]], [X_[idx % 2]])

            def b_att(idx):
                qt, c, t0, cnt, lastc = chunks[idx]
                w = 128 * cnt
                ab = att[idx % 2]
                T.op("dve", lambda e: e.tensor_tensor(out=ab[:, 0:w], in0=E_[idx % 4][:, 0:w], in1=X_[idx % 2][:, 0:w],
                                                      op=ALU.mult), [E_[idx % 4], X_[idx % 2]], [ab])
                if c == 0:
                    T.op("pool", lambda e: e.tensor_tensor(out=ab[:, 0:128], in0=ab[:, 0:128], in1=mS_bf[:, :],
                                                           op=ALU.mult), [ab, mS_bf], [ab])

            pTv = [pTb[:, 0:512], pO[3][:, :].bitcast(BF16)[:, 0:512]]
            pTb2 = [pTb, pO[3]]

            def b_T(idx):
                qt, c, t0, cnt, lastc = chunks[idx]
                ab = att[idx % 2]
                ptb = pTb2[idx % 2]
                pv = pTv[idx % 2]
                for m in range(cnt):
                    T.op("pe", lambda e: e.transpose(out=pv[:, m * 128:(m + 1) * 128],
                                                     in_=ab[:, m * 128:(m + 1) * 128], identity=ident_bf[:, :]),
                         [ab, ident_bf], [ptb], inc=(m == cnt - 1))

            def b_evac(idx):
                qt, c, t0, cnt, lastc = chunks[idx]
                w = 128 * cnt
                pv = pTv[idx % 2]
                if idx % 2 == 0:
                    T.op("act", lambda e: e.activation(out=attT[idx % 2][:, 0:w], in_=pv[:, 0:w], func=AF.Copy),
                         [pTb2[idx % 2]], [attT[idx % 2]])
                else:
                    T.op("dve", lambda e: e.tensor_copy(out=attT[idx % 2][:, 0:w], in_=pv[:, 0:w]),
                         [pTb2[idx % 2]], [attT[idx % 2]])

            def b_PV(idx):
                qt, c, t0, cnt, lastc = chunks[idx]
                aT = attT[idx % 2]
                po = pO[qt % 2]
                for m in range(cnt):
                    lastm = lastc and (m == cnt - 1)
                    T.op("pe", lambda e: e.matmul(out=po[:, 0:64], lhsT=aT[:, m * 128:(m + 1) * 128],
                                                  rhs=V[:, t0 + m, 0:64], start=(c == 0 and m == 0), stop=lastm),
                         [aT, V], [po], inc=(m == cnt - 1))
                if lastc:
                    T.op("act", lambda e: e.activation(out=Oall[:, qt, 512 + h * 64:512 + (h + 1) * 64],
                                                       in_=po[:, 0:64], func=AF.Copy), [po], [Oall])

            for i in range(min(3, NCH)):
                qk(i)
            for i in range(min(2, NCH)):
                f_E(i)
                f_L(i)
                f_scan(i)
            for idx in range(NCH + 2):
                if idx + 3 < NCH:
                    qk(idx + 3)
                if idx + 2 < NCH:
                    f_E(idx + 2)
                if idx < NCH:
                    b_X(idx)
                if idx + 2 < NCH:
                    f_L(idx + 2)
                if idx < NCH:
                    b_att(idx)
                if idx + 2 < NCH:
                    f_scan(idx + 2)
                if SBLAG == 2:
                    if 2 <= idx:
                        b_PV(idx - 2)
                    if idx < NCH:
                        b_T(idx)
                    if 1 <= idx <= NCH:
                        b_evac(idx - 1)
                else:
                    if 1 <= idx <= NCH:
                        b_PV(idx - 1)
                    if idx < NCH:
                        b_T(idx)
                        b_evac(idx)

    def phase_B0(es_outer, hook=None):
        Oall = sb(es_outer, "Oall", [128, NT, D], BF16)
        GBUFS.append(Oall)
        if hook is not None:
            hook(es_outer)
        es = contextlib.ExitStack()
        QTa = [sb(es, f"b_QTa{i}", [128, S], BF16) for i in range(2)]
        KTa = [sb(es, f"b_KTa{i}", [128, S], BF16) for i in range(2)]
        Va = [sb(es, f"b_Va{i}", [128, NT, 65], BF16) for i in range(2)]
        PT = [sb(es, f"b_PT{i}", [128, 512], BF16) for i in range(3)]
        sb_small["rn"] = sb(es, "b_rn", [128, 4], F32)
        E_ = [sb(es, f"b_E{i}", [128, 512], F32) for i in range(4)]
        L_ = [sb(es, f"b_L{i}", [128, 512], F32) for i in range(2)]
        C_ = [sb(es, f"b_C{i}", [128, 512], F32) for i in range(3)]
        X_ = [sb(es, f"b_X{i}", [128, 512], F32) for i in range(2)]
        att = [sb(es, f"b_att{i}", [128, 512], BF16) for i in range(2)]
        attT = [sb(es, f"b_attT{i}", [128, 512], BF16) for i in range(2)]
        pS = [ps(es, f"b_pS{i}", [128, 512], F32) for i in range(3)]
        pO = [ps(es, f"b_pO{i}", [128, 512], F32) for i in range(4)]
        pTb = ps(es, "b_pTb", [128, 1024], BF16)
        for i in range(2):
            T.op("pool", lambda e: e.memset(Va[i][:, :, 64:65], 1.0), [], [Va[i]])

        def loader(h, Q, K, V):
            T.dma("sp", [(Q[0:96, :], QTm[h]), (K[0:96, :], KTm[h])], writes=[Q, K])
            T.dma("sp", [(V[:, :, 0:64], Vm.rearrange("(t p) h d -> p t h d", p=128)[:, :, h, :])], writes=[V])

        softmax_heads(es, Oall, [(h, h * 64) for h in range(8)], 96, 96.0 ** -0.5, mM_bf, loader,
                      QTa, KTa, Va, PT, pS, pO)
        sb_heads(es, Oall, QTa, KTa, Va, (E_, L_, C_, X_, att, attT), pS, pO, pTb)
        phase_end()
        es.close()
        return Oall

    def phase_B1(es_outer, hook=None):
        Oall = sb(es_outer, "Oall1", [128, NT, D], BF16)
        GBUFS.append(Oall)
        if hook is not None:
            hook(es_outer)
        es = contextlib.ExitStack()
        QTa = [sb(es, f"b1_QTa{i}", [128, S], BF16) for i in range(2)]
        KTa = [sb(es, f"b1_KTa{i}", [128, S], BF16) for i in range(2)]
        Va = [sb(es, f"b1_Va{i}", [128, NT, 65], BF16) for i in range(2)]
        PT = [sb(es, f"b1_PT{i}", [128, 512], BF16) for i in range(3)]
        sb_small["rn"] = sb(es, "b1_rn", [128, 4], F32)
        pS = [ps(es, f"b1_pS{i}", [128, 512], F32) for i in range(4)]
        pO = [ps(es, f"b1_pO{i}", [128, 512], F32) for i in range(4)]
        for i in range(2):
            T.op("pool", lambda e: e.memset(Va[i][:, :, 64:65], 1.0), [], [Va[i]])
            T.op("pool", lambda e: e.memset(QTa[i][64:70, :], 1.0), [], [QTa[i]])
            T.op("pool", lambda e: e.memset(KTa[i][64:70, :], 1.0), [], [KTa[i]])

        def loader(h, Q, K, V):
            T.dma("sp", [(Q[0:64, :], QTf[h * 64:(h + 1) * 64, :]), (Q[64:67, :], QDf[:, h, :]),
                         (K[0:64, :], KTf[h * 64:(h + 1) * 64, :]), (K[67:70, :], KDf[:, h, :])], writes=[Q, K])
            T.dma("sp", [(V[:, :, 0:64], Vf.rearrange("(t p) h d -> p t h d", p=128)[:, :, h, :])], writes=[V])

        softmax_heads(es, Oall, [(h, h * 64) for h in range(16)], 70, 0.125, mC_bf, loader,
                      QTa, KTa, Va, PT, pS, pO)
        phase_end()
        es.close()
        return Oall

    def preload_C1(es, li, w_out):
        Wout = load_w(es, "c1_wout", w_out, 8, D)
        g_bc = load_bc(es, "c1_g", ln1_g[li])
        b_bc = load_bc(es, "c1_b", ln1_b[li])
        GBUFS.extend([Wout, g_bc, b_bc])
        return [Wout, g_bc, b_bc]

    def phase_C1(li, Oall, pre, xsrc):
        es = contextlib.ExitStack()
        Wout, g_bc, b_bc = pre
        xt = [sb(es, f"c1_x{i}", [128, D], F32) for i in range(2)]
        rr = [sb(es, f"c1_r{i}", [128, D], F32) for i in range(2)]
        OTs = [sb(es, f"c1_OT{i}", [128, 8, 128], BF16) for i in range(2)]
        st = sb(es, "c1_st", [128, 32], F32)
        pB = ps(es, "c1_pB", [128, 1024], BF16)
        pMs = [ps(es, f"c1_pM{i}", [128, 1024], F32) for i in range(2)]

        def front(tt):
            OT, pM = OTs[tt % 2], pMs[tt % 2]
            for c in range(8):
                T.op("pe", lambda e: e.transpose(out=pB[:, c * 128:(c + 1) * 128], in_=Oall[:, tt, c * 128:(c + 1) * 128],
                                                 identity=ident_bf[:, :]), [Oall, ident_bf], [pB], inc=(c == 7))
            T.op("act", lambda e: e.activation(out=OT[:, :, :], in_=pB[:, :].rearrange("p (c t) -> p c t", t=128),
                                               func=AF.Copy), [pB], [OT])
            for n0 in (0, 512):
                for kc in range(8):
                    T.op("pe", lambda e: e.matmul(out=pM[:, n0:n0 + 512], lhsT=OT[:, kc, :], rhs=Wout[:, kc, n0:n0 + 512],
                                                  start=(kc == 0), stop=(kc == 7)), [OT, Wout], [pM],
                         inc=(kc == 7 and n0 == 512))

        def back(tt):
            x, r, pM = xt[tt % 2], rr[tt % 2], pMs[tt % 2]
            for n0 in (0, 512):
                T.op("dve", lambda e: e.scalar_tensor_tensor(out=r[:, n0:n0 + 512], in0=x[:, n0:n0 + 512], scalar=ALPHA,
                                                              in1=pM[:, n0:n0 + 512], op0=ALU.mult, op1=ALU.add),
                     [x, pM], [r])
            layer_norm(r, g_bc, b_bc, r, st)
            T.dma("sp", [(Y1[tt * 128:(tt + 1) * 128, :], r[:, :])], reads=[r])

        T.dma("sp", [(xt[0][:, :], xsrc[0:128, :])], writes=[xt[0]])
        front(0)
        for tt in range(NT):
            if tt + 1 < NT:
                nx = xt[(tt + 1) % 2]
                T.dma("sp", [(nx[:, :], xsrc[(tt + 1) * 128:(tt + 2) * 128, :])], writes=[nx])
                front(tt + 1)
            back(tt)
        phase_end()
        es.close()

    def phase_C2(li, W13):
        es = contextlib.ExitStack()
        W1, W3 = W13
        W2 = load_w(es, "c2_w2", ffn_w2[li], NFC, D)
        g_bc = load_bc(es, "c2_g", ln2_g[li])
        b_bc = load_bc(es, "c2_b", ln2_b[li])
        yt = [sb(es, f"c2_y{i}", [128, D], F32) for i in range(2)]
        rr = [sb(es, f"c2_r{i}", [128, D], F32) for i in range(2)]
        yTs = [sb(es, f"c2_yT{i}", [128, 8, 128], BF16) for i in range(2)]
        sl_ = [sb(es, f"c2_s{i}", [128, 512], F32) for i in range(2)]
        ggs = [sb(es, f"c2_g2{i}", [128, DFF], BF16) for i in range(2)]
        gT = sb(es, "c2_gT", [128, NFC, 128], BF16)
        st = sb(es, "c2_st", [128, 32], F32)
        pT = ps(es, "c2_pT", [128, 512], F32)
        pH1 = [ps(es, f"c2_pH1{i}", [128, 512], F32) for i in range(2)]
        pH3 = [ps(es, f"c2_pH3{i}", [128, 512], F32) for i in range(2)]
        pB = ps(es, "c2_pB", [128, 1024], BF16)
        pO = ps(es, "c2_pO", [128, 1024], F32)
        blocks = [(n0, min(512, DFF - n0)) for n0 in range(0, DFF, 512)]

        def emit_H(tt):
            yT, gg = yTs[tt % 2], ggs[tt % 2]
            for bi, (n0, w) in enumerate(blocks):
                p1, p3, s_ = pH1[bi % 2], pH3[bi % 2], sl_[bi % 2]
                for kc in range(8):
                    T.op("pe", lambda e: e.matmul(out=p1[:, 0:w], lhsT=yT[:, kc, :], rhs=W1[:, kc, n0:n0 + w],
                                                  start=(kc == 0), stop=(kc == 7)), [yT, W1], [p1], inc=(kc == 7))
                for kc in range(8):
                    T.op("pe", lambda e: e.matmul(out=p3[:, 0:w], lhsT=yT[:, kc, :], rhs=W3[:, kc, n0:n0 + w],
                                                  start=(kc == 0), stop=(kc == 7)), [yT, W3], [p3], inc=(kc == 7))
                T.op("act", lambda e: e.activation(out=s_[:, 0:w], in_=p1[:, 0:w], func=AF.Silu), [p1], [s_])
                T.op("dve", lambda e: e.tensor_tensor(out=gg[:, n0:n0 + w], in0=s_[:, 0:w], in1=p3[:, 0:w],
                                                       op=ALU.mult), [s_, p3], [gg])

        def emit_tail(tt):
            y, r, gg = yt[tt % 2], rr[tt % 2], ggs[tt % 2]
            for f0 in range(0, NFC, 8):
                n = min(8, NFC - f0)
                for i in range(n):
                    T.op("pe", lambda e: e.transpose(out=pB[:, i * 128:(i + 1) * 128],
                                                     in_=gg[:, (f0 + i) * 128:(f0 + i + 1) * 128],
                                                     identity=ident_bf[:, :]), [gg, ident_bf], [pB], inc=(i == n - 1))
                T.op("act", lambda e: e.activation(out=gT[:, f0:f0 + n, :],
                                                   in_=pB[:, 0:n * 128].rearrange("p (c t) -> p c t", t=128),
                                                   func=AF.Copy), [pB], [gT])
            for n0 in (0, 512):
                for fc in range(NFC):
                    T.op("pe", lambda e: e.matmul(out=pO[:, n0:n0 + 512], lhsT=gT[:, fc, :], rhs=W2[:, fc, n0:n0 + 512],
                                                  start=(fc == 0), stop=(fc == NFC - 1)), [gT, W2], [pO],
                         inc=(fc == NFC - 1 and n0 == 512))
            for n0 in (0, 512):
                T.op("dve", lambda e: e.scalar_tensor_tensor(out=r[:, n0:n0 + 512], in0=y[:, n0:n0 + 512], scalar=ALPHA,
                                                              in1=pO[:, n0:n0 + 512], op0=ALU.mult, op1=ALU.add),
                     [y, pO], [r])
            layer_norm(r, g_bc, b_bc, r, st)
            T.dma("sp", [(Y2[tt * 128:(tt + 1) * 128, :], r[:, :])], reads=[r])

        T.dma("sp", [(yt[0][:, :], Y1[0:128, :])], writes=[yt[0]])
        transpose_f32_tile(yt[0], yTs[0], pT, 8)
        for tt in range(NT):
            if tt + 1 < NT:
                ny = yt[(tt + 1) % 2]
                T.dma("sp", [(ny[:, :], Y1[(tt + 1) * 128:(tt + 2) * 128, :])], writes=[ny])
            emit_H(tt)
            if tt + 1 < NT:
                transpose_f32_tile(yt[(tt + 1) % 2], yTs[(tt + 1) % 2], pT, 8)
            emit_tail(tt)
        phase_end()
        es.close()

    def phase_C3(li, dst):
        es = contextlib.ExitStack()
        Wg = load_w(es, "c3_wg", ple_w_gate[li], 8, D)
        Wp = load_w(es, "c3_wp", ple_w_proj[li], 2, D)
        bg_bc = load_bc(es, "c3_bg", ple_b_gate[li])
        yt = [sb(es, f"c3_y{i}", [128, D], F32) for i in range(3)]
        pt_ = [sb(es, f"c3_p{i}", [128, 256], F32) for i in range(3)]
        oo = [sb(es, f"c3_o{i}", [128, D], F32) for i in range(2)]
        tgs = [sb(es, f"c3_tg{i}", [128, D], F32) for i in range(2)]
        yTs = [sb(es, f"c3_yT{i}", [128, 8, 128], BF16) for i in range(2)]
        pTs = [sb(es, f"c3_pT{i}", [128, 2, 128], BF16) for i in range(2)]
        pT = ps(es, "c3_pTp", [128, 512], F32)
        pG = ps(es, "c3_pG", [128, 1024], F32)
        pP = ps(es, "c3_pP", [128, 1024], F32)

        def loads(tt):
            T.dma("sp", [(yt[tt % 3][:, :], Y2[tt * 128:(tt + 1) * 128, :])], writes=[yt[tt % 3]])
            T.dma("sp", [(pt_[tt % 3][:, :], p_in[li, tt * 128:(tt + 1) * 128, :])], writes=[pt_[tt % 3]])

        def frontA(tt):
            transpose_f32_tile(yt[tt % 3], yTs[tt % 2], pT, 8)
            transpose_f32_tile(pt_[tt % 3], pTs[tt % 2], pT, 2)

        def frontB(tt):
            yT, pT_ = yTs[tt % 2], pTs[tt % 2]
            for n0 in (0, 512):
                for kc in range(8):
                    T.op("pe", lambda e: e.matmul(out=pG[:, n0:n0 + 512], lhsT=yT[:, kc, :], rhs=Wg[:, kc, n0:n0 + 512],
                                                  start=(kc == 0), stop=(kc == 7)), [yT, Wg], [pG],
                         inc=(kc == 7 and n0 == 512))
            for n0 in (0, 512):
                for kc in range(2):
                    T.op("pe", lambda e: e.matmul(out=pP[:, n0:n0 + 512], lhsT=pT_[:, kc, :], rhs=Wp[:, kc, n0:n0 + 512],
                                                  start=(kc == 0), stop=(kc == 1)), [pT_, Wp], [pP],
                         inc=(kc == 1 and n0 == 512))

        def back1(tt):
            tg = tgs[tt % 2]
            for n0 in (0, 512):
                T.op("dve", lambda e: e.tensor_tensor(out=tg[:, n0:n0 + 512], in0=pG[:, n0:n0 + 512],
                                                       in1=bg_bc[:, n0:n0 + 512], op=ALU.add), [pG, bg_bc], [tg])
            T.op("act", lambda e: e.activation(out=tg[:, :], in_=tg[:, :], func=AF.Sigmoid), [tg], [tg])
            for n0 in (0, 512):
                T.op("dve", lambda e: e.tensor_tensor(out=tg[:, n0:n0 + 512], in0=tg[:, n0:n0 + 512],
                                                       in1=pP[:, n0:n0 + 512], op=ALU.mult), [tg, pP], [tg])

        def back2(tt):
            tg, y, o = tgs[tt % 2], yt[tt % 3], oo[tt % 2]
            T.op("pool", lambda e: e.tensor_tensor(out=o[:, :], in0=tg[:, :], in1=y[:, :], op=ALU.add),
                 [tg, y], [o])
            T.dma("sp", [(dst[tt * 128:(tt + 1) * 128, :], o[:, :])], reads=[o])

        loads(0)
        if NT > 1:
            loads(1)
        frontA(0)
        frontB(0)
        for tt in range(NT):
            if tt + 2 < NT:
                loads(tt + 2)
            if tt + 1 < NT:
                frontA(tt + 1)
            back1(tt)
            if tt + 1 < NT:
                frontB(tt + 1)
            back2(tt)
        phase_end()
        es.close()

    import os
    nstop = int(os.environ.get("KSTOP", "99"))
    phase_end()
    def _run():
        n = 0
        def chk():
            nonlocal n
            n += 1
            return n > nstop
        if chk(): return
        phase_A0()
        for li in range(2):
            eo = contextlib.ExitStack()
            c1w = []

            def hook(es_, li=li):
                c1w.extend(preload_C1(es_, li, a_w_out if li == 0 else c_w_out))

            Oall = phase_B0(eo, hook) if li == 0 else phase_B1(eo, hook)
            er = contextlib.ExitStack()
            W13 = [load_w(er, "c2_w1", ffn_w1[li], 8, DFF, side="right"),
                   load_w(er, "c2_w3", ffn_w3[li], 8, DFF, side="right")]
            GBUFS.extend(W13)
            phase_C1(li, Oall, c1w, x_in if li == 0 else X1)
            for b_ in c1w + [Oall]:
                GBUFS.remove(b_)
            eo.close()
            phase_C2(li, W13)
            for b_ in W13:
                GBUFS.remove(b_)
            er.close()
            if li == 0:
                era = contextlib.ExitStack()
                WinA1 = load_w(era, "c_win", c_w_in, 8, 3088, side="right")
                GBUFS.append(WinA1)
                phase_C3(0, X1)
                phase_A1(WinA1)
                GBUFS.remove(WinA1)
                era.close()
            else:
                phase_C3(1, out_d)
    _run()
    if os.environ.get("KSIM"):
        simulate_sync(T)
    gs.close()
    return nc


def make_consts(S):
    NT = S // 128
    p = np.arange(128)[:, None]
    f = np.arange(128)[None, :]
    c = np.zeros((128, 128 * 5 + NT + 16), np.float32)
    c[:, 0:128] = (p == f)
    c[:, 128:256] = (p + f == 127)
    c[:, 256:384] = (p <= f)
    c[:, 384:512] = ((p // 64) <= (f // 64))
    c[:, 512:640] = (p + f >= 128)
    c[:, 640:640 + NT] = (np.arange(NT)[None, :] * 128 + p).astype(np.float32)
    inv = (1.0 / (np.float32(10000.0) ** (np.arange(0, 32, 2, dtype=np.float32) / np.float32(32)))).astype(np.float32)
    c[:, 640 + NT:] = inv[None, :]
    return c


_CACHE = {}


def run(inputs, S):
    B = inputs["x"].shape[0]
    if S not in _CACHE:
        _CACHE[S] = build(S)
    nc = _CACHE[S]
    cst = make_consts(S)
    f32 = lambda a: np.ascontiguousarray(np.asarray(a, dtype=np.float32))
    shared = {
        "a_w_in": f32(inputs["a_w_in"][0]), "a_q_norm": f32(inputs["a_q_norm"][0]),
        "a_w_uq": f32(inputs["a_w_uq"][0]), "a_kv_norm": f32(inputs["a_kv_norm"][0]),
        "a_w_ukv": f32(inputs["a_w_ukv"][0]), "a_w_out": f32(inputs["a_w_out"][0]),
        "c_w_in": f32(inputs["c_w_in"][0]), "c_b_f": f32(inputs["c_b_f"][0]),
        "c_w_out": f32(inputs["c_w_out"][0]),
        "ffn_w1": f32(inputs["ffn_w1"]), "ffn_w3": f32(inputs["ffn_w3"]), "ffn_w2": f32(inputs["ffn_w2"]),
        "ln1_g": f32(inputs["ln1_g"]), "ln1_b": f32(inputs["ln1_b"]),
        "ln2_g": f32(inputs["ln2_g"]), "ln2_b": f32(inputs["ln2_b"]),
        "ple_w_proj": f32(inputs["ple_w_proj"]), "ple_w_gate": f32(inputs["ple_w_gate"]),
        "ple_b_gate": f32(inputs["ple_b_gate"]), "cst": cst,
    }
    xs = f32(inputs["x"])
    ps_ = f32(inputs["p"])
    in_maps = []
    for b in range(B):
        m = dict(shared)
        m["x"] = np.ascontiguousarray(xs[b])
        m["p"] = np.ascontiguousarray(ps_[:, b])
        in_maps.append(m)
    res = run_bass_kernel_spmd(nc, in_maps, core_ids=list(range(B)))
    return np.stack([np.asarray(r["out"], dtype=np.float32) for r in res.results], axis=0)


def kernel(**inputs):
    return run(inputs, int(inputs["x"].shape[1]))
```
